# Optimizing a Trainium2 kernel written in Bass

```python
import math
import jax, jax.numpy as jnp
from jax import lax
import numpy as np

D_MODEL = 2048
BATCH = 4
SEQ = 4096
DEPTH = 2

GRID_W = 64
CTX_LEN = 256
MIX_WIDTH = D_MODEL
HEAD_DIM = 128
S5_WIDTH = MIX_WIDTH // 4
S5_GROUP = 16
S5_GROUPS = S5_WIDTH // S5_GROUP
S5_STATE = 64
DT_MIN = 1e-3
DT_MAX = 1e-1
DIFF_WIDTH = MIX_WIDTH // 4
DIFF_V_DIM = HEAD_DIM
DIFF_HEADS = DIFF_WIDTH // DIFF_V_DIM
DIFF_QK_DIM = DIFF_V_DIM // 2
DIFF_QK_WIDTH = DIFF_HEADS * 2 * DIFF_QK_DIM
GQA_WIDTH = MIX_WIDTH - S5_WIDTH - DIFF_WIDTH
GQA_HEADS = GQA_WIDTH // HEAD_DIM
GQA_KV_HEADS = 2
GQA_REP = GQA_HEADS // GQA_KV_HEADS
GQA_KV_WIDTH = GQA_KV_HEADS * HEAD_DIM
IN_SIZES = (S5_WIDTH, DIFF_QK_WIDTH, DIFF_QK_WIDTH, DIFF_WIDTH, GQA_WIDTH, GQA_KV_WIDTH, GQA_KV_WIDTH)
IN_COLS = S5_WIDTH + 2 * DIFF_QK_WIDTH + DIFF_WIDTH + GQA_WIDTH + 2 * GQA_KV_WIDTH
FFN_DIM = ((8 * D_MODEL // 3 + 255) // 256) * 256
N_MOD = 9
Q_BLOCK = 128
ROPE_THETA = 10000.0
EPS = 1e-6

kernel_name = "hybrid_s5_diffattn_gqa_macaron_dit"

f32 = jnp.float32


def rms_norm(x, g):
    xf = x.astype(f32)
    y = xf * lax.rsqrt(jnp.mean(xf * xf, axis=-1, keepdims=True) + EPS)
    return (y * g.astype(f32)).astype(x.dtype)


def modulate(h, m, k):
    return h * (1 + m[:, 3 * k + 1]) + m[:, 3 * k]


def macaron_ffn(x, m, k, g, w_gate, w_up, w_down):
    h = modulate(rms_norm(x, g), m, k)
    y = (jax.nn.silu(h @ w_gate) * (h @ w_up)) @ w_down
    return x + 0.5 * m[:, 3 * k + 2] * y


def split_in(p):
    outs, start = [], 0
    for size in IN_SIZES:
        outs.append(p[..., start:start + size])
        start += size
    return outs


def rope_1d(x, pos):
    half = x.shape[-1] // 2
    freqs = ROPE_THETA ** (-jnp.arange(half, dtype=f32) / half)
    ang = pos.astype(f32)[:, None] * freqs
    shape = (1, x.shape[1]) + (1,) * (x.ndim - 3) + (half,)
    cos = jnp.cos(ang).reshape(shape).astype(x.dtype)
    sin = jnp.sin(ang).reshape(shape).astype(x.dtype)
    x1, x2 = x[..., :half], x[..., half:]
    return jnp.concatenate([x1 * cos - x2 * sin, x1 * sin + x2 * cos], axis=-1)


def rope_axial(x, pos_row, pos_col):
    a = x.shape[-1] // 2
    return jnp.concatenate([rope_1d(x[..., :a], pos_row), rope_1d(x[..., a:], pos_col)], axis=-1)


def block_attention(q, k, v):
    b, l, g, r, dk = q.shape
    nb = l // Q_BLOCK
    qb = jnp.moveaxis(q.reshape(b, nb, Q_BLOCK, g, r, dk), 1, 0)
    scale = dk ** -0.5

    def one_block(qblk):
        s = jnp.einsum('bqgrd,bkgd->bgrqk', qblk, k).astype(f32) * scale
        p = jax.nn.softmax(s, axis=-1).astype(v.dtype)
        return jnp.einsum('bgrqk,bkgd->bqgrd', p, v)

    out = lax.map(one_block, qb)
    return jnp.moveaxis(out, 0, 1).reshape(b, l, g, r, v.shape[-1])


def s5_discretize(lam_re, lam_im, log_dt, b_re, b_im):
    lam = lax.complex(lam_re.astype(f32), lam_im.astype(f32))
    dt = jnp.exp(log_dt.astype(f32))[:, None]
    a_bar = jnp.exp(lam * dt)
    b_bar = ((a_bar - 1) / lam)[..., None] * lax.complex(b_re.astype(f32), b_im.astype(f32))
    return a_bar, b_bar


def ssm_combine(left, right):
    a_l, b_l = left
    a_r, b_r = right
    return a_r * a_l, a_r * b_l + b_r


def s5_scan(u, a_bar, b_bar, h0, reverse):
    bu = lax.complex(jnp.einsum('blgh,gph->blgp', u, b_bar.real),
                     jnp.einsum('blgh,gph->blgp', u, b_bar.imag))
    if reverse:
        bu = jnp.flip(bu, axis=1)
    if h0 is not None:
        bu = bu.at[:, 0].add(a_bar * h0)
    a = jnp.broadcast_to(a_bar, bu.shape)
    _, h = lax.associative_scan(ssm_combine, (a, bu), axis=1)
    if reverse:
        h = jnp.flip(h, axis=1)
    return h


def s5_readout(h, c_re, c_im):
    return (jnp.einsum('blgp,ghp->blgh', h.real, c_re.astype(f32))
            - jnp.einsum('blgp,ghp->blgh', h.imag, c_im.astype(f32)))


def s5_glu(y, u, d_skip, glu_w, glu_b):
    uf = u.astype(f32)
    y = y.reshape(uf.shape) + d_skip.astype(f32) * uf
    y = jax.nn.gelu(y)
    y = y * jax.nn.sigmoid(y @ glu_w.astype(f32) + glu_b.astype(f32))
    return y.astype(u.dtype)


def s5_mixer(u_x, u_c, need_ctx, lam_re, lam_im, log_dt, b_re, b_im, c_re, c_im, d_skip, glu_w, glu_b):
    ux = u_x.astype(f32).reshape(u_x.shape[:2] + (S5_GROUPS, S5_GROUP))
    uc = u_c.astype(f32).reshape(u_c.shape[:2] + (S5_GROUPS, S5_GROUP))
    y_x = 0.0
    y_c = 0.0
    for direction in range(2):
        rev = direction == 1
        a_bar, b_bar = s5_discretize(lam_re[direction], lam_im[direction], log_dt[direction],
                                     b_re[direction], b_im[direction])
        h_c = s5_scan(uc, a_bar, b_bar, None, rev)
        h_end = h_c[:, 0] if rev else h_c[:, -1]
        h_x = s5_scan(ux, a_bar, b_bar, h_end, rev)
        y_x = y_x + s5_readout(h_x, c_re[direction], c_im[direction])
        if need_ctx:
            y_c = y_c + s5_readout(h_c, c_re[direction], c_im[direction])
    out_x = s5_glu(y_x, u_x, d_skip, glu_w, glu_b)
    out_c = s5_glu(y_c, u_c, d_skip, glu_w, glu_b) if need_ctx else None
    return out_x, out_c


def diff_attention(qkv_x, qkv_c, need_ctx, pos_row, pos_col, lam_vecs, subln_g, lam_init):
    def heads(q, k, v):
        bb, ll = q.shape[:2]
        return (q.reshape(bb, ll, DIFF_HEADS, 2, DIFF_QK_DIM),
                k.reshape(bb, ll, DIFF_HEADS, 2, DIFF_QK_DIM),
                v.reshape(bb, ll, DIFF_HEADS, DIFF_V_DIM))

    qx, kx, vx = heads(*qkv_x)
    qc, kc, vc = heads(*qkv_c)
    qx = rope_axial(qx, pos_row, pos_col)
    kx = rope_axial(kx, pos_row, pos_col)
    lv = lam_vecs.astype(f32)
    lam = jnp.exp(jnp.sum(lv[0] * lv[1])) - jnp.exp(jnp.sum(lv[2] * lv[3])) + lam_init

    def diff_attend(q, k, v):
        o1 = block_attention(q[:, :, :, 0:1], k[:, :, :, 0], v)
        o2 = block_attention(q[:, :, :, 1:2], k[:, :, :, 1], v)
        o = (o1 - lam.astype(o1.dtype) * o2)[:, :, :, 0]
        o = rms_norm(o, subln_g) * (1.0 - lam_init)
        return o.reshape(o.shape[0], o.shape[1], DIFF_WIDTH)

    out_x = diff_attend(qx, jnp.concatenate([kc, kx], axis=1), jnp.concatenate([vc, vx], axis=1))
    out_c = diff_attend(qc, kc, vc) if need_ctx else None
    return out_x, out_c


def gqa_attention(qkv_x, qkv_c, need_ctx, pos_row, pos_col, q_g, k_g):
    def heads(q, k, v):
        bb, ll = q.shape[:2]
        q = rms_norm(q.reshape(bb, ll, GQA_KV_HEADS, GQA_REP, HEAD_DIM), q_g)
        k = rms_norm(k.reshape(bb, ll, GQA_KV_HEADS, HEAD_DIM), k_g)
        v = v.reshape(bb, ll, GQA_KV_HEADS, HEAD_DIM)
        return q, k, v

    qx, kx, vx = heads(*qkv_x)
    qc, kc, vc = heads(*qkv_c)
    qx = rope_axial(qx, pos_row, pos_col)
    kx = rope_axial(kx, pos_row, pos_col)
    ox = block_attention(qx, jnp.concatenate([kc, kx], axis=1), jnp.concatenate([vc, vx], axis=1))
    out_x = ox.reshape(ox.shape[0], ox.shape[1], GQA_WIDTH)
    out_c = None
    if need_ctx:
        oc = block_attention(qc, kc, vc)
        out_c = oc.reshape(oc.shape[0], oc.shape[1], GQA_WIDTH)
    return out_x, out_c


def setup_inputs(seed: int = 0) -> dict:
    key = jax.random.key(seed)
    ks = jax.random.split(key, 32)

    def nrm(k, shape, scale):
        return jax.random.normal(k, shape, f32) * scale

    G, P, H = S5_GROUPS, S5_STATE, S5_GROUP
    x = nrm(ks[0], (BATCH, SEQ, D_MODEL), 1.0)
    c = nrm(ks[1], (BATCH, D_MODEL), 1.0)
    ctx = nrm(ks[2], (BATCH, CTX_LEN, D_MODEL), 1.0)
    c_ctx = nrm(ks[3], (D_MODEL,), 1.0)
    ada_w = nrm(ks[4], (DEPTH, D_MODEL, N_MOD * D_MODEL), 0.5 * D_MODEL ** -0.5)
    ada_b = nrm(ks[5], (DEPTH, N_MOD * D_MODEL), 0.02)
    norm_g = 1.0 + nrm(ks[6], (DEPTH, 3, D_MODEL), 0.05)
    ffn_w_gate = nrm(ks[7], (DEPTH, 2, D_MODEL, FFN_DIM), D_MODEL ** -0.5)
    ffn_w_up = nrm(ks[8], (DEPTH, 2, D_MODEL, FFN_DIM), D_MODEL ** -0.5)
    ffn_w_down = nrm(ks[9], (DEPTH, 2, FFN_DIM, D_MODEL), FFN_DIM ** -0.5)
    w_in = nrm(ks[10], (DEPTH, D_MODEL, IN_COLS), D_MODEL ** -0.5)
    w_out = nrm(ks[11], (DEPTH, MIX_WIDTH, D_MODEL), MIX_WIDTH ** -0.5)
    s5_lam_re = -0.5 + nrm(ks[12], (DEPTH, 2, G, P), 0.01)
    s5_lam_im = math.pi * jnp.arange(P, dtype=f32) + nrm(ks[13], (DEPTH, 2, G, P), 0.01)
    s5_log_dt = jax.random.uniform(ks[14], (DEPTH, 2, G), f32, math.log(DT_MIN), math.log(DT_MAX))
    s5_b_re = nrm(ks[15], (DEPTH, 2, G, P, H), (2 * H) ** -0.5)
    s5_b_im = nrm(ks[16], (DEPTH, 2, G, P, H), (2 * H) ** -0.5)
    s5_c_re = nrm(ks[17], (DEPTH, 2, G, H, P), P ** -0.5)
    s5_c_im = nrm(ks[18], (DEPTH, 2, G, H, P), P ** -0.5)
    s5_d = nrm(ks[19], (DEPTH, S5_WIDTH), 0.5)
    s5_glu_w = nrm(ks[20], (DEPTH, S5_WIDTH, S5_WIDTH), S5_WIDTH ** -0.5)
    s5_glu_b = nrm(ks[21], (DEPTH, S5_WIDTH), 0.02)
    diff_lam = nrm(ks[22], (DEPTH, 4, DIFF_QK_DIM), 0.1)
    diff_subln_g = 1.0 + nrm(ks[23], (DEPTH, DIFF_V_DIM), 0.05)
    gqa_q_g = 1.0 + nrm(ks[24], (DEPTH, HEAD_DIM), 0.05)
    gqa_k_g = 1.0 + nrm(ks[25], (DEPTH, HEAD_DIM), 0.05)
    final_g = 1.0 + nrm(ks[26], (D_MODEL,), 0.05)
    return {"x": x, "c": c, "ctx": ctx, "c_ctx": c_ctx, "ada_w": ada_w, "ada_b": ada_b,
            "norm_g": norm_g, "ffn_w_gate": ffn_w_gate, "ffn_w_up": ffn_w_up, "ffn_w_down": ffn_w_down,
            "w_in": w_in, "w_out": w_out, "s5_lam_re": s5_lam_re, "s5_lam_im": s5_lam_im,
            "s5_log_dt": s5_log_dt, "s5_b_re": s5_b_re, "s5_b_im": s5_b_im, "s5_c_re": s5_c_re,
            "s5_c_im": s5_c_im, "s5_d": s5_d, "s5_glu_w": s5_glu_w, "s5_glu_b": s5_glu_b,
            "diff_lam": diff_lam, "diff_subln_g": diff_subln_g, "gqa_q_g": gqa_q_g, "gqa_k_g": gqa_k_g,
            "final_g": final_g}


def reference(x, c, ctx, c_ctx, ada_w, ada_b, norm_g, ffn_w_gate, ffn_w_up, ffn_w_down, w_in, w_out,
              s5_lam_re, s5_lam_im, s5_log_dt, s5_b_re, s5_b_im, s5_c_re, s5_c_im, s5_d, s5_glu_w, s5_glu_b,
              diff_lam, diff_subln_g, gqa_q_g, gqa_k_g, final_g):
    b, seq = x.shape[0], x.shape[1]
    rows = seq // GRID_W
    pos_row = jnp.repeat(jnp.arange(rows, dtype=jnp.int32), GRID_W)
    pos_col = jnp.tile(jnp.arange(GRID_W, dtype=jnp.int32), rows)
    xc = ctx
    for i in range(DEPTH):
        need_ctx = i < DEPTH - 1
        lam_init = 0.8 - 0.6 * math.exp(-0.3 * i)
        m_x = (jax.nn.silu(c) @ ada_w[i] + ada_b[i]).reshape(b, N_MOD, 1, D_MODEL)
        m_c = (jax.nn.silu(c_ctx) @ ada_w[i] + ada_b[i]).reshape(1, N_MOD, 1, D_MODEL)

        xc = macaron_ffn(xc, m_c, 0, norm_g[i, 0], ffn_w_gate[i, 0], ffn_w_up[i, 0], ffn_w_down[i, 0])
        x = macaron_ffn(x, m_x, 0, norm_g[i, 0], ffn_w_gate[i, 0], ffn_w_up[i, 0], ffn_w_down[i, 0])

        hc = modulate(rms_norm(xc, norm_g[i, 1]), m_c, 1)
        hx = modulate(rms_norm(x, norm_g[i, 1]), m_x, 1)
        pc = split_in(hc @ w_in[i])
        px = split_in(hx @ w_in[i])

        s5_x, s5_c = s5_mixer(px[0], pc[0], need_ctx, s5_lam_re[i], s5_lam_im[i], s5_log_dt[i],
                              s5_b_re[i], s5_b_im[i], s5_c_re[i], s5_c_im[i], s5_d[i], s5_glu_w[i], s5_glu_b[i])
        dif_x, dif_c = diff_attention(px[1:4], pc[1:4], need_ctx, pos_row, pos_col,
                                      diff_lam[i], diff_subln_g[i], lam_init)
        gqa_x, gqa_c = gqa_attention(px[4:7], pc[4:7], need_ctx, pos_row, pos_col, gqa_q_g[i], gqa_k_g[i])

        x = x + m_x[:, 5] * (jnp.concatenate([s5_x, dif_x, gqa_x], axis=-1) @ w_out[i])
        if need_ctx:
            xc = xc + m_c[:, 5] * (jnp.concatenate([s5_c, dif_c, gqa_c], axis=-1) @ w_out[i])
            xc = macaron_ffn(xc, m_c, 2, norm_g[i, 2], ffn_w_gate[i, 1], ffn_w_up[i, 1], ffn_w_down[i, 1])

        x = macaron_ffn(x, m_x, 2, norm_g[i, 2], ffn_w_gate[i, 1], ffn_w_up[i, 1], ffn_w_down[i, 1])
    return rms_norm(x, final_g)
```

```python
import math
import numpy as np
import concourse.bass as bass
import concourse.mybir as mybir
from concourse.bass_utils import run_bass_kernel_spmd
from contextlib import ExitStack

F32 = mybir.dt.float32
BF16 = mybir.dt.bfloat16
I32 = mybir.dt.int32
ALU = mybir.AluOpType
AF = mybir.ActivationFunctionType

D = 2048
DC = 16
FF = 5632
FC = 44
LAT = 4096
CTX = 256
S = 4352
NB = 34
NCORES = 4
TILES = [(0, 256, True)] + [(256 + 512 * i, 512, False) for i in range(8)]
EPS = 1e-6
PI = math.pi


class Eng:
    def __init__(self, fw, name, h, is_pe=False):
        self.fw, self.name, self.h, self.is_pe = fw, name, h, is_pe
        self.nsem = 0
        self.newsem()
        self.seen = {}

    def newsem(self):
        self.sem = self.fw.stack.enter_context(self.fw.nc.semaphore("s_%s%d" % (self.name, self.nsem)))
        self.nsem += 1
        self.seq = 0


class Buf:
    __slots__ = ("name", "w", "r", "dsem", "dval", "excl")

    def __init__(self, name="", excl=False):
        self.name = name
        self.excl = excl
        self.w = None
        self.r = {}
        self.dsem = None
        self.dval = 0


def bufs(n, name=""):
    return [Buf(name + str(i)) for i in range(n)]


class FW:
    def __init__(self, nc):
        self.nc = nc
        self.stack = ExitStack()
        self.pe = Eng(self, "pe", nc.tensor, True)
        self.act = Eng(self, "act", nc.scalar)
        self.dve = Eng(self, "dve", nc.vector)
        self.pool = Eng(self, "pool", nc.gpsimd)
        self.sp = Eng(self, "sp", nc.sync)
        self.engs = [self.pe, self.act, self.dve, self.pool, self.sp]
        self.inflight = []
        self.nd = 0
        self.uid = 0
        self.sem_pool = []
        self.scopes = []

    def push_scope(self):
        self.scopes.append([])

    def pop_scope(self):
        for b in self.scopes.pop():
            if b.dsem is not None:
                self.sem_pool.append([b.dsem, b.dval])
                b.dsem = None

    def scope(self):
        return _Scope(self)

    def name(self, s):
        self.uid += 1
        return "%s_%d" % (s, self.uid)

    def sbuf(self, name, shape, dt, stack=None):
        return (stack or self.stack).enter_context(self.nc.sbuf_tensor(self.name(name), shape, dt))

    def psum(self, name, shape, dt):
        return self.stack.enter_context(self.nc.psum_tensor(self.name(name), shape, dt))

    def _wait(self, eng, dep):
        sem, val, ename = dep
        key = id(sem)
        if eng.seen.get(key, 0) >= val:
            return
        if eng.is_pe and ename == "pe":
            return
        eng.h.wait_ge(sem, val)
        eng.seen[key] = val

    def _deps(self, eng, reads, writes):
        for b in reads:
            if b.w is not None:
                self._wait(eng, b.w)
            if b.excl:
                for d in b.r.values():
                    self._wait(eng, d)
        for b in writes:
            if b.w is not None and b.w[2] != eng.name:
                self._wait(eng, b.w)
            for d in b.r.values():
                if d[2] != eng.name:
                    self._wait(eng, d)

    def _mark(self, d, key, reads, writes):
        for b in writes:
            b.w = d
            b.r = {}
        for b in reads:
            if b not in writes:
                b.r[key] = d

    def op(self, eng, fn, reads=(), writes=()):
        self._deps(eng, reads, writes)
        if eng.seq >= 30000:
            eng.newsem()
        ins = fn(eng.h)
        eng.seq += 1
        ins.then_inc(eng.sem, 1)
        self._mark((eng.sem, eng.seq, eng.name), eng.name, reads, writes)

    def group(self, eng, fns, reads=(), writes=()):
        self._deps(eng, reads, writes)
        if eng.seq >= 30000:
            eng.newsem()
        ins = None
        for f in fns:
            ins = f(eng.h)
        eng.seq += 1
        ins.then_inc(eng.sem, 1)
        self._mark((eng.sem, eng.seq, eng.name), eng.name, reads, writes)

    def dma(self, eng, out, in_, reads=(), writes=(), sbuf_side=None, track=True, **kw):
        self._deps(eng, reads, writes)
        ins = eng.h.dma_start(out=out, in_=in_, **kw)
        tgt = sbuf_side
        if tgt.dsem is None:
            if self.sem_pool and self.sem_pool[0][1] < 20000:
                tgt.dsem, tgt.dval = self.sem_pool.pop(0)
            else:
                tgt.dsem = self.stack.enter_context(self.nc.semaphore("d%d" % self.nd))
                self.nd += 1
                tgt.dval = 0
            if self.scopes:
                self.scopes[-1].append(tgt)
        tgt.dval += 16
        ins.then_inc(tgt.dsem, 16)
        d = (tgt.dsem, tgt.dval, "dma")
        self._mark(d, "dma%d" % id(tgt), reads, writes)
        if track:
            self.inflight.append(d)
        return d

    def barrier(self):
        deps = [(e.sem, e.seq, e.name) for e in self.engs if e.seq > 0] + self.inflight
        for e in self.engs:
            for d in deps:
                if d[2] != e.name:
                    self._wait(e, d)
        self.inflight = []


class _Scope:
    def __init__(self, fw):
        self.fw = fw
        self.es = ExitStack()

    def __enter__(self):
        self.fw.push_scope()
        self.es.__enter__()
        return self.es

    def __exit__(self, *a):
        if a[0] is None:
            self.fw.barrier()
            self.fw.pop_scope()
        return self.es.__exit__(*a)


class WStream:
    SLOT = 8192

    def __init__(self, fw, nslots, sc=None):
        self.fw = fw
        self.t = [fw.sbuf("ring", [128, self.SLOT], BF16, sc) for _ in range(nslots)]
        self.b = bufs(nslots, "ring")
        self.q = []
        self.issued = 0
        self.taken = 0

    def enqueue(self, fn):
        self.q.append(fn)

    def take(self):
        n = len(self.t)
        lim = min(self.taken + n - 1, len(self.q))
        while self.issued < lim:
            k = self.issued % n
            self.q[self.issued](self.t[k], self.b[k])
            self.issued += 1
        k = self.taken % n
        assert self.taken < self.issued
        self.taken += 1
        return self.t[k], self.b[k]


class Prog:
    def __init__(self, debug=None):
        self.debug = debug or ()
        self.nc = nc = bass.Bass("TRN2", target_bir_lowering=False)
        self.fw = fw = FW(nc)
        self.ext_in = {}
        self.ext_out = {}

    def din(self, name, shape, dt=F32):
        t = self.nc.dram_tensor(name, list(shape), dt, kind="ExternalInput").ap()
        self.ext_in[name] = t
        return t

    def dscratch(self, name, shape, dt):
        kind = "ExternalOutput" if name in self.debug else "Internal"
        t = self.nc.dram_tensor(name, list(shape), dt, kind=kind).ap()
        return t

    def declare(self):
        p = self
        p.x = p.din("x", [LAT, D])
        p.ctx = p.din("ctx", [CTX, D])
        p.cT = p.din("cT", [128, DC, 2])
        p.ada_w = p.din("ada_w", [2, D, 9 * D])
        p.ada_bT = p.din("ada_bT", [2, 128, 144])
        p.norm_gT = p.din("norm_gT", [2, 128, 3, DC])
        p.final_gT = p.din("final_gT", [128, DC])
        p.w_gate = p.din("ffn_w_gate", [2, 2, D, FF])
        p.w_up = p.din("ffn_w_up", [2, 2, D, FF])
        p.w_down = p.din("ffn_w_down", [2, 2, FF, D])
        p.w_in = p.din("w_in", [2, D, 3584])
        p.w_in_rot = p.din("w_in_rot", [2, D, 2304])
        p.w_out = p.din("w_out", [2, D, D])
        p.lamT = p.din("lamT", [2, 2, 128, 3, 16])
        p.XB = p.din("XB", [2, 2, 2, 16, 128, 128])
        p.YC = p.din("YC", [2, 2, 2, 16, 128, 128])
        p.s5dT = p.din("s5dT", [2, 128, 2, 4])
        p.glu_w = p.din("s5_glu_w", [2, 512, 512])
        p.dlam = p.din("dlam", [2, 1, 256])
        p.hgT = p.din("hgT", [2, 128, 5])
        p.out = p.nc.dram_tensor("out", [LAT, D], F32, kind="ExternalOutput").ap()
        p.xT = p.dscratch("xT", [D, S], F32)
        p.wguS = [[p.dscratch("wgu%d%d" % (i, f), [FC // 2, 128, 2, 2, DC, 128], BF16) for f in range(2)] for i in range(2)]
        p.wdS = [[p.dscratch("wd%d%d" % (i, f), [DC, 128, FC, 128], BF16) for f in range(2)] for i in range(2)]
        p.winS = [p.dscratch("win%d" % i, [10, 128, 4, DC, 128], BF16) for i in range(2)]
        p.wvdS = [p.dscratch("wvd%d" % i, [128, DC, 512], BF16) for i in range(2)]
        p.wvgS = [p.dscratch("wvg%d" % i, [128, DC, 256], BF16) for i in range(2)]
        p.woutS = [p.dscratch("wout%d" % i, [4, 128, 4, DC, 128], BF16) for i in range(2)]
        p.adaS = [p.dscratch("ada%d" % i, [36, 128, DC, 512], BF16) for i in range(2)]
        p.gluS = [p.dscratch("glu%d" % i, [128, 4, 512], BF16) for i in range(2)]
        p.uT = p.dscratch("uT", [512, S], F32)
        p.uTb = p.dscratch("uTb", [512, S], BF16)
        p.qd = p.dscratch("qd", [512, S], BF16)
        p.kd = p.dscratch("kd", [512, S], BF16)
        p.vd = p.dscratch("vd", [4, 128, NB, 128], BF16)
        p.qg = p.dscratch("qg", [1024, S], BF16)
        p.kg = p.dscratch("kg", [256, S], BF16)
        p.vg = p.dscratch("vg", [2, 128, NB, 128], BF16)
        p.yF = p.dscratch("yF", [512, S], F32)
        p.yR = p.dscratch("yR", [512, S], F32)
        p.s5T = p.dscratch("s5T", [512, S], BF16)
        p.ropeT = p.dscratch("ropeT", [4, 128, LAT], F32)
        p.mdbg = p.dscratch("mdbg", [128, 144, 2], F32)
        p.wbuf = {}

    def conv(self, key, out, in_):
        fw = self.fw
        if self.conv_filter is not None and not any(key.startswith(k) for k in self.conv_filter):
            return
        b = self.wbuf.setdefault(key, Buf(key))
        fw.dma(fw.pool, out, in_, writes=[b], sbuf_side=b, track=False)

    def convert_layer(self, i):
        p = self
        for blk in range(36):
            p.conv("ada%d" % i, p.adaS[i][blk], p.ada_w[i][:, blk * 512:(blk + 1) * 512].rearrange("(kc p) n -> p kc n", p=128))
        self.convert_ffn(i, 0)
        wi, wr = p.w_in[i], p.w_in_rot[i]

        def colchunk(src, c0):
            return src[:, c0:c0 + 128].rearrange("(kc p) n -> p kc n", p=128)
        groups = [
            [(wi, 0), (wi, 128), (wi, 256), (wi, 384)],
            [(wi, 512 + 128 * k) for k in range(4)],
            [(wr, 0 + 128 * k) for k in range(4)],
            [(wi, 1024 + 128 * k) for k in range(4)],
            [(wr, 512 + 128 * k) for k in range(4)],
            [(wi, 2048 + 128 * k) for k in range(4)],
            [(wr, 1024 + 128 * k) for k in range(4)],
            [(wi, 2560 + 128 * k) for k in range(4)],
            [(wr, 1536 + 128 * k) for k in range(4)],
            [(wi, 3072), (wi, 3200), (wr, 2048), (wr, 2176)],
        ]
        for g, lst in enumerate(groups):
            for ci, (src, c0) in enumerate(lst):
                p.conv("win%d" % i, p.winS[i][g, :, ci], colchunk(src, c0))
        p.conv("wv%d" % i, p.wvdS[i], wi[:, 1536:2048].rearrange("(kc p) n -> p kc n", p=128))
        p.conv("wv%d" % i, p.wvgS[i], wi[:, 3328:3584].rearrange("(kc p) n -> p kc n", p=128))
        p.conv("glu%d" % i, p.gluS[i], p.glu_w[i].rearrange("(fi p) n -> p fi n", p=128))
        for dg in range(4):
            for ci in range(4):
                dc = dg * 4 + ci
                p.conv("wout%d" % i, p.woutS[i][dg, :, ci], p.w_out[i][:, dc * 128:(dc + 1) * 128].rearrange("(kc p) n -> p kc n", p=128))
        self.convert_ffn(i, 1)

    def convert_ffn(self, i, f):
        p = self
        for jp in range(FC // 2):
            for jj in range(2):
                j = 2 * jp + jj
                p.conv("wgu%d%d_%d" % (i, f, jp // 6), p.wguS[i][f][jp, :, jj, 0], p.w_gate[i, f][:, j * 128:(j + 1) * 128].rearrange("(kc p) n -> p kc n", p=128))
                p.conv("wgu%d%d_%d" % (i, f, jp // 6), p.wguS[i][f][jp, :, jj, 1], p.w_up[i, f][:, j * 128:(j + 1) * 128].rearrange("(kc p) n -> p kc n", p=128))
        for dc in range(DC):
            for h2 in range(2):
                p.conv("wd%d%d_%d" % (i, f, dc // 8), p.wdS[i][f][dc, :, h2 * 22:(h2 + 1) * 22],
                       p.w_down[i, f][h2 * 2816:(h2 + 1) * 2816, dc * 128:(dc + 1) * 128].rearrange("(j p) n -> p j n", p=128))

    def setup(self):
        p, fw, nc = self, self.fw, self.nc
        p.bpair = [fw.psum("bpair", [128, 2, 512], F32) for _ in range(4)]
        p.banks = [p.bpair[k // 2][:, k % 2, :] for k in range(8)]
        p.bb = [Buf("bank%d" % k, excl=True) for k in range(8)]
        p.ones = fw.sbuf("ones", [128, 128], BF16)
        p.bconst = Buf("const")
        p.ident = fw.sbuf("ident", [128, 128], F32)
        p.ones32 = fw.sbuf("ones32", [128, 128], F32)
        p.epsc = fw.sbuf("epsc", [128, 1], F32)
        p.negpi = fw.sbuf("negpi", [128, 1], F32)
        p.mT = fw.sbuf("mT", [128, 144, 2], F32)
        p.bmT = Buf("mT")
        p.Acol = fw.sbuf("Acol", [128, 3, DC, 2], F32)
        p.Gcol = fw.sbuf("Gcol", [128, 3, DC, 2], F32)
        p.bcols = Buf("cols")
        p.ngT = fw.sbuf("ngT", [128, 2, 3, DC], F32)
        p.fgT = fw.sbuf("fgT", [128, DC], F32)
        p.hg = fw.sbuf("hg", [128, 2, 5], F32)
        p.s5d = fw.sbuf("s5d", [128, 2, 2, 4], F32)
        p.sq = [fw.sbuf("sq", [128, 512], BF16) for _ in range(2)]
        p.bsq = bufs(2, "sq")
        p.tf = [fw.sbuf("tf", [128, 512], F32) for _ in range(4)]
        p.btf = bufs(4, "tf")
        p.rs = fw.sbuf("rs", [128, 512], F32)
        p.brs = Buf("rs")
        p.lamc = fw.sbuf("lamc", [128, 4], F32)
        p.blamc = Buf("lamc")
        bc = p.bconst
        fw.op(fw.pool, lambda e: e.memset(p.ones[:], 1.0), writes=[bc])
        fw.op(fw.pool, lambda e: e.memset(p.ones32[:], 1.0), writes=[bc])
        fw.op(fw.pool, lambda e: e.memset(p.ident[:], 0.0), writes=[bc])
        fw.op(fw.pool, lambda e: e.affine_select(out=p.ident[:], in_=p.ident[:], pattern=[[-1, 128]], compare_op=ALU.not_equal,
                                                  fill=1.0, base=0, channel_multiplier=1), reads=[bc], writes=[bc])
        fw.op(fw.pool, lambda e: e.memset(p.epsc[:], EPS), writes=[bc])
        fw.op(fw.pool, lambda e: e.memset(p.negpi[:], -PI), writes=[bc])
        bl = Buf("smallloads")
        fw.dma(fw.sp, p.ngT[:], p.norm_gT.rearrange("i p k c -> p i k c"), writes=[bl], sbuf_side=bl)
        fw.dma(fw.sp, p.fgT[:], p.final_gT, writes=[bl], sbuf_side=bl)
        fw.dma(fw.sp, p.hg[:], p.hgT.rearrange("i p k -> p i k"), writes=[bl], sbuf_side=bl)
        fw.dma(fw.sp, p.s5d[:], p.s5dT.rearrange("i p a k -> p i a k"), writes=[bl], sbuf_side=bl)
        p.bsmall = bl

    def rope_tables(self):
        p, fw = self, self.fw
        with self.fw.scope() as sc:
            di = fw.sbuf("di", [128, 1], F32, sc)
            col = fw.sbuf("col", [128, 8], F32, sc)
            prow = fw.sbuf("prow", [128, LAT], F32, sc)
            pcol = fw.sbuf("pcol", [128, LAT], F32, sc)
            ang = fw.sbuf("ang", [128, LAT], F32, sc)
            tb = fw.sbuf("tb", [128, LAT], F32, sc)
            prow2 = fw.sbuf("prow2", [128, LAT], F32, sc)
            b = Buf("rope")
            fw.op(fw.pool, lambda e: e.iota(prow[:].rearrange("p (r c) -> p r c", c=64), pattern=[[1, 64], [0, 64]], base=0,
                                            channel_multiplier=0, allow_small_or_imprecise_dtypes=True), writes=[b])
            fw.op(fw.pool, lambda e: e.iota(pcol[:].rearrange("p (r c) -> p r c", c=64), pattern=[[0, 64], [1, 64]], base=0,
                                            channel_multiplier=0, allow_small_or_imprecise_dtypes=True), reads=[b], writes=[b])
            fw.op(fw.pool, lambda e: e.iota(di[:], pattern=[[0, 1]], base=0, channel_multiplier=1,
                                            allow_small_or_imprecise_dtypes=True), reads=[b], writes=[b])
            dv = fw.dve

            def o(fn):
                fw.op(dv, fn, reads=[b], writes=[b])
            MAGIC = 12582912.0
            o(lambda e: e.tensor_single_scalar(out=col[:, 3:4], in_=di[:], scalar=63.5, op=ALU.is_gt))
            o(lambda e: e.scalar_tensor_tensor(out=col[:, 6:7], in0=col[:, 3:4], scalar=-64.0, in1=di[:], op0=ALU.mult, op1=ALU.add))
            o(lambda e: e.tensor_single_scalar(out=col[:, 4:5], in_=col[:, 6:7], scalar=31.5, op=ALU.is_gt))
            o(lambda e: e.scalar_tensor_tensor(out=col[:, 7:8], in0=col[:, 4:5], scalar=-32.0, in1=col[:, 6:7], op0=ALU.mult, op1=ALU.add))
            o(lambda e: e.tensor_single_scalar(out=col[:, 5:6], in_=col[:, 7:8], scalar=15.5, op=ALU.is_gt))
            o(lambda e: e.scalar_tensor_tensor(out=col[:, 6:7], in0=col[:, 5:6], scalar=-16.0, in1=col[:, 7:8], op0=ALU.mult, op1=ALU.add))
            for kind in range(2):
                half = 32 if kind == 0 else 16
                jcol = col[:, 7:8] if kind == 0 else col[:, 6:7]
                rowb = col[:, 3:4] if kind == 0 else col[:, 4:5]
                sgnb = col[:, 4:5] if kind == 0 else col[:, 5:6]
                fw.op(fw.act, lambda e: e.activation(out=col[:, 0:1], in_=jcol, func=AF.Exp, scale=-math.log(10000.0) / half),
                      reads=[b], writes=[b])
                o(lambda e: e.tensor_scalar(out=col[:, 1:2], in0=rowb, scalar1=-1.0, scalar2=1.0, op0=ALU.mult, op1=ALU.add))
                o(lambda e: e.tensor_scalar(out=col[:, 2:3], in0=sgnb, scalar1=2.0, scalar2=-1.0, op0=ALU.mult, op1=ALU.add))
                o(lambda e: e.tensor_tensor(out=ang[:], in0=prow[:], in1=pcol[:], op=ALU.subtract))
                o(lambda e: e.scalar_tensor_tensor(out=ang[:], in0=ang[:], scalar=col[:, 1:2], in1=pcol[:], op0=ALU.mult, op1=ALU.add))
                o(lambda e: e.tensor_scalar(out=ang[:], in0=ang[:], scalar1=col[:, 0:1], scalar2=None, op0=ALU.mult))
                for which in range(2):
                    if which == 0:
                        o(lambda e: e.tensor_scalar(out=tb[:], in0=ang[:], scalar1=0.5 * PI, scalar2=None, op0=ALU.add))
                    else:
                        o(lambda e: e.tensor_copy(out=tb[:], in_=ang[:]))
                    o(lambda e: e.tensor_scalar(out=prow2[:], in0=tb[:], scalar1=1.0 / (2 * PI), scalar2=MAGIC, op0=ALU.mult, op1=ALU.add))
                    o(lambda e: e.tensor_scalar(out=prow2[:], in0=prow2[:], scalar1=-MAGIC, scalar2=None, op0=ALU.add))
                    o(lambda e: e.scalar_tensor_tensor(out=tb[:], in0=prow2[:], scalar=-2 * PI, in1=tb[:], op0=ALU.mult, op1=ALU.add))
                    o(lambda e: e.tensor_scalar(out=tb[:], in0=tb[:], scalar1=PI, scalar2=-PI, op0=ALU.min, op1=ALU.max))
                    fw.op(fw.act, lambda e: e.activation(out=tb[:], in_=tb[:], func=AF.Sin), reads=[b], writes=[b])
                    if which == 1:
                        o(lambda e: e.tensor_scalar(out=tb[:], in0=tb[:], scalar1=col[:, 2:3], scalar2=None, op0=ALU.mult))
                    fw.dma(fw.sp, p.ropeT[2 * kind + which], tb[:], reads=[b], sbuf_side=b)

    def pass_alloc(self, sc):
        p, fw = self, self.fw
        p.ws = WStream(fw, 4, sc)
        p.xt = fw.sbuf("xt", [128, DC, 512], F32, sc)
        p.bxt = bufs(DC, "xt")
        p.ht = fw.sbuf("ht", [128, DC, 512], BF16, sc)
        p.bht = bufs(DC, "ht")

    def modulation(self, i):
        p, fw = self, self.fw
        with self.fw.scope() as sc:
            ws = p.ws = WStream(fw, 4, sc)
            cs = fw.sbuf("cs", [128, DC, 2], F32, sc)
            sc_b = fw.sbuf("scb", [128, DC, 2], BF16, sc)
            adab = fw.sbuf("adab", [128, 144], F32, sc)
            bl = Buf("modl")
            fw.dma(fw.sp, cs[:], p.cT, writes=[bl], sbuf_side=bl)
            fw.dma(fw.sp, adab[:], p.ada_bT[i], writes=[bl], sbuf_side=bl)
            fw.op(fw.act, lambda e: e.activation(out=sc_b[:], in_=cs[:], func=AF.Silu), reads=[bl], writes=[bl])
            wb = p.wbuf["ada%d" % i]
            for blk in range(36):
                ws.enqueue(lambda t, b, blk=blk: fw.dma(fw.sp, t[:, :].rearrange("p (kc n) -> p kc n", kc=DC), p.adaS[i][blk],
                                                         reads=[wb], writes=[b], sbuf_side=b))
            bank, bbk = p.banks[0], p.bb[0]
            for blk in range(36):
                t, b = ws.take()
                tv = t[:, :].rearrange("p (kc n) -> p kc n", kc=DC)
                for c4 in range(4):
                    cc = blk * 4 + c4
                    fw.group(fw.pe, [
                        (lambda e, kc=kc, c4=c4, cc=cc: e.matmul(bank[:, 2 * cc:2 * cc + 2], lhsT=tv[:, kc, c4 * 128:(c4 + 1) * 128],
                                                                 rhs=sc_b[:, kc, :], start=(kc == 0), stop=(kc == DC - 1)))
                        for kc in range(DC)], reads=[b, bl], writes=[bbk])
            fw.op(fw.dve, lambda e: e.tensor_tensor(out=p.mT[:], in0=bank[:, 0:288].rearrange("p (c t) -> p c t", t=2),
                                                    in1=adab[:].unsqueeze(2).broadcast_to([128, 144, 2]), op=ALU.add),
                  reads=[bbk, bl], writes=[p.bmT])
            m4 = p.mT[:].rearrange("p (k c) t -> p k c t", c=DC)
            for k in range(3):
                g = p.ngT[:, i, k, :].unsqueeze(2).broadcast_to([128, DC, 2])
                fw.op(fw.dve, lambda e, k=k: e.tensor_scalar(out=p.Acol[:, k], in0=m4[:, 3 * k + 1], scalar1=1.0, scalar2=None, op0=ALU.add),
                      reads=[p.bmT], writes=[p.bcols])
                fw.op(fw.dve, lambda e, k=k, g=g: e.tensor_tensor(out=p.Acol[:, k], in0=p.Acol[:, k], in1=g, op=ALU.mult),
                      reads=[p.bcols, p.bsmall], writes=[p.bcols])
                fw.op(fw.dve, lambda e, k=k: e.tensor_scalar(out=p.Gcol[:, k], in0=m4[:, 3 * k + 2], scalar1=(1.0 if k == 1 else 0.5),
                                                            scalar2=None, op0=ALU.mult), reads=[p.bmT, p.bcols], writes=[p.bcols])
            if "mdbg" in p.debug:
                fw.dma(fw.act, p.mdbg, p.mT[:], reads=[p.bmT], sbuf_side=p.bmT)
            fw.barrier()

    def rmsnorm_mod(self, i, k, c, N):
        p, fw = self, self.fw
        ssb, bss = p.banks[6], p.bb[6]
        for kc in range(DC):
            s, bs = p.sq[kc % 2], p.bsq[kc % 2]
            fw.op(fw.act, lambda e, kc=kc, s=s: e.activation(out=s[:, :N], in_=p.xt[:, kc, :N], func=AF.Square),
                  reads=[p.bxt[kc]], writes=[bs])
            fw.op(fw.pe, lambda e, kc=kc, s=s: e.matmul(ssb[:, :N], lhsT=p.ones[:], rhs=s[:, :N], start=(kc == 0), stop=(kc == DC - 1)),
                  reads=[bs, p.bconst], writes=[bss])
        fw.op(fw.act, lambda e: e.activation(out=p.rs[:, :N], in_=ssb[:, :N], func=AF.Sqrt, scale=1.0 / D, bias=p.epsc[:]),
              reads=[bss, p.bconst], writes=[p.brs])
        fw.op(fw.dve, lambda e: e.reciprocal(out=p.rs[:, :N], in_=p.rs[:, :N]), reads=[p.brs], writes=[p.brs])
        m4 = p.mT[:].rearrange("p (k c) t -> p k c t", c=DC)
        for kc in range(DC):
            t, bt = p.tf[kc % 2], p.btf[kc % 2]
            fw.op(fw.dve, lambda e, kc=kc, t=t: e.tensor_tensor(out=t[:, :N], in0=p.xt[:, kc, :N], in1=p.rs[:, :N], op=ALU.mult),
                  reads=[p.bxt[kc], p.brs], writes=[bt])
            fw.op(fw.act, lambda e, kc=kc, t=t: e.activation(out=p.ht[:, kc, :N], in_=t[:, :N], func=AF.Identity,
                                                             scale=p.Acol[:, k, kc, c:c + 1], bias=m4[:, 3 * k, kc, c:c + 1]),
                  reads=[bt, p.bcols, p.bmT], writes=[p.bht[kc]])

    def enqueue_ffn(self, i, f):
        p, fw, ws = self, self.fw, self.ws
        for jp in range(FC // 2):
            ws.enqueue(lambda t, b, jp=jp: fw.dma(fw.sp, t[:, :].rearrange("p (a g kc n) -> p a g kc n", a=2, g=2, kc=DC),
                                                   p.wguS[i][f][jp], reads=[p.wbuf["wgu%d%d_%d" % (i, f, jp // 6)]], writes=[b], sbuf_side=b))
        for dc in range(DC):
            ws.enqueue(lambda t, b, dc=dc: fw.dma(fw.sp, t[:, 0:FC * 128].rearrange("p (j n) -> p j n", j=FC),
                                                   p.wdS[i][f][dc], reads=[p.wbuf["wd%d%d_%d" % (i, f, dc // 8)]], writes=[b], sbuf_side=b))

    def ffn(self, i, f, k, c, N, act, bact):
        p, fw, ws = self, self.fw, self.ws
        for jp in range(FC // 2):
            t, b = ws.take()
            tv = t[:, :].rearrange("p (a g kc n) -> p a g kc n", a=2, g=2, kc=DC)
            for jj in range(2):
                j = 2 * jp + jj
                gb, bgb = p.banks[j % 2], p.bb[j % 2]
                ub, bub = p.banks[2 + j % 2], p.bb[2 + j % 2]
                fw.group(fw.pe, [(lambda e, kc=kc, jj=jj, gb=gb: e.matmul(gb[:, :N], lhsT=tv[:, jj, 0, kc, :], rhs=p.ht[:, kc, :N],
                                                                         start=(kc == 0), stop=(kc == DC - 1))) for kc in range(DC)],
                         reads=[b] + p.bht, writes=[bgb])
                fw.group(fw.pe, [(lambda e, kc=kc, jj=jj, ub=ub: e.matmul(ub[:, :N], lhsT=tv[:, jj, 1, kc, :], rhs=p.ht[:, kc, :N],
                                                                         start=(kc == 0), stop=(kc == DC - 1))) for kc in range(DC)],
                         reads=[b] + p.bht, writes=[bub])
                st, bst = p.tf[2 + j % 2], p.btf[2 + j % 2]
                fw.op(fw.act, lambda e, gb=gb, st=st: e.activation(out=st[:, :N], in_=gb[:, :N], func=AF.Silu), reads=[bgb], writes=[bst])
                fw.op(fw.dve, lambda e, ub=ub, st=st, j=j: e.tensor_tensor(out=act[:, j, :N], in0=st[:, :N], in1=ub[:, :N], op=ALU.mult),
                      reads=[bst, bub], writes=[bact[j]])
        for dc in range(DC):
            t, b = ws.take()
            tv = t[:, 0:FC * 128].rearrange("p (j n) -> p j n", j=FC)
            yb, byb = p.banks[4 + dc % 2], p.bb[4 + dc % 2]
            fw.group(fw.pe, [(lambda e, j=j, yb=yb: e.matmul(yb[:, :N], lhsT=tv[:, j, :], rhs=act[:, j, :N], start=(j == 0), stop=(j == FC - 1)))
                             for j in range(FC)], reads=[b] + bact, writes=[byb])
            fw.op(fw.dve, lambda e, dc=dc, yb=yb: e.scalar_tensor_tensor(out=p.xt[:, dc, :N], in0=yb[:, :N], scalar=p.Gcol[:, k, dc, c:c + 1],
                                                                        in1=p.xt[:, dc, :N], op0=ALU.mult, op1=ALU.add),
                  reads=[byb, p.bcols, p.bxt[dc]], writes=[p.bxt[dc]])

    def load_x_tile(self, i, t0, N, is_ctx):
        p, fw = self, self.fw
        if i > 0:
            bl = p.bxt
            fw.dma(fw.sp, p.xt[:, :, :N], p.xT.rearrange("(kc q) t -> q kc t", q=128)[:, :, t0:t0 + N], writes=bl, sbuf_side=bl[0])
            return
        with self.fw.scope() as sc:
            xtok = [fw.sbuf("xtok", [128, D], F32, sc) for _ in range(2)]
            bxk = bufs(2, "xtok")
            nb = 0
            for blk in range(N // 128):
                src = p.ctx[blk * 128:(blk + 1) * 128, :] if is_ctx else p.x[t0 - CTX + blk * 128: t0 - CTX + (blk + 1) * 128, :]
                xk, bk = xtok[blk % 2], bxk[blk % 2]
                fw.dma(fw.sp, xk[:], src, writes=[bk], sbuf_side=bk)
                for dg in range(4):
                    bank, bbk = p.banks[nb % 4], p.bb[nb % 4]
                    nb += 1
                    fw.group(fw.pe, [(lambda e, q=q, dg=dg, bank=bank, xk=xk: e.transpose(out=bank[:, q * 128:(q + 1) * 128],
                                                                                      in_=xk[:, (dg * 4 + q) * 128:(dg * 4 + q + 1) * 128],
                                                                                      identity=p.ident[:])) for q in range(4)],
                             reads=[bk, p.bconst], writes=[bbk])
                    eng = fw.act if dg % 2 == 0 else fw.dve
                    outap = p.xt[:, dg * 4:(dg + 1) * 4, blk * 128:(blk + 1) * 128]
                    inap = bank[:, :].rearrange("p (q n) -> p q n", q=4)
                    if eng is fw.act:
                        fw.op(eng, lambda e, outap=outap, inap=inap: e.activation(out=outap, in_=inap, func=AF.Copy),
                              reads=[bbk], writes=p.bxt[dg * 4:(dg + 1) * 4])
                    else:
                        fw.op(eng, lambda e, outap=outap, inap=inap: e.tensor_copy(out=outap, in_=inap),
                              reads=[bbk], writes=p.bxt[dg * 4:(dg + 1) * 4])
            fw.barrier()

    def pass_a(self, i):
        with self.fw.scope() as sc:
            self.pass_alloc(sc)
            self._pass_a(i)

    def _pass_a(self, i):
        p, fw, ws = self, self.fw, self.ws
        wbin = p.wbuf["win%d" % i]
        wbv = p.wbuf["wv%d" % i]
        for (t0, N, is_ctx) in TILES:
            p.enqueue_ffn(i, 0)
            for g in range(10):
                if is_ctx and g in (2, 4, 6, 8):
                    continue
                ws.enqueue(lambda t, b, g=g: fw.dma(fw.sp, t[:, :].rearrange("p (c kc n) -> p c kc n", c=4, kc=DC), p.winS[i][g],
                                                     reads=[wbin], writes=[b], sbuf_side=b))
            ws.enqueue(lambda t, b: fw.dma(fw.sp, t[:, :].rearrange("p (kc n) -> p kc n", kc=DC), p.wvdS[i],
                                           reads=[wbv], writes=[b], sbuf_side=b))
            ws.enqueue(lambda t, b: fw.dma(fw.sp, t[:, 0:DC * 256].rearrange("p (kc n) -> p kc n", kc=DC), p.wvgS[i],
                                           reads=[wbv], writes=[b], sbuf_side=b))
        for (t0, N, is_ctx) in TILES[:p.ntiles]:
            c = 1 if is_ctx else 0
            p.load_x_tile(i, t0, N, is_ctx)
            with self.fw.scope() as sc:
                act = fw.sbuf("act", [128, FC, 512], BF16, sc)
                bact = bufs(FC, "act")
                p.rmsnorm_mod(i, 0, c, N)
                p.ffn(i, 0, 0, c, N, act, bact)
                fw.barrier()
            fw.dma(fw.act, p.xT.rearrange("(kc q) t -> q kc t", q=128)[:, :, t0:t0 + N], p.xt[:, :, :N], reads=p.bxt, sbuf_side=p.bxt[0])
            if p.stop == "ffn":
                continue
            p.rmsnorm_mod(i, 1, c, N)
            with self.fw.scope() as sc:
                p.in_proj(i, t0, N, is_ctx, sc)
                fw.barrier()

    def in_proj(self, i, t0, N, is_ctx, sc):
        p, fw, ws = self, self.fw, self.ws
        ust = fw.sbuf("ust", [128, 4, 512], F32, sc)
        usb = fw.sbuf("usb", [128, 4, 512], BF16, sc)
        qst = [fw.sbuf("qst", [128, 4, 512], BF16, sc) for _ in range(2)]
        bqst = [Buf("qst0"), Buf("qst1")]
        vsd = fw.sbuf("vsd", [128, 4, 512], BF16, sc)
        vsg = fw.sbuf("vsg", [128, 4, 256], BF16, sc)
        bu, bub, bvd, bvg = Buf("ust"), Buf("usb"), Buf("vsd"), Buf("vsg")
        rp = fw.sbuf("rp", [128, 4, 512], F32, sc)
        rq = fw.sbuf("rq", [128, 4, 512], F32, sc)
        brp, brq = Buf("rp"), Buf("rq")
        if not is_ctx:
            l0 = t0 - CTX
            fw.dma(fw.sp, rp[:, :, :N], p.ropeT.rearrange("k q t -> q k t")[:, :, l0:l0 + N], writes=[brp], sbuf_side=brp)
            for n, (tb, gi) in enumerate([(0, 1), (1, 2), (0, 3), (1, 4)]):
                fw.op(fw.dve, lambda e, n=n, tb=tb, gi=gi: e.tensor_scalar(out=rq[:, n, :N], in0=rp[:, tb, :N], scalar1=p.hg[:, i, gi:gi + 1],
                                                                           scalar2=None, op0=ALU.mult), reads=[brp, p.bsmall], writes=[brq])
        nbank = [0]

        def mm_chunk(tv, ci):
            bank, bbk = p.banks[nbank[0] % 6], p.bb[nbank[0] % 6]
            nbank[0] += 1
            return bank, bbk, [(lambda e, kc=kc: e.matmul(bank[:, :N], lhsT=tv[:, ci, kc, :], rhs=p.ht[:, kc, :N], start=(kc == 0),
                                                          stop=(kc == DC - 1))) for kc in range(DC)]

        def view(t):
            return t[:, :].rearrange("p (c kc n) -> p c kc n", c=4, kc=DC)
        t, b = ws.take()
        tv = view(t)
        for ci in range(4):
            bank, bbk, mms = mm_chunk(tv, ci)
            fw.group(fw.pe, mms, reads=[b] + p.bht, writes=[bbk])
            fw.op(fw.act, lambda e, ci=ci, bank=bank: e.activation(out=ust[:, ci, :N], in_=bank[:, :N], func=AF.Copy), reads=[bbk], writes=[bu])
            fw.op(fw.dve, lambda e, ci=ci: e.tensor_copy(out=usb[:, ci, :N], in_=ust[:, ci, :N]), reads=[bu], writes=[bub])
        fw.dma(fw.act, p.uT.rearrange("(c q) t -> q c t", q=128)[:, :, t0:t0 + N], ust[:, :, :N], reads=[bu], sbuf_side=bu)
        fw.dma(fw.act, p.uTb.rearrange("(c q) t -> q c t", q=128)[:, :, t0:t0 + N], usb[:, :, :N], reads=[bub], sbuf_side=bub)

        def qk_group(dst, dst_c0, nchunks, kind, sidx, gain_main, tabs):
            st, bs_ = qst[sidx], bqst[sidx]
            tm, bm = ws.take()
            tvm = view(tm)
            if not is_ctx and kind != 'gk':
                tr, br = ws.take()
                tvr = view(tr)
            elif not is_ctx:
                tr, br, tvr = tm, bm, tvm
            for ci in range(nchunks):
                bankA, bbA, mmsA = mm_chunk(tvm, ci)
                fw.group(fw.pe, mmsA, reads=[bm] + p.bht, writes=[bbA])
                if not is_ctx:
                    bankB, bbB, mmsB = mm_chunk(tvr, ci + (2 if kind == 'gk' else 0))
                    fw.group(fw.pe, mmsB, reads=[br] + p.bht, writes=[bbB])
                if kind != 'd':
                    s, bs = p.sq[ci % 2], p.bsq[ci % 2]
                    ssb, bss = p.banks[6 + ci % 2], p.bb[6 + ci % 2]
                    fw.op(fw.act, lambda e, s=s, bankA=bankA: e.activation(out=s[:, :N], in_=bankA[:, :N], func=AF.Square), reads=[bbA], writes=[bs])
                    fw.op(fw.pe, lambda e, s=s, ssb=ssb: e.matmul(ssb[:, :N], lhsT=p.ones[:], rhs=s[:, :N], start=True, stop=True),
                          reads=[bs, p.bconst], writes=[bss])
                    fw.op(fw.act, lambda e, ssb=ssb: e.activation(out=p.rs[:, :N], in_=ssb[:, :N], func=AF.Sqrt, scale=1.0 / 128, bias=p.epsc[:]),
                          reads=[bss, p.bconst], writes=[p.brs])
                    fw.op(fw.dve, lambda e: e.reciprocal(out=p.rs[:, :N], in_=p.rs[:, :N]), reads=[p.brs], writes=[p.brs])
                if is_ctx:
                    if kind == 'd':
                        fw.op(fw.dve, lambda e, ci=ci, bankA=bankA: e.tensor_copy(out=st[:, ci, :N], in_=bankA[:, :N]), reads=[bbA], writes=[bs_])
                    else:
                        fw.op(fw.dve, lambda e, ci=ci, bankA=bankA: e.scalar_tensor_tensor(out=st[:, ci, :N], in0=bankA[:, :N],
                                                                                         scalar=p.hg[:, i, gain_main:gain_main + 1],
                                                                                         in1=p.rs[:, :N], op0=ALU.mult, op1=ALU.mult),
                              reads=[bbA, p.brs, p.bsmall], writes=[bs_])
                else:
                    tabt, btab = (rp, brp) if kind == 'd' else (rq, brq)
                    t1, b1 = p.tf[0], p.btf[0]
                    t2, b2 = p.tf[1], p.btf[1]
                    fw.op(fw.dve, lambda e, bankA=bankA, t1=t1: e.tensor_tensor(out=t1[:, :N], in0=bankA[:, :N], in1=tabt[:, tabs[0], :N], op=ALU.mult),
                          reads=[bbA, btab], writes=[b1])
                    fw.op(fw.dve, lambda e, bankB=bankB, t2=t2: e.tensor_tensor(out=t2[:, :N], in0=bankB[:, :N], in1=tabt[:, tabs[1], :N], op=ALU.mult),
                          reads=[bbB, btab], writes=[b2])
                    if kind == 'd':
                        fw.op(fw.dve, lambda e, ci=ci: e.tensor_tensor(out=st[:, ci, :N], in0=t1[:, :N], in1=t2[:, :N], op=ALU.add),
                              reads=[b1, b2], writes=[bs_])
                    else:
                        fw.op(fw.dve, lambda e: e.tensor_tensor(out=t1[:, :N], in0=t1[:, :N], in1=t2[:, :N], op=ALU.add),
                              reads=[b1, b2], writes=[b1])
                        fw.op(fw.dve, lambda e, ci=ci: e.tensor_tensor(out=st[:, ci, :N], in0=t1[:, :N], in1=p.rs[:, :N], op=ALU.mult),
                              reads=[b1, p.brs], writes=[bs_])
            fw.dma(fw.act, dst.rearrange("(c q) t -> q c t", q=128)[:, dst_c0:dst_c0 + nchunks, t0:t0 + N], st[:, 0:nchunks, :N],
                   reads=[bs_], sbuf_side=bs_)
        qk_group(p.qd, 0, 4, 'd', 0, None, (2, 3))
        qk_group(p.kd, 0, 4, 'd', 1, None, (2, 3))
        qk_group(p.qg, 0, 4, 'g', 0, 1, (0, 1))
        qk_group(p.qg, 4, 4, 'g', 1, 1, (0, 1))
        qk_group(p.kg, 0, 2, 'gk', 0, 3, (2, 3))
        tvd, bvdw = ws.take()
        tvg, bvgw = ws.take()
        tvdv = tvd[:, :].rearrange("p (kc n) -> p kc n", kc=DC)
        tvgv = tvg[:, 0:DC * 256].rearrange("p (kc n) -> p kc n", kc=DC)
        for blk in range(N // 128):
            bank, bbk = p.banks[blk % 2], p.bb[blk % 2]
            fw.group(fw.pe, [(lambda e, kc=kc, bank=bank, blk=blk: e.matmul(bank[:, :], lhsT=p.ht[:, kc, blk * 128:(blk + 1) * 128], rhs=tvdv[:, kc, :],
                                                                          start=(kc == 0), stop=(kc == DC - 1))) for kc in range(DC)],
                     reads=[bvdw] + p.bht, writes=[bbk])
            fw.op(fw.act, lambda e, blk=blk, bank=bank: e.activation(out=vsd[:, blk, :], in_=bank[:, :], func=AF.Copy), reads=[bbk], writes=[bvd])
            bank2, bbk2 = p.banks[2 + blk % 2], p.bb[2 + blk % 2]
            fw.group(fw.pe, [(lambda e, kc=kc, bank2=bank2, blk=blk: e.matmul(bank2[:, 0:256], lhsT=p.ht[:, kc, blk * 128:(blk + 1) * 128], rhs=tvgv[:, kc, :],
                                                                            start=(kc == 0), stop=(kc == DC - 1))) for kc in range(DC)],
                     reads=[bvgw] + p.bht, writes=[bbk2])
            fw.op(fw.dve, lambda e, blk=blk, bank2=bank2: e.tensor_copy(out=vsg[:, blk, :], in_=bank2[:, 0:256]), reads=[bbk2], writes=[bvg])
        nbk = N // 128
        b0 = t0 // 128
        for hh in range(4):
            fw.dma(fw.act, p.vd[hh, :, b0:b0 + nbk, :], vsd[:, 0:nbk, hh * 128:(hh + 1) * 128], reads=[bvd], sbuf_side=bvd)
        for hh in range(2):
            fw.dma(fw.act, p.vg[hh, :, b0:b0 + nbk, :], vsg[:, 0:nbk, hh * 128:(hh + 1) * 128], reads=[bvg], sbuf_side=bvg)

    def s5(self, i):
        p, fw = self, self.fw
        MAGIC = 12582912.0
        with fw.scope() as sc:
            dv, pl, ac = fw.dve, fw.pool, fw.act
            bs = Buf("s5setup")
            lamc = fw.sbuf("lamc", [128, 2, 3, 16], F32, sc)
            fw.dma(fw.sp, lamc[:], p.lamT[i].rearrange("d q k c -> q d k c"), writes=[bs], sbuf_side=bs)
            Uf = fw.sbuf("Uf", [128, 128], BF16, sc)
            Ur = fw.sbuf("Ur", [128, 128], BF16, sc)
            with fw.scope() as sc0:
                U32 = fw.sbuf("U32", [128, 128], F32, sc0)
                for (Ux, pat, cm) in ((Uf, 1, -1), (Ur, -1, 1)):
                    fw.op(pl, lambda e: e.memset(U32[:], 1.0), reads=[bs], writes=[bs])
                    fw.op(pl, lambda e: e.affine_select(out=U32[:], in_=U32[:], pattern=[[pat, 128]], compare_op=ALU.is_ge, fill=0.0, base=0,
                                                        channel_multiplier=cm), reads=[bs], writes=[bs])
                    fw.op(pl, lambda e: e.tensor_copy(out=Ux[:], in_=U32[:]), reads=[bs], writes=[bs])
            PCt = [fw.sbuf("PCt", [128, 2048], F32, sc) for _ in range(2)]
            PSt = [fw.sbuf("PSt", [128, 2048], F32, sc) for _ in range(2)]
            QC = [fw.sbuf("QC", [128, 16, 128], F32, sc) for _ in range(2)]
            QS = [fw.sbuf("QS", [128, 16, 128], F32, sc) for _ in range(2)]
            RB = [fw.sbuf("RB", [128, 4, 1024], BF16, sc) for _ in range(2)]
            CT = [fw.sbuf("CT", [128, 16, 2, 128], BF16, sc) for _ in range(2)]
            carry = fw.sbuf("carry", [128, 2, 16, 2], F32, sc)
            btab = Buf("s5tab")

            def o(eng, fn):
                fw.op(eng, fn, reads=[bs, p.bconst], writes=[bs])
            with fw.scope() as sc2:
                cols = fw.sbuf("cols", [128, 2, 12, 16], F32, sc2)
                kv = fw.sbuf("kv", [128, 128], F32, sc2)
                ang = fw.sbuf("ang", [128, 16, 128], F32, sc2)
                nn = fw.sbuf("nn", [128, 16, 128], F32, sc2)
                tc_ = fw.sbuf("tc", [128, 16, 128], F32, sc2)
                ts_ = fw.sbuf("ts", [128, 16, 128], F32, sc2)
                mg = fw.sbuf("mg", [128, 16, 128], F32, sc2)
                xb = fw.sbuf("xb", [128, 2, 16, 128], F32, sc2)
                xw = fw.sbuf("xw", [128, 2, 128], F32, sc2)

                def wrap_sin(dst, src, shift):
                    o(dv, lambda e: e.tensor_scalar(out=dst, in0=src, scalar1=shift, scalar2=None, op0=ALU.add))
                    o(dv, lambda e: e.tensor_scalar(out=nn[:], in0=dst, scalar1=1.0 / (2 * PI), scalar2=MAGIC, op0=ALU.mult, op1=ALU.add))
                    o(dv, lambda e: e.tensor_scalar(out=nn[:], in0=nn[:], scalar1=-MAGIC, scalar2=-2 * PI, op0=ALU.add, op1=ALU.mult))
                    o(dv, lambda e: e.tensor_tensor(out=dst, in0=dst, in1=nn[:], op=ALU.add))
                    o(dv, lambda e: e.tensor_scalar(out=dst, in0=dst, scalar1=PI, scalar2=-PI, op0=ALU.min, op1=ALU.max))
                    o(ac, lambda e: e.activation(out=dst, in_=dst, func=AF.Sin))

                def gen_table(d, base, step):
                    o(pl, lambda e: e.iota(kv[:], pattern=[[step, 128]], base=base, channel_multiplier=0, allow_small_or_imprecise_dtypes=True))
                    for q in range(16):
                        o(dv, lambda e, q=q: e.tensor_scalar(out=ang[:, q, :], in0=kv[:], scalar1=cols[:, d, 2, q:q + 1], scalar2=None, op0=ALU.mult))
                        o(ac, lambda e, q=q: e.activation(out=mg[:, q, :], in_=kv[:], func=AF.Exp, scale=cols[:, d, 1, q:q + 1]))
                    wrap_sin(ts_[:], ang[:], 0.0)
                    wrap_sin(tc_[:], ang[:], 0.5 * PI)
                    o(dv, lambda e: e.tensor_tensor(out=ts_[:], in0=ts_[:], in1=mg[:], op=ALU.mult))
                    o(dv, lambda e: e.tensor_tensor(out=tc_[:], in0=tc_[:], in1=mg[:], op=ALU.mult))

                for d in range(2):
                    c_ = lambda k: cols[:, d, k, :]
                    o(ac, lambda e: e.activation(out=c_(0), in_=lamc[:, d, 2, :], func=AF.Exp))
                    o(dv, lambda e: e.tensor_tensor(out=c_(1), in0=lamc[:, d, 0, :], in1=c_(0), op=ALU.mult))
                    o(dv, lambda e: e.tensor_tensor(out=c_(2), in0=lamc[:, d, 1, :], in1=c_(0), op=ALU.mult))
                    o(ac, lambda e: e.activation(out=c_(9), in_=c_(1), func=AF.Exp))
                    for (dst, shift) in ((4, 0.0), (3, 0.5 * PI)):
                        o(dv, lambda e: e.tensor_scalar(out=c_(11), in0=c_(2), scalar1=shift, scalar2=None, op0=ALU.add))
                        o(dv, lambda e: e.tensor_scalar(out=c_(10), in0=c_(11), scalar1=1.0 / (2 * PI), scalar2=MAGIC, op0=ALU.mult, op1=ALU.add))
                        o(dv, lambda e: e.tensor_scalar(out=c_(10), in0=c_(10), scalar1=-MAGIC, scalar2=-2 * PI, op0=ALU.add, op1=ALU.mult))
                        o(dv, lambda e: e.tensor_tensor(out=c_(11), in0=c_(11), in1=c_(10), op=ALU.add))
                        o(dv, lambda e: e.tensor_scalar(out=c_(11), in0=c_(11), scalar1=PI, scalar2=-PI, op0=ALU.min, op1=ALU.max))
                        o(ac, lambda e, dst=dst: e.activation(out=c_(dst), in_=c_(11), func=AF.Sin))
                        o(dv, lambda e, dst=dst: e.tensor_tensor(out=c_(dst), in0=c_(dst), in1=c_(9), op=ALU.mult))
                    o(dv, lambda e: e.tensor_tensor(out=c_(10), in0=c_(9), in1=c_(9), op=ALU.mult))
                    o(dv, lambda e: e.reciprocal(out=c_(10), in_=c_(10)))
                    o(dv, lambda e: e.tensor_tensor(out=c_(7), in0=c_(3), in1=c_(10), op=ALU.mult))
                    o(dv, lambda e: e.scalar_tensor_tensor(out=c_(8), in0=c_(4), scalar=-1.0, in1=c_(10), op0=ALU.mult, op1=ALU.mult))
                    o(dv, lambda e: e.tensor_tensor(out=c_(10), in0=lamc[:, d, 0, :], in1=lamc[:, d, 0, :], op=ALU.mult))
                    o(dv, lambda e: e.tensor_tensor(out=c_(11), in0=lamc[:, d, 1, :], in1=lamc[:, d, 1, :], op=ALU.mult))
                    o(dv, lambda e: e.tensor_tensor(out=c_(10), in0=c_(10), in1=c_(11), op=ALU.add))
                    o(dv, lambda e: e.reciprocal(out=c_(10), in_=c_(10)))
                    o(dv, lambda e: e.tensor_scalar(out=c_(9), in0=c_(3), scalar1=-1.0, scalar2=None, op0=ALU.add))
                    o(dv, lambda e: e.tensor_tensor(out=c_(5), in0=c_(9), in1=lamc[:, d, 0, :], op=ALU.mult))
                    o(dv, lambda e: e.tensor_tensor(out=c_(11), in0=c_(4), in1=lamc[:, d, 1, :], op=ALU.mult))
                    o(dv, lambda e: e.tensor_tensor(out=c_(5), in0=c_(5), in1=c_(11), op=ALU.add))
                    o(dv, lambda e: e.tensor_tensor(out=c_(5), in0=c_(5), in1=c_(10), op=ALU.mult))
                    o(dv, lambda e: e.tensor_tensor(out=c_(6), in0=c_(4), in1=lamc[:, d, 0, :], op=ALU.mult))
                    o(dv, lambda e: e.tensor_tensor(out=c_(11), in0=c_(9), in1=lamc[:, d, 1, :], op=ALU.mult))
                    o(dv, lambda e: e.tensor_tensor(out=c_(6), in0=c_(6), in1=c_(11), op=ALU.subtract))
                    o(dv, lambda e: e.tensor_tensor(out=c_(6), in0=c_(6), in1=c_(10), op=ALU.mult))
                    fw.dma(fw.sp, xb[:], p.XB[i, d].rearrange("r q c f -> c r q f"), reads=[bs], writes=[bs], sbuf_side=bs)
                    for q in range(16):
                        fc, ql = q // 4, q % 4
                        g_re, g_im = cols[:, d, 5, q:q + 1], cols[:, d, 6, q:q + 1]
                        o(dv, lambda e: e.tensor_scalar(out=xw[:, 0, :], in0=xb[:, 1, q, :], scalar1=g_im, scalar2=None, op0=ALU.mult))
                        o(dv, lambda e: e.scalar_tensor_tensor(out=xw[:, 0, :], in0=xb[:, 0, q, :], scalar=g_re, in1=xw[:, 0, :], op0=ALU.mult, op1=ALU.subtract))
                        o(dv, lambda e: e.tensor_scalar(out=xw[:, 1, :], in0=xb[:, 0, q, :], scalar1=g_im, scalar2=None, op0=ALU.mult))
                        o(dv, lambda e: e.scalar_tensor_tensor(out=xw[:, 1, :], in0=xb[:, 1, q, :], scalar=g_re, in1=xw[:, 1, :], op0=ALU.mult, op1=ALU.add))
                        bank, bbk = p.banks[q % 2], p.bb[q % 2]
                        fw.group(fw.pe, [(lambda e, ri=ri: e.transpose(out=bank[:, ri * 128:(ri + 1) * 128], in_=xw[:, ri, :], identity=p.ident[:]))
                                         for ri in range(2)], reads=[bs, p.bconst], writes=[bbk])
                        for ri in range(2):
                            fw.op(ac, lambda e, ri=ri: e.activation(out=RB[d][:, fc, ri * 512 + ql * 128: ri * 512 + (ql + 1) * 128],
                                                                    in_=bank[:, ri * 128:(ri + 1) * 128], func=AF.Copy),
                                  reads=[bbk], writes=[btab])
                    fw.dma(fw.sp, xb[:], p.YC[i, d].rearrange("r q c f -> c r q f"), reads=[bs], writes=[bs], sbuf_side=bs)
                    for q in range(16):
                        ai_re, ai_im = cols[:, d, 7, q:q + 1], cols[:, d, 8, q:q + 1]
                        o(dv, lambda e: e.tensor_scalar(out=xw[:, 0, :], in0=xb[:, 1, q, :], scalar1=ai_im, scalar2=None, op0=ALU.mult))
                        fw.op(dv, lambda e: e.scalar_tensor_tensor(out=CT[d][:, q, 0, :], in0=xb[:, 0, q, :], scalar=ai_re, in1=xw[:, 0, :], op0=ALU.mult,
                                                                   op1=ALU.subtract), reads=[bs], writes=[btab])
                        o(dv, lambda e: e.tensor_scalar(out=xw[:, 1, :], in0=xb[:, 0, q, :], scalar1=ai_im, scalar2=-1.0, op0=ALU.mult, op1=ALU.mult))
                        o(dv, lambda e: e.tensor_scalar(out=xw[:, 0, :], in0=xb[:, 1, q, :], scalar1=ai_re, scalar2=None, op0=ALU.mult))
                        fw.op(dv, lambda e: e.tensor_tensor(out=CT[d][:, q, 1, :], in0=xw[:, 1, :], in1=xw[:, 0, :], op=ALU.subtract),
                              reads=[bs], writes=[btab])
                    if d == 0:
                        gen_table(d, 1, 1)
                    else:
                        gen_table(d, 128, -1)
                    fw.op(dv, lambda e: e.tensor_copy(out=QC[d][:], in_=tc_[:]), reads=[bs], writes=[btab])
                    fw.op(dv, lambda e: e.tensor_copy(out=QS[d][:], in_=ts_[:]), reads=[bs], writes=[btab])
                    if d == 0:
                        gen_table(d, 0, -1)
                    else:
                        gen_table(d, -127, 1)
                    nbk = 0
                    for (src, dst) in ((tc_, PCt[d]), (ts_, PSt[d])):
                        for qg in range(4):
                            bank, bbk = p.banks[nbk % 4], p.bb[nbk % 4]
                            nbk += 1
                            fw.group(fw.pe, [(lambda e, k=k: e.transpose(out=bank[:, k * 128:(k + 1) * 128], in_=src[:, qg * 4 + k, :], identity=p.ident[:]))
                                             for k in range(4)], reads=[bs, p.bconst], writes=[bbk])
                            eng = ac if qg % 2 == 0 else dv
                            if eng is ac:
                                fw.op(eng, lambda e: e.activation(out=dst[:, qg * 512:(qg + 1) * 512], in_=bank[:, :], func=AF.Copy), reads=[bbk], writes=[btab])
                            else:
                                fw.op(eng, lambda e: e.tensor_copy(out=dst[:, qg * 512:(qg + 1) * 512], in_=bank[:, :]), reads=[bbk], writes=[btab])
            if "s5dbg" in p.debug:
                for nm, tl in (("dPC", PCt[0]), ("dPS", PSt[0]), ("dQC", QC[0]), ("dQS", QS[0]), ("dRB", RB[0]), ("dCT", CT[0]),
                               ("dPC1", PCt[1]), ("dQC1", QC[1])):
                    dd = p.nc.dram_tensor(nm, list(tl.shape), tl.dtype, kind="ExternalOutput").ap()
                    fw.dma(fw.sp, dd, tl[:], reads=[btab], sbuf_side=btab)
            with fw.scope() as scp:
                ub = [fw.sbuf("ub", [128, 4, 128], BF16, scp) for _ in range(2)]
                bub = bufs(2, "ub")
                Z = [fw.sbuf("Z", [128, 2, 512], BF16, scp) for _ in range(8)]
                bZ = bufs(8, "Z")
                P4a = [fw.sbuf("P4", [128, 512], F32, scp) for _ in range(8)]
                bP4a = bufs(8, "P4")
                Hh = [fw.sbuf("Hh", [128, 2, 128], BF16, scp) for _ in range(32)]
                bH = bufs(32, "Hh")
                Pp = [fw.sbuf("Pp", [128, 4, 128], F32, scp) for _ in range(4)]
                bPp = bufs(4, "Pp")
                yst = [fw.sbuf("yst", [128, 4, 128], F32, scp) for _ in range(2)]
                byst = bufs(2, "yst")
                bcars = [[Buf("carry") for _ in range(16)] for _ in range(2)]
                uview = p.uTb.rearrange("(c q) t -> q c t", q=128)
                for d in range(2):
                    order = list(range(NB)) if d == 0 else [1, 0] + list(range(NB - 1, 1, -1))
                    U = Uf if d == 0 else Ur
                    L = 127 if d == 0 else 0
                    ydst = (p.yF if d == 0 else p.yR).rearrange("(c q) t -> q c t", q=128)
                    fw.op(dv, lambda e: e.memset(carry[:, d], 0.0), writes=bcars[d])

                    def stage_a(n):
                        blk = order[n]
                        u_, bu_ = ub[n % 2], bub[n % 2]
                        fw.dma(fw.sp, u_[:], uview[:, :, blk * 128:(blk + 1) * 128], writes=[bu_], sbuf_side=bu_)
                        for fc in range(4):
                            bre, bbre = p.banks[0], p.bb[0]
                            bim, bbim = p.banks[1], p.bb[1]
                            fw.op(fw.pe, lambda e: e.matmul(bre[:, :], lhsT=u_[:, fc, :], rhs=RB[d][:, fc, 0:512], start=True, stop=True),
                                  reads=[bu_, btab], writes=[bbre])
                            fw.op(fw.pe, lambda e: e.matmul(bim[:, :], lhsT=u_[:, fc, :], rhs=RB[d][:, fc, 512:1024], start=True, stop=True),
                                  reads=[bu_, btab], writes=[bbim])
                            pc_ = PCt[d][:, fc * 512:(fc + 1) * 512]
                            ps_ = PSt[d][:, fc * 512:(fc + 1) * 512]
                            z, bz = Z[4 * (n % 2) + fc], bZ[4 * (n % 2) + fc]
                            P4, bP4 = P4a[4 * (fc % 2):4 * (fc % 2) + 4], bP4a[4 * (fc % 2):4 * (fc % 2) + 4]
                            fw.op(dv, lambda e: e.tensor_tensor(out=P4[0][:], in0=bre[:, :], in1=pc_, op=ALU.mult), reads=[bbre, btab], writes=[bP4[0]])
                            fw.op(dv, lambda e: e.tensor_tensor(out=P4[1][:], in0=bim[:, :], in1=ps_, op=ALU.mult), reads=[bbim, btab], writes=[bP4[1]])
                            fw.op(pl, lambda e: e.tensor_tensor(out=z[:, 0, :], in0=P4[0][:], in1=P4[1][:], op=ALU.subtract), reads=[bP4[0], bP4[1]], writes=[bz])
                            fw.op(dv, lambda e: e.tensor_tensor(out=P4[2][:], in0=bim[:, :], in1=pc_, op=ALU.mult), reads=[bbim, btab], writes=[bP4[2]])
                            fw.op(dv, lambda e: e.tensor_tensor(out=P4[3][:], in0=bre[:, :], in1=ps_, op=ALU.mult), reads=[bbre, btab], writes=[bP4[3]])
                            fw.op(pl, lambda e: e.tensor_tensor(out=z[:, 1, :], in0=P4[2][:], in1=P4[3][:], op=ALU.add), reads=[bP4[2], bP4[3]], writes=[bz])

                    def stage_b(n):
                        for q in range(16):
                            fc, ql = q // 4, q % 4
                            z, bz = Z[4 * (n % 2) + fc], bZ[4 * (n % 2) + fc]
                            gb, bgb = p.banks[2 + q % 4], p.bb[2 + q % 4]
                            fw.group(fw.pe, [(lambda e, ri=ri: e.matmul(gb[:, ri * 128:(ri + 1) * 128], lhsT=z[:, ri, ql * 128:(ql + 1) * 128], rhs=U[:],
                                                                         start=True, stop=True)) for ri in range(2)],
                                     reads=[bz, bs], writes=[bgb])
                            pp, bpp = Pp[q % 4], bPp[q % 4]
                            cre, cim = carry[:, d, q, 0:1], carry[:, d, q, 1:2]
                            gre, gim = gb[:, 0:128], gb[:, 128:256]
                            qc_, qs_ = QC[d][:, q, :], QS[d][:, q, :]
                            bcar = bcars[d][q]
                            rd = [bgb, btab, bcar]
                            fw.op(dv, lambda e: e.scalar_tensor_tensor(out=pp[:, 0, :], in0=gre, scalar=cre, in1=qc_, op0=ALU.add, op1=ALU.mult), reads=rd, writes=[bpp])
                            fw.op(dv, lambda e: e.scalar_tensor_tensor(out=pp[:, 1, :], in0=gim, scalar=cim, in1=qs_, op0=ALU.add, op1=ALU.mult), reads=rd, writes=[bpp])
                            fw.op(dv, lambda e: e.scalar_tensor_tensor(out=pp[:, 2, :], in0=gim, scalar=cim, in1=qc_, op0=ALU.add, op1=ALU.mult), reads=rd, writes=[bpp])
                            fw.op(dv, lambda e: e.scalar_tensor_tensor(out=pp[:, 3, :], in0=gre, scalar=cre, in1=qs_, op0=ALU.add, op1=ALU.mult), reads=rd, writes=[bpp])
                            h, bh = Hh[16 * (n % 2) + q], bH[16 * (n % 2) + q]
                            fw.op(pl, lambda e: e.tensor_tensor(out=h[:, 0, :], in0=pp[:, 0, :], in1=pp[:, 1, :], op=ALU.subtract), reads=[bpp], writes=[bh])
                            fw.op(pl, lambda e: e.tensor_tensor(out=h[:, 1, :], in0=pp[:, 2, :], in1=pp[:, 3, :], op=ALU.add), reads=[bpp], writes=[bh])
                            fw.op(dv, lambda e: e.tensor_tensor(out=cre, in0=pp[:, 0, L:L + 1], in1=pp[:, 1, L:L + 1], op=ALU.subtract), reads=[bpp, bcar], writes=[bcar])
                            fw.op(dv, lambda e: e.tensor_tensor(out=cim, in0=pp[:, 2, L:L + 1], in1=pp[:, 3, L:L + 1], op=ALU.add), reads=[bpp, bcar], writes=[bcar])

                    def stage_c(n):
                        blk = order[n]
                        ybank, bybank = p.banks[6 + n % 2], p.bb[6 + n % 2]
                        ys, bys = yst[n % 2], byst[n % 2]
                        for fc in range(4):
                            fw.group(fw.pe, [(lambda e, ql=ql, ri=ri: e.matmul(ybank[:, fc * 128:(fc + 1) * 128], lhsT=CT[d][:, 4 * fc + ql, ri, :],
                                                                               rhs=Hh[16 * (n % 2) + 4 * fc + ql][:, ri, :], start=(ql == 0 and ri == 0),
                                                                               stop=(ql == 3 and ri == 1))) for ql in range(4) for ri in range(2)],
                                     reads=[btab] + [bH[16 * (n % 2) + 4 * fc + ql] for ql in range(4)], writes=[bybank])
                        fw.op(ac, lambda e: e.activation(out=ys[:], in_=ybank[:, :].rearrange("p (c t) -> p c t", c=4), func=AF.Copy), reads=[bybank], writes=[bys])
                        fw.dma(fw.act, ydst[:, :, blk * 128:(blk + 1) * 128], ys[:], reads=[bys], sbuf_side=bys)

                    stage_a(0)
                    for n in range(len(order)):
                        if n + 1 < len(order):
                            stage_a(n + 1)
                        stage_b(n)
                        stage_c(n)
            fw.barrier()
            gw = fw.sbuf("gw", [128, 4, 512], BF16, sc)
            bgw = Buf("gw")
            fw.dma(fw.sp, gw[:], p.gluS[i], reads=[p.wbuf["glu%d" % i]], writes=[bgw], sbuf_side=bgw)
            A_ = [fw.sbuf("tA", [128, 4, 512], F32, sc) for _ in range(3)]
            bA = bufs(3, "tA")
            y3 = fw.sbuf("y3", [128, 4, 512], F32, sc)
            y3b = fw.sbuf("y3b", [128, 4, 512], BF16, sc)
            so = fw.sbuf("so", [128, 4, 512], BF16, sc)
            by3, by3b, bso = Buf("y3"), Buf("y3b"), Buf("so")
            for (t0, N, is_ctx) in TILES:
                vF = p.yF.rearrange("(c q) t -> q c t", q=128)[:, :, t0:t0 + N]
                vR = p.yR.rearrange("(c q) t -> q c t", q=128)[:, :, t0:t0 + N]
                vU = p.uT.rearrange("(c q) t -> q c t", q=128)[:, :, t0:t0 + N]
                fw.dma(fw.sp, A_[0][:, :, :N], vF, writes=[bA[0]], sbuf_side=bA[0])
                fw.dma(fw.sp, A_[1][:, :, :N], vR, writes=[bA[1]], sbuf_side=bA[1])
                fw.dma(fw.sp, A_[2][:, :, :N], vU, writes=[bA[2]], sbuf_side=bA[2])
                fw.op(pl, lambda e: e.tensor_tensor(out=A_[0][:, :, :N], in0=A_[0][:, :, :N], in1=A_[1][:, :, :N], op=ALU.add), reads=[bA[0], bA[1]], writes=[bA[0]])
                for fc in range(4):
                    fw.op(dv, lambda e: e.scalar_tensor_tensor(out=A_[0][:, fc, :N], in0=A_[2][:, fc, :N], scalar=p.s5d[:, i, 0, fc:fc + 1],
                                                               in1=A_[0][:, fc, :N], op0=ALU.mult, op1=ALU.add), reads=[bA[0], bA[2], p.bsmall], writes=[bA[0]])
                y2 = A_[0]
                fw.op(ac, lambda e: e.activation(out=A_[1][:, :, :N], in_=y2[:, :, :N], func=AF.Square), reads=[bA[0]], writes=[bA[1]])
                fw.op(dv, lambda e: e.tensor_scalar(out=A_[1][:, :, :N], in0=A_[1][:, :, :N], scalar1=0.044715, scalar2=1.0, op0=ALU.mult, op1=ALU.add),
                      reads=[bA[1]], writes=[bA[1]])
                fw.op(dv, lambda e: e.tensor_tensor(out=A_[1][:, :, :N], in0=A_[1][:, :, :N], in1=y2[:, :, :N], op=ALU.mult), reads=[bA[0], bA[1]], writes=[bA[1]])
                fw.op(ac, lambda e: e.activation(out=A_[1][:, :, :N], in_=A_[1][:, :, :N], func=AF.Sigmoid, scale=2.0 * math.sqrt(2.0 / PI)),
                      reads=[bA[1]], writes=[bA[1]])
                fw.op(dv, lambda e: e.tensor_tensor(out=y3[:, :, :N], in0=A_[1][:, :, :N], in1=y2[:, :, :N], op=ALU.mult), reads=[bA[0], bA[1]], writes=[by3])
                fw.op(pl, lambda e: e.tensor_copy(out=y3b[:, :, :N], in_=y3[:, :, :N]), reads=[by3], writes=[by3b])
                for fo in range(4):
                    bank, bbk = p.banks[fo % 2], p.bb[fo % 2]
                    fw.group(fw.pe, [(lambda e, fi=fi: e.matmul(bank[:, :N], lhsT=gw[:, fi, fo * 128:(fo + 1) * 128], rhs=y3b[:, fi, :N],
                                                                start=(fi == 0), stop=(fi == 3))) for fi in range(4)],
                             reads=[bgw, by3b], writes=[bbk])
                    fw.op(ac, lambda e: e.activation(out=A_[2][:, fo, :N], in_=bank[:, :N], func=AF.Sigmoid, bias=p.s5d[:, i, 1, fo:fo + 1]),
                          reads=[bbk, p.bsmall], writes=[bA[2]])
                fw.op(dv, lambda e: e.tensor_tensor(out=so[:, :, :N], in0=A_[2][:, :, :N], in1=y3[:, :, :N], op=ALU.mult), reads=[bA[2], by3], writes=[bso])
                fw.dma(fw.act, p.s5T.rearrange("(c q) t -> q c t", q=128)[:, :, t0:t0 + N], so[:, :, :N], reads=[bso], sbuf_side=bso)

    def lam_prep(self, i):
        p, fw = self, self.fw
        lam_init = 0.8 - 0.6 * math.exp(-0.3 * i)
        with fw.scope() as sc:
            dl = fw.sbuf("dl", [1, 256], F32, sc)
            pr = fw.sbuf("pr", [1, 128], F32, sc)
            sm = fw.sbuf("sm", [1, 4], F32, sc)
            one1 = fw.sbuf("one1", [1, 128], F32, sc)
            b = Buf("lam")
            fw.dma(fw.sp, dl[:], p.dlam[i], writes=[b], sbuf_side=b)
            fw.op(fw.dve, lambda e: e.memset(one1[:], 1.0), reads=[b], writes=[b])
            dl4 = dl[:, :].rearrange("p (k c) -> p k c", k=4)
            fw.op(fw.dve, lambda e: e.tensor_tensor(out=pr[:, 0:64], in0=dl4[:, 0, :], in1=dl4[:, 1, :], op=ALU.mult), reads=[b], writes=[b])
            fw.op(fw.dve, lambda e: e.tensor_tensor(out=pr[:, 64:128], in0=dl4[:, 2, :], in1=dl4[:, 3, :], op=ALU.mult), reads=[b], writes=[b])
            fw.op(fw.dve, lambda e: e.reduce_sum(out=sm[:, 0:2], in_=pr[:, :].rearrange("p (k c) -> p k c", k=2), axis=mybir.AxisListType.X),
                  reads=[b], writes=[b])
            fw.op(fw.act, lambda e: e.activation(out=sm[:, 0:2], in_=sm[:, 0:2], func=AF.Exp), reads=[b], writes=[b])
            fw.op(fw.dve, lambda e: e.tensor_tensor(out=sm[:, 2:3], in0=sm[:, 1:2], in1=sm[:, 0:1], op=ALU.subtract), reads=[b], writes=[b])
            fw.op(fw.dve, lambda e: e.tensor_scalar(out=sm[:, 2:3], in0=sm[:, 2:3], scalar1=-lam_init, scalar2=None, op0=ALU.add), reads=[b], writes=[b])
            bank, bbk = p.banks[0], p.bb[0]
            fw.op(fw.pe, lambda e: e.matmul(bank[:, 0:1], lhsT=one1[:], rhs=sm[:, 2:3], start=True, stop=True), reads=[b], writes=[bbk])
            fw.op(fw.dve, lambda e: e.tensor_copy(out=p.lamc[:, 0:1], in_=bank[:, 0:1]), reads=[bbk], writes=[p.blamc])
            fw.op(fw.dve, lambda e: e.tensor_scalar(out=p.lamc[:, 1:2], in0=p.hg[:, i, 0:1], scalar1=1.0 - lam_init, scalar2=None, op0=ALU.mult),
                  reads=[p.bsmall, p.blamc], writes=[p.blamc])

    def pass_b(self, i, last):
        p, fw = self, self.fw
        p.lam_prep(i)
        with fw.scope() as sc:
            p.pass_alloc(sc)
            ws = p.ws
            mix = fw.sbuf("mix", [128, DC, 512], BF16, sc)
            bmix = bufs(DC, "mix")
            tiles = [t for t in TILES if not (last and t[2])]
            wbo = p.wbuf["wout%d" % i]
            for _ in tiles:
                for dg in range(4):
                    ws.enqueue(lambda t, b, dg=dg: fw.dma(fw.sp, t[:, :].rearrange("p (c kc n) -> p c kc n", c=4, kc=DC), p.woutS[i][dg],
                                                           reads=[wbo], writes=[b], sbuf_side=b))
                p.enqueue_ffn(i, 1)
            for (t0, N, is_ctx) in tiles:
                c = 1 if is_ctx else 0
                fw.dma(fw.sp, p.xt[:, :, :N], p.xT.rearrange("(kc q) t -> q kc t", q=128)[:, :, t0:t0 + N], writes=p.bxt, sbuf_side=p.bxt[0])
                fw.dma(fw.sp, mix[:, 0:4, :N], p.s5T.rearrange("(c q) t -> q c t", q=128)[:, :, t0:t0 + N], writes=bmix[0:4], sbuf_side=bmix[0])
                with fw.scope() as sc2:
                    p.attention(i, t0, N, is_ctx, mix, bmix, sc2)
                for dg in range(4):
                    t, b = ws.take()
                    tv = t[:, :].rearrange("p (c kc n) -> p c kc n", c=4, kc=DC)
                    for ci in range(4):
                        dc = dg * 4 + ci
                        yb, byb = p.banks[4 + dc % 2], p.bb[4 + dc % 2]
                        fw.group(fw.pe, [(lambda e, kc=kc: e.matmul(yb[:, :N], lhsT=tv[:, ci, kc, :], rhs=mix[:, kc, :N], start=(kc == 0), stop=(kc == DC - 1)))
                                         for kc in range(DC)], reads=[b] + bmix, writes=[byb])
                        fw.op(fw.dve, lambda e: e.scalar_tensor_tensor(out=p.xt[:, dc, :N], in0=yb[:, :N], scalar=p.Gcol[:, 1, dc, c:c + 1],
                                                                      in1=p.xt[:, dc, :N], op0=ALU.mult, op1=ALU.add),
                              reads=[byb, p.bcols, p.bxt[dc]], writes=[p.bxt[dc]])
                with fw.scope() as sc2:
                    act = fw.sbuf("act", [128, FC, 512], BF16, sc2)
                    bact = bufs(FC, "act")
                    p.rmsnorm_mod(i, 2, c, N)
                    p.ffn(i, 1, 2, c, N, act, bact)
                if not last:
                    fw.dma(fw.act, p.xT.rearrange("(kc q) t -> q kc t", q=128)[:, :, t0:t0 + N], p.xt[:, :, :N], reads=p.bxt, sbuf_side=p.bxt[0])
                else:
                    with fw.scope() as sc2:
                        p.final_out(t0, N, sc2)

    def final_out(self, t0, N, sc):
        p, fw = self, self.fw
        ssb, bss = p.banks[6], p.bb[6]
        for kc in range(DC):
            s_, bs = p.sq[kc % 2], p.bsq[kc % 2]
            fw.op(fw.act, lambda e: e.activation(out=s_[:, :N], in_=p.xt[:, kc, :N], func=AF.Square), reads=[p.bxt[kc]], writes=[bs])
            fw.op(fw.pe, lambda e: e.matmul(ssb[:, :N], lhsT=p.ones[:], rhs=s_[:, :N], start=(kc == 0), stop=(kc == DC - 1)),
                  reads=[bs, p.bconst], writes=[bss])
        fw.op(fw.act, lambda e: e.activation(out=p.rs[:, :N], in_=ssb[:, :N], func=AF.Sqrt, scale=1.0 / D, bias=p.epsc[:]),
              reads=[bss, p.bconst], writes=[p.brs])
        fw.op(fw.dve, lambda e: e.reciprocal(out=p.rs[:, :N], in_=p.rs[:, :N]), reads=[p.brs], writes=[p.brs])
        for kc in range(DC):
            fw.op(fw.dve, lambda e: e.scalar_tensor_tensor(out=p.xt[:, kc, :N], in0=p.xt[:, kc, :N], scalar=p.fgT[:, kc:kc + 1], in1=p.rs[:, :N],
                                                          op0=ALU.mult, op1=ALU.mult), reads=[p.bxt[kc], p.brs, p.bsmall], writes=[p.bxt[kc]])
        otok = [fw.sbuf("otok", [128, D], F32, sc) for _ in range(2)]
        bot = bufs(2, "otok")
        nb = 0
        for blk in range(N // 128):
            ot, bo = otok[blk % 2], bot[blk % 2]
            for dg in range(4):
                bank, bbk = p.banks[nb % 4], p.bb[nb % 4]
                nb += 1
                fw.group(fw.pe, [(lambda e, q=q: e.transpose(out=bank[:, q * 128:(q + 1) * 128], in_=p.xt[:, dg * 4 + q, blk * 128:(blk + 1) * 128],
                                                             identity=p.ident[:])) for q in range(4)],
                         reads=p.bxt[dg * 4:(dg + 1) * 4] + [p.bconst], writes=[bbk])
                if dg % 2 == 0:
                    fw.op(fw.act, lambda e: e.activation(out=ot[:, dg * 512:(dg + 1) * 512], in_=bank[:, :], func=AF.Copy), reads=[bbk], writes=[bo])
                else:
                    fw.op(fw.dve, lambda e: e.tensor_copy(out=ot[:, dg * 512:(dg + 1) * 512], in_=bank[:, :]), reads=[bbk], writes=[bo])
            r0 = t0 - CTX + blk * 128
            fw.dma(fw.act, p.out[r0:r0 + 128, :], ot[:], reads=[bo], sbuf_side=bo)

    def attention(self, i, t0, N, is_ctx, mix, bmix, sc):
        p, fw = self, self.fw
        nkb = 2 if is_ctx else NB
        nk = nkb * 128
        qd = fw.sbuf("qd", [128, 4, 512], BF16, sc)
        qg = fw.sbuf("qg", [128, 8, 512], BF16, sc)
        bq = Buf("q")
        fw.dma(fw.sp, qd[:, :, :N], p.qd.rearrange("(c q) t -> q c t", q=128)[:, :, t0:t0 + N], writes=[bq], sbuf_side=bq)
        bq2 = Buf("q2")
        fw.dma(fw.sp, qg[:, :, :N], p.qg.rearrange("(c q) t -> q c t", q=128)[:, :, t0:t0 + N], writes=[bq2], sbuf_side=bq2)
        KT = [fw.sbuf("KT", [128, S], BF16, sc) for _ in range(2)]
        VV = [fw.sbuf("VV", [128, NB, 128], BF16, sc) for _ in range(2)]
        bkv = bufs(2, "kv")
        Pt = [fw.sbuf("Pt", [128, 2, 512], BF16, sc) for _ in range(3)]
        bPt = bufs(3, "Pt")
        Ps2 = [fw.sbuf("Ps2", [128, 512], BF16, sc) for _ in range(2)]
        bPs2 = bufs(2, "Ps2")
        rc = [fw.sbuf("rc", [128, 512], F32, sc) for _ in range(2)]
        brc = bufs(2, "rc")
        units = [("d", h) for h in range(4)] + [("g", kv) for kv in range(2)]

        def load_unit(u):
            kind, idx = units[u]
            k, b = u % 2, bkv[u % 2]
            ksrc = p.kd if kind == "d" else p.kg
            vsrc = p.vd if kind == "d" else p.vg
            fw.dma(fw.sp, KT[k][:, 0:nk], ksrc[idx * 128:(idx + 1) * 128, 0:nk], writes=[b], sbuf_side=b)
            fw.dma(fw.sp, VV[k][:, 0:nkb, :], vsrc[idx, :, 0:nkb, :], writes=[b], sbuf_side=b)
        npt = [0]

        def softmax_av(kt, vv, bkvu, qap, pbase, K, scale, obank, bob, dbank, bdb, readsq):
            npair = nkb // 2

            def s_pair(pi):
                sp, bsp0, bsp1 = p.bpair[pi % 2], p.bb[2 * (pi % 2)], p.bb[2 * (pi % 2) + 1]
                fw.group(fw.pe, [(lambda e, j=j: e.matmul(sp[:, j, :N], lhsT=kt[pbase:pbase + K, (2 * pi + j) * 128:(2 * pi + j + 1) * 128], rhs=qap,
                                                          start=True, stop=True)) for j in range(2)],
                         reads=[bkvu, readsq], writes=[bsp0, bsp1])
            s_pair(0)
            for pi in range(npair):
                if pi + 1 < npair:
                    s_pair(pi + 1)
                sp, bsp0, bsp1 = p.bpair[pi % 2], p.bb[2 * (pi % 2)], p.bb[2 * (pi % 2) + 1]
                pt, bpt = Pt[npt[0] % 3], bPt[npt[0] % 3]
                npt[0] += 1
                fw.op(fw.act, lambda e: e.activation(out=pt[:, :, :N], in_=sp[:, :, :N], func=AF.Exp, scale=scale), reads=[bsp0, bsp1], writes=[bpt])
                fw.group(fw.pe, [(lambda e, j=j: e.matmul(obank[:, :N], lhsT=vv[:, 2 * pi + j, :], rhs=pt[:, j, :N], start=(pi == 0 and j == 0),
                                                          stop=(pi == npair - 1 and j == 1))) for j in range(2)],
                         reads=[bkvu, bpt], writes=[bob])
                ps2, bps2 = Ps2[pi % 2], bPs2[pi % 2]
                fw.op(fw.dve, lambda e: e.tensor_tensor(out=ps2[:, :N], in0=pt[:, 0, :N], in1=pt[:, 1, :N], op=ALU.add), reads=[bpt], writes=[bps2])
                fw.op(fw.pe, lambda e: e.matmul(dbank[:, :N], lhsT=p.ones[:], rhs=ps2[:, :N], start=(pi == 0), stop=(pi == npair - 1)),
                      reads=[bps2, p.bconst], writes=[bdb])
        load_unit(0)
        for u, (kind, idx) in enumerate(units):
            if u + 1 < len(units):
                load_unit(u + 1)
            kt, vv, bkvu = KT[u % 2], VV[u % 2], bkv[u % 2]
            if kind == "d":
                h = idx
                for m in range(2):
                    softmax_av(kt, vv, bkvu, qd[64 * m:64 * m + 64, h, :N], 64 * m, 64, 0.125,
                               p.banks[4 + m], p.bb[4 + m], p.banks[6 + m], p.bb[6 + m], bq)
                t1, b1 = p.tf[0], p.btf[0]
                t2, b2 = p.tf[1], p.btf[1]
                for m, (tt, bt) in enumerate(((t1, b1), (t2, b2))):
                    fw.op(fw.dve, lambda e: e.reciprocal(out=rc[m][:, :N], in_=p.banks[6 + m][:, :N]), reads=[p.bb[6 + m]], writes=[brc[m]])
                    fw.op(fw.dve, lambda e: e.tensor_tensor(out=tt[:, :N], in0=p.banks[4 + m][:, :N], in1=rc[m][:, :N], op=ALU.mult),
                          reads=[p.bb[4 + m], brc[m]], writes=[bt])
                fw.op(fw.dve, lambda e: e.scalar_tensor_tensor(out=t1[:, :N], in0=t2[:, :N], scalar=p.lamc[:, 0:1], in1=t1[:, :N], op0=ALU.mult, op1=ALU.add),
                      reads=[b1, b2, p.blamc], writes=[b1])
                s_, bs = p.sq[h % 2], p.bsq[h % 2]
                ssb, bss = p.banks[0], p.bb[0]
                fw.op(fw.act, lambda e: e.activation(out=s_[:, :N], in_=t1[:, :N], func=AF.Square), reads=[b1], writes=[bs])
                fw.op(fw.pe, lambda e: e.matmul(ssb[:, :N], lhsT=p.ones[:], rhs=s_[:, :N], start=True, stop=True), reads=[bs, p.bconst], writes=[bss])
                fw.op(fw.act, lambda e: e.activation(out=p.rs[:, :N], in_=ssb[:, :N], func=AF.Sqrt, scale=1.0 / 128, bias=p.epsc[:]),
                      reads=[bss, p.bconst], writes=[p.brs])
                fw.op(fw.dve, lambda e: e.reciprocal(out=p.rs[:, :N], in_=p.rs[:, :N]), reads=[p.brs], writes=[p.brs])
                fw.op(fw.dve, lambda e: e.scalar_tensor_tensor(out=mix[:, 4 + h, :N], in0=t1[:, :N], scalar=p.lamc[:, 1:2], in1=p.rs[:, :N],
                                                              op0=ALU.mult, op1=ALU.mult), reads=[b1, p.brs, p.blamc], writes=[bmix[4 + h]])
            else:
                kv = idx
                for r in range(4):
                    h = kv * 4 + r
                    ob, bob = p.banks[4 + r % 2], p.bb[4 + r % 2]
                    db, bdb = p.banks[6 + r % 2], p.bb[6 + r % 2]
                    softmax_av(kt, vv, bkvu, qg[:, h, :N], 0, 128, 128.0 ** -0.5, ob, bob, db, bdb, bq2)
                    fw.op(fw.dve, lambda e: e.reciprocal(out=rc[r % 2][:, :N], in_=db[:, :N]), reads=[bdb], writes=[brc[r % 2]])
                    fw.op(fw.dve, lambda e: e.tensor_tensor(out=mix[:, 8 + h, :N], in0=ob[:, :N], in1=rc[r % 2][:, :N], op=ALU.mult),
                          reads=[bob, brc[r % 2]], writes=[bmix[8 + h]])

    def finish(self):
        self.fw.barrier()


def build(debug=None, upto="all", conv="all", ntiles=9, stop=None):
    p = Prog(debug)
    p.ntiles = ntiles
    p.stop = stop
    p.declare()
    fw = p.fw
    with fw.stack:
        p.setup()
        p.conv_filter = None if conv == "all" else conv.split(",")
        p.rope_tables()
        p.convert_layer(0)
        p.convert_layer(1)
        for i in range(2):
            if upto == "rope":
                break
            p.modulation(i)
            if upto == "mod":
                break
            p.pass_a(i)
            if upto == "A":
                break
            p.s5(i)
            if upto == "S5":
                break
            p.pass_b(i, i == 1)
            if upto == "B":
                break
        p.finish()
    return p


def host_inputs(inp):
    f = np.float32
    g = {k: np.asarray(v) for k, v in inp.items()}
    common = {}
    common["ada_w"] = np.ascontiguousarray(g["ada_w"], f)
    common["ada_bT"] = np.ascontiguousarray(g["ada_b"].reshape(2, 144, 128).transpose(0, 2, 1), f)
    common["norm_gT"] = np.ascontiguousarray(g["norm_g"].reshape(2, 3, DC, 128).transpose(0, 3, 1, 2), f)
    common["final_gT"] = np.ascontiguousarray(g["final_g"].reshape(DC, 128).T, f)
    common["ffn_w_gate"] = np.ascontiguousarray(g["ffn_w_gate"], f)
    common["ffn_w_up"] = np.ascontiguousarray(g["ffn_w_up"], f)
    common["ffn_w_down"] = np.ascontiguousarray(g["ffn_w_down"], f)
    w_in = np.ascontiguousarray(g["w_in"], f)
    common["w_in"] = w_in
    i64 = np.arange(512) ^ 16
    i128 = np.arange(1024) ^ 32
    k128 = np.arange(256) ^ 32
    common["w_in_rot"] = np.ascontiguousarray(np.concatenate([w_in[:, :, 512 + i64], w_in[:, :, 1024 + i64], w_in[:, :, 2048 + i128],
                                                              w_in[:, :, 3072 + k128]], axis=2), f)
    common["w_out"] = np.ascontiguousarray(g["w_out"], f)
    lam = np.stack([g["s5_lam_re"], g["s5_lam_im"], np.broadcast_to(g["s5_log_dt"][..., None], g["s5_lam_re"].shape)], axis=0)
    lamT = lam.reshape(3, 2, 2, 16, 128).transpose(1, 2, 4, 0, 3)
    common["lamT"] = np.ascontiguousarray(lamT, f)
    XB = np.zeros((2, 2, 2, 16, 128, 128), f)
    YC = np.zeros((2, 2, 2, 16, 128, 128), f)
    for ri, (bsrc, csrc) in enumerate([(g["s5_b_re"], g["s5_c_re"]), (g["s5_b_im"], g["s5_c_im"])]):
        for q in range(16):
            for e in range(2):
                gg = 2 * q + e
                gl = gg % 8
                XB[:, :, ri, q, e * 64:(e + 1) * 64, gl * 16:(gl + 1) * 16] = bsrc[:, :, gg]
                YC[:, :, ri, q, e * 64:(e + 1) * 64, gl * 16:(gl + 1) * 16] = csrc[:, :, gg].transpose(0, 1, 3, 2)
    common["XB"] = XB
    common["YC"] = YC
    common["s5dT"] = np.ascontiguousarray(np.stack([g["s5_d"].reshape(2, 4, 128), g["s5_glu_b"].reshape(2, 4, 128)], axis=1).transpose(0, 3, 1, 2), f)
    common["s5_glu_w"] = np.ascontiguousarray(g["s5_glu_w"], f)
    common["dlam"] = np.ascontiguousarray(g["diff_lam"].reshape(2, 1, 256), f)
    p32 = np.arange(128) ^ 32
    common["hgT"] = np.ascontiguousarray(np.stack([g["diff_subln_g"], g["gqa_q_g"], g["gqa_q_g"][:, p32], g["gqa_k_g"], g["gqa_k_g"][:, p32]], axis=2), f)
    maps = []
    for b in range(NCORES):
        m = dict(common)
        m["x"] = np.ascontiguousarray(g["x"][b], f)
        m["ctx"] = np.ascontiguousarray(g["ctx"][b], f)
        m["cT"] = np.ascontiguousarray(np.stack([g["c"][b].reshape(DC, 128).T, g["c_ctx"].reshape(DC, 128).T], axis=2), f)
        maps.append(m)
    idle = dict(common)
    for k in ("ada_w", "ffn_w_gate", "ffn_w_up", "ffn_w_down", "w_in", "w_in_rot", "w_out", "s5_glu_w", "XB", "YC"):
        idle[k] = np.zeros_like(common[k])
    idle["x"] = np.zeros_like(maps[0]["x"])
    idle["ctx"] = np.zeros_like(maps[0]["ctx"])
    idle["cT"] = np.zeros_like(maps[0]["cT"])
    full = []
    for b in range(NCORES):
        full.append(maps[b])
        full.append(idle)
    return full


_CACHE = {}


def kernel(**inputs):
    maps = host_inputs(inputs)
    if "p" not in _CACHE:
        _CACHE["p"] = build()
    p = _CACHE["p"]
    res = run_bass_kernel_spmd(p.nc, maps, core_ids=list(range(2 * NCORES)))
    return np.stack([np.asarray(res.results[2 * b]["out"], np.float32) for b in range(NCORES)], axis=0)
```

```python
import math
import numpy as np
import concourse.bass as bass
import concourse.mybir as mybir
from concourse.bass_utils import run_bass_kernel_spmd
from contextlib import ExitStack

F32 = mybir.dt.float32
BF16 = mybir.dt.bfloat16
I32 = mybir.dt.int32
ALU = mybir.AluOpType
AF = mybir.ActivationFunctionType

D = 2048
DC = 16
FF = 5632
FC = 44
LAT = 4096
CTX = 256
S = 4352
NB = 34
NCORES = 4
TILES = [(0, 256, True)] + [(256 + 512 * i, 512, False) for i in range(8)]
EPS = 1e-6
PI = math.pi


class Eng:
    def __init__(self, fw, name, h, is_pe=False):
        self.fw, self.name, self.h, self.is_pe = fw, name, h, is_pe
        self.nsem = 0
        self.newsem()
        self.seen = {}

    def newsem(self):
        self.sem = self.fw.stack.enter_context(self.fw.nc.semaphore("s_%s%d" % (self.name, self.nsem)))
        self.nsem += 1
        self.seq = 0


class Buf:
    __slots__ = ("name", "w", "r", "dsem", "dval", "excl")

    def __init__(self, name="", excl=False):
        self.name = name
        self.excl = excl
        self.w = None
        self.r = {}
        self.dsem = None
        self.dval = 0


def bufs(n, name=""):
    return [Buf(name + str(i)) for i in range(n)]


class FW:
    def __init__(self, nc):
        self.nc = nc
        self.stack = ExitStack()
        self.pe = Eng(self, "pe", nc.tensor, True)
        self.act = Eng(self, "act", nc.scalar)
        self.dve = Eng(self, "dve", nc.vector)
        self.pool = Eng(self, "pool", nc.gpsimd)
        self.sp = Eng(self, "sp", nc.sync)
        self.engs = [self.pe, self.act, self.dve, self.pool, self.sp]
        self.inflight = []
        self.nd = 0
        self.uid = 0
        self.sem_pool = []
        self.scopes = []

    def push_scope(self):
        self.scopes.append([])

    def pop_scope(self):
        for b in self.scopes.pop():
            if b.dsem is not None:
                self.sem_pool.append([b.dsem, b.dval])
                b.dsem = None

    def scope(self):
        return _Scope(self)

    def name(self, s):
        self.uid += 1
        return "%s_%d" % (s, self.uid)

    def sbuf(self, name, shape, dt, stack=None):
        return (stack or self.stack).enter_context(self.nc.sbuf_tensor(self.name(name), shape, dt))

    def psum(self, name, shape, dt):
        return self.stack.enter_context(self.nc.psum_tensor(self.name(name), shape, dt))

    def _wait(self, eng, dep):
        sem, val, ename = dep
        key = id(sem)
        if eng.seen.get(key, 0) >= val:
            return
        if eng.is_pe and ename == "pe":
            return
        eng.h.wait_ge(sem, val)
        eng.seen[key] = val

    def _deps(self, eng, reads, writes):
        for b in reads:
            if b.w is not None:
                self._wait(eng, b.w)
            if b.excl:
                for d in b.r.values():
                    self._wait(eng, d)
        for b in writes:
            if b.w is not None and b.w[2] != eng.name:
                self._wait(eng, b.w)
            for d in b.r.values():
                if d[2] != eng.name:
                    self._wait(eng, d)

    def _mark(self, d, key, reads, writes):
        for b in writes:
            b.w = d
            b.r = {}
        for b in reads:
            if b not in writes:
                b.r[key] = d

    def op(self, eng, fn, reads=(), writes=()):
        self._deps(eng, reads, writes)
        if eng.seq >= 30000:
            eng.newsem()
        ins = fn(eng.h)
        eng.seq += 1
        ins.then_inc(eng.sem, 1)
        self._mark((eng.sem, eng.seq, eng.name), eng.name, reads, writes)

    def group(self, eng, fns, reads=(), writes=()):
        self._deps(eng, reads, writes)
        if eng.seq >= 30000:
            eng.newsem()
        ins = None
        for f in fns:
            ins = f(eng.h)
        eng.seq += 1
        ins.then_inc(eng.sem, 1)
        self._mark((eng.sem, eng.seq, eng.name), eng.name, reads, writes)

    def dma(self, eng, out, in_, reads=(), writes=(), sbuf_side=None, track=True, **kw):
        self._deps(eng, reads, writes)
        ins = eng.h.dma_start(out=out, in_=in_, **kw)
        tgt = sbuf_side
        if tgt.dsem is None:
            if self.sem_pool and self.sem_pool[0][1] < 20000:
                tgt.dsem, tgt.dval = self.sem_pool.pop(0)
            else:
                tgt.dsem = self.stack.enter_context(self.nc.semaphore("d%d" % self.nd))
                self.nd += 1
                tgt.dval = 0
            if self.scopes:
                self.scopes[-1].append(tgt)
        tgt.dval += 16
        ins.then_inc(tgt.dsem, 16)
        d = (tgt.dsem, tgt.dval, "dma")
        self._mark(d, "dma%d" % id(tgt), reads, writes)
        if track:
            self.inflight.append(d)
        return d

    def barrier(self):
        deps = [(e.sem, e.seq, e.name) for e in self.engs if e.seq > 0] + self.inflight
        for e in self.engs:
            for d in deps:
                if d[2] != e.name:
                    self._wait(e, d)
        self.inflight = []


class _Scope:
    def __init__(self, fw):
        self.fw = fw
        self.es = ExitStack()

    def __enter__(self):
        self.fw.push_scope()
        self.es.__enter__()
        return self.es

    def __exit__(self, *a):
        if a[0] is None:
            self.fw.barrier()
            self.fw.pop_scope()
        return self.es.__exit__(*a)


class WStream:
    SLOT = 8192

    def __init__(self, fw, nslots, sc=None):
        self.fw = fw
        self.t = [fw.sbuf("ring", [128, self.SLOT], BF16, sc) for _ in range(nslots)]
        self.b = bufs(nslots, "ring")
        self.q = []
        self.issued = 0
        self.taken = 0

    def enqueue(self, fn):
        self.q.append(fn)

    def take(self):
        n = len(self.t)
        lim = min(self.taken + n - 1, len(self.q))
        while self.issued < lim:
            k = self.issued % n
            self.q[self.issued](self.t[k], self.b[k])
            self.issued += 1
        k = self.taken % n
        assert self.taken < self.issued
        self.taken += 1
        return self.t[k], self.b[k]


class Prog:
    def __init__(self, debug=None):
        self.debug = debug or ()
        self.nc = nc = bass.Bass("TRN2", target_bir_lowering=False)
        self.fw = fw = FW(nc)
        self.ext_in = {}
        self.ext_out = {}

    def din(self, name, shape, dt=F32):
        t = self.nc.dram_tensor(name, list(shape), dt, kind="ExternalInput").ap()
        self.ext_in[name] = t
        return t

    def dscratch(self, name, shape, dt):
        kind = "ExternalOutput" if name in self.debug else "Internal"
        t = self.nc.dram_tensor(name, list(shape), dt, kind=kind).ap()
        return t

    def declare(self):
        p = self
        p.x = p.din("x", [LAT, D])
        p.ctx = p.din("ctx", [CTX, D])
        p.cT = p.din("cT", [128, DC, 2])
        p.ada_w = p.din("ada_w", [2, D, 9 * D])
        p.ada_bT = p.din("ada_bT", [2, 128, 144])
        p.norm_gT = p.din("norm_gT", [2, 128, 3, DC])
        p.final_gT = p.din("final_gT", [128, DC])
        p.w_gate = p.din("ffn_w_gate", [2, 2, D, FF])
        p.w_up = p.din("ffn_w_up", [2, 2, D, FF])
        p.w_down = p.din("ffn_w_down", [2, 2, FF, D])
        p.w_in = p.din("w_in", [2, D, 3584])
        p.w_in_rot = p.din("w_in_rot", [2, D, 2304])
        p.w_out = p.din("w_out", [2, D, D])
        p.lamT = p.din("lamT", [2, 2, 128, 3, 16])
        p.XB = p.din("XB", [2, 2, 2, 16, 128, 128])
        p.YC = p.din("YC", [2, 2, 2, 16, 128, 128])
        p.s5dT = p.din("s5dT", [2, 128, 2, 4])
        p.glu_w = p.din("s5_glu_w", [2, 512, 512])
        p.dlam = p.din("dlam", [2, 1, 256])
        p.hgT = p.din("hgT", [2, 128, 5])
        p.out = p.nc.dram_tensor("out", [LAT, D], F32, kind="ExternalOutput").ap()
        p.xT = p.dscratch("xT", [D, S], F32)
        p.wguS = [[p.dscratch("wgu%d%d" % (i, f), [FC // 2, 128, 2, DC, 256], BF16) for f in range(2)] for i in range(2)]
        p.wdS = [[p.dscratch("wd%d%d" % (i, f), [DC, 128, FC, 128], BF16) for f in range(2)] for i in range(2)]
        p.winS = [p.dscratch("win%d" % i, [10, 128, 4, DC, 128], BF16) for i in range(2)]
        p.wvdS = [p.dscratch("wvd%d" % i, [128, DC, 512], BF16) for i in range(2)]
        p.wvgS = [p.dscratch("wvg%d" % i, [128, DC, 256], BF16) for i in range(2)]
        p.woutS = [p.dscratch("wout%d" % i, [4, 128, 4, DC, 128], BF16) for i in range(2)]
        p.adaS = [p.dscratch("ada%d" % i, [36, 128, DC, 512], BF16) for i in range(2)]
        p.gluS = [p.dscratch("glu%d" % i, [128, 4, 512], BF16) for i in range(2)]
        p.uT = p.dscratch("uT", [512, S], F32)
        p.uTb = p.dscratch("uTb", [512, S], BF16)
        p.qd = p.dscratch("qd", [512, S], BF16)
        p.kd = p.dscratch("kd", [512, S], BF16)
        p.vd = p.dscratch("vd", [4, 128, NB, 128], BF16)
        p.qg = p.dscratch("qg", [1024, S], BF16)
        p.kg = p.dscratch("kg", [256, S], BF16)
        p.vg = p.dscratch("vg", [2, 128, NB, 128], BF16)
        p.yF = p.dscratch("yF", [512, S], F32)
        p.yR = p.dscratch("yR", [512, S], F32)
        p.s5T = p.dscratch("s5T", [512, S], BF16)
        p.ropeT = p.dscratch("ropeT", [4, 128, LAT], F32)
        p.mdbg = p.dscratch("mdbg", [128, 144, 2], F32)
        p.wbuf = {}

    def conv(self, key, out, in_):
        fw = self.fw
        if self.conv_filter is not None and not any(key.startswith(k) for k in self.conv_filter):
            return
        b = self.wbuf.setdefault(key, Buf(key))
        fw.dma(fw.pool, out, in_, writes=[b], sbuf_side=b, track=False)

    def convert_layer(self, i):
        p = self
        for blk in range(36):
            p.conv("ada%d" % i, p.adaS[i][blk], p.ada_w[i][:, blk * 512:(blk + 1) * 512].rearrange("(kc p) n -> p kc n", p=128))
        self.convert_ffn(i, 0)
        wi, wr = p.w_in[i], p.w_in_rot[i]

        def colchunk(src, c0):
            return src[:, c0:c0 + 128].rearrange("(kc p) n -> p kc n", p=128)
        groups = [
            [(wi, 0), (wi, 128), (wi, 256), (wi, 384)],
            [(wi, 512 + 128 * k) for k in range(4)],
            [(wr, 0 + 128 * k) for k in range(4)],
            [(wi, 1024 + 128 * k) for k in range(4)],
            [(wr, 512 + 128 * k) for k in range(4)],
            [(wi, 2048 + 128 * k) for k in range(4)],
            [(wr, 1024 + 128 * k) for k in range(4)],
            [(wi, 2560 + 128 * k) for k in range(4)],
            [(wr, 1536 + 128 * k) for k in range(4)],
            [(wi, 3072), (wi, 3200), (wr, 2048), (wr, 2176)],
        ]
        for g, lst in enumerate(groups):
            if g in (2, 4, 6, 8):
                continue
            for ci, (src, c0) in enumerate(lst):
                if g == 9 and ci >= 2:
                    continue
                p.conv("win%d" % i, p.winS[i][g, :, ci], colchunk(src, c0))
        p.conv("wv%d" % i, p.wvdS[i], wi[:, 1536:2048].rearrange("(kc p) n -> p kc n", p=128))
        p.conv("wv%d" % i, p.wvgS[i], wi[:, 3328:3584].rearrange("(kc p) n -> p kc n", p=128))
        p.conv("glu%d" % i, p.gluS[i], p.glu_w[i].rearrange("(fi p) n -> p fi n", p=128))
        for dg in range(4):
            for ci in range(4):
                dc = dg * 4 + ci
                p.conv("wout%d" % i, p.woutS[i][dg, :, ci], p.w_out[i][:, dc * 128:(dc + 1) * 128].rearrange("(kc p) n -> p kc n", p=128))
        self.convert_ffn(i, 1)

    def convert_ffn(self, i, f):
        p = self
        for jp in range(FC // 2):
            p.conv("wgu%d%d_%d" % (i, f, jp // 6), p.wguS[i][f][jp, :, 0], p.w_gate[i, f][:, jp * 256:(jp + 1) * 256].rearrange("(kc p) n -> p kc n", p=128))
            p.conv("wgu%d%d_%d" % (i, f, jp // 6), p.wguS[i][f][jp, :, 1], p.w_up[i, f][:, jp * 256:(jp + 1) * 256].rearrange("(kc p) n -> p kc n", p=128))
        for dc in range(DC):
            for h2 in range(2):
                p.conv("wd%d%d_%d" % (i, f, dc // 8), p.wdS[i][f][dc, :, h2 * 22:(h2 + 1) * 22],
                       p.w_down[i, f][h2 * 2816:(h2 + 1) * 2816, dc * 128:(dc + 1) * 128].rearrange("(j p) n -> p j n", p=128))

    def setup(self):
        p, fw, nc = self, self.fw, self.nc
        p.bpair = [fw.psum("bpair", [128, 2, 512], F32) for _ in range(4)]
        p.banks = [p.bpair[k // 2][:, k % 2, :] for k in range(8)]
        p.bb = [Buf("bank%d" % k, excl=True) for k in range(8)]
        p.ones = fw.sbuf("ones", [128, 128], BF16)
        p.bconst = Buf("const")
        p.ident = fw.sbuf("ident", [128, 128], F32)
        p.ones32 = fw.sbuf("ones32", [128, 128], F32)
        p.perm = fw.sbuf("perm", [128, 2, 128], BF16)
        p.epsc = fw.sbuf("epsc", [128, 1], F32)
        p.negpi = fw.sbuf("negpi", [128, 1], F32)
        p.mT = fw.sbuf("mT", [128, 144, 2], F32)
        p.bmT = Buf("mT")
        p.Acol = fw.sbuf("Acol", [128, 3, DC, 2], F32)
        p.Gcol = fw.sbuf("Gcol", [128, 3, DC, 2], F32)
        p.bcols = Buf("cols")
        p.ngT = fw.sbuf("ngT", [128, 2, 3, DC], F32)
        p.fgT = fw.sbuf("fgT", [128, DC], F32)
        p.hg = fw.sbuf("hg", [128, 2, 5], F32)
        p.s5d = fw.sbuf("s5d", [128, 2, 2, 4], F32)
        p.sq = [fw.sbuf("sq", [128, 512], BF16) for _ in range(2)]
        p.bsq = bufs(2, "sq")
        p.tf = [fw.sbuf("tf", [128, 512], F32) for _ in range(4)]
        p.btf = bufs(4, "tf")
        p.rs = fw.sbuf("rs", [128, 512], F32)
        p.brs = Buf("rs")
        p.lamc = fw.sbuf("lamc", [128, 4], F32)
        p.blamc = Buf("lamc")
        bc = p.bconst
        fw.op(fw.pool, lambda e: e.memset(p.ones[:], 1.0), writes=[bc])
        fw.op(fw.pool, lambda e: e.memset(p.ones32[:], 1.0), writes=[bc])
        fw.op(fw.pool, lambda e: e.memset(p.ident[:], 0.0), writes=[bc])
        fw.op(fw.pool, lambda e: e.affine_select(out=p.ident[:], in_=p.ident[:], pattern=[[-1, 128]], compare_op=ALU.not_equal,
                                                  fill=1.0, base=0, channel_multiplier=1), reads=[bc], writes=[bc])
        for b_ in range(4):
            fw.op(fw.pool, lambda e, b_=b_: e.tensor_copy(out=p.perm[:, 0, 32 * b_:32 * b_ + 32], in_=p.ident[:, 32 * (b_ ^ 1):32 * (b_ ^ 1) + 32]),
                  reads=[bc], writes=[bc])
        for b_ in range(8):
            fw.op(fw.pool, lambda e, b_=b_: e.tensor_copy(out=p.perm[:, 1, 16 * b_:16 * b_ + 16], in_=p.ident[:, 16 * (b_ ^ 1):16 * (b_ ^ 1) + 16]),
                  reads=[bc], writes=[bc])
        fw.op(fw.pool, lambda e: e.memset(p.epsc[:], EPS), writes=[bc])
        fw.op(fw.pool, lambda e: e.memset(p.negpi[:], -PI), writes=[bc])
        bl = Buf("smallloads")
        fw.dma(fw.sp, p.ngT[:], p.norm_gT.rearrange("i p k c -> p i k c"), writes=[bl], sbuf_side=bl)
        fw.dma(fw.sp, p.fgT[:], p.final_gT, writes=[bl], sbuf_side=bl)
        fw.dma(fw.sp, p.hg[:], p.hgT.rearrange("i p k -> p i k"), writes=[bl], sbuf_side=bl)
        fw.dma(fw.sp, p.s5d[:], p.s5dT.rearrange("i p a k -> p i a k"), writes=[bl], sbuf_side=bl)
        p.bsmall = bl

    def rope_tables(self):
        p, fw = self, self.fw
        with self.fw.scope() as sc:
            di = fw.sbuf("di", [128, 1], F32, sc)
            col = fw.sbuf("col", [128, 8], F32, sc)
            prow = fw.sbuf("prow", [128, LAT], F32, sc)
            pcol = fw.sbuf("pcol", [128, LAT], F32, sc)
            ang = fw.sbuf("ang", [128, LAT], F32, sc)
            tb = fw.sbuf("tb", [128, LAT], F32, sc)
            prow2 = fw.sbuf("prow2", [128, LAT], F32, sc)
            b = Buf("rope")
            fw.op(fw.pool, lambda e: e.iota(prow[:].rearrange("p (r c) -> p r c", c=64), pattern=[[1, 64], [0, 64]], base=0,
                                            channel_multiplier=0, allow_small_or_imprecise_dtypes=True), writes=[b])
            fw.op(fw.pool, lambda e: e.iota(pcol[:].rearrange("p (r c) -> p r c", c=64), pattern=[[0, 64], [1, 64]], base=0,
                                            channel_multiplier=0, allow_small_or_imprecise_dtypes=True), reads=[b], writes=[b])
            fw.op(fw.pool, lambda e: e.iota(di[:], pattern=[[0, 1]], base=0, channel_multiplier=1,
                                            allow_small_or_imprecise_dtypes=True), reads=[b], writes=[b])
            dv = fw.dve

            def o(fn):
                fw.op(dv, fn, reads=[b], writes=[b])
            MAGIC = 12582912.0
            o(lambda e: e.tensor_single_scalar(out=col[:, 3:4], in_=di[:], scalar=63.5, op=ALU.is_gt))
            o(lambda e: e.scalar_tensor_tensor(out=col[:, 6:7], in0=col[:, 3:4], scalar=-64.0, in1=di[:], op0=ALU.mult, op1=ALU.add))
            o(lambda e: e.tensor_single_scalar(out=col[:, 4:5], in_=col[:, 6:7], scalar=31.5, op=ALU.is_gt))
            o(lambda e: e.scalar_tensor_tensor(out=col[:, 7:8], in0=col[:, 4:5], scalar=-32.0, in1=col[:, 6:7], op0=ALU.mult, op1=ALU.add))
            o(lambda e: e.tensor_single_scalar(out=col[:, 5:6], in_=col[:, 7:8], scalar=15.5, op=ALU.is_gt))
            o(lambda e: e.scalar_tensor_tensor(out=col[:, 6:7], in0=col[:, 5:6], scalar=-16.0, in1=col[:, 7:8], op0=ALU.mult, op1=ALU.add))
            for kind in range(2):
                half = 32 if kind == 0 else 16
                jcol = col[:, 7:8] if kind == 0 else col[:, 6:7]
                rowb = col[:, 3:4] if kind == 0 else col[:, 4:5]
                sgnb = col[:, 4:5] if kind == 0 else col[:, 5:6]
                fw.op(fw.act, lambda e: e.activation(out=col[:, 0:1], in_=jcol, func=AF.Exp, scale=-math.log(10000.0) / half),
                      reads=[b], writes=[b])
                o(lambda e: e.tensor_scalar(out=col[:, 1:2], in0=rowb, scalar1=-1.0, scalar2=1.0, op0=ALU.mult, op1=ALU.add))
                o(lambda e: e.tensor_scalar(out=col[:, 2:3], in0=sgnb, scalar1=2.0, scalar2=-1.0, op0=ALU.mult, op1=ALU.add))
                o(lambda e: e.tensor_tensor(out=ang[:], in0=prow[:], in1=pcol[:], op=ALU.subtract))
                o(lambda e: e.scalar_tensor_tensor(out=ang[:], in0=ang[:], scalar=col[:, 1:2], in1=pcol[:], op0=ALU.mult, op1=ALU.add))
                o(lambda e: e.tensor_scalar(out=ang[:], in0=ang[:], scalar1=col[:, 0:1], scalar2=None, op0=ALU.mult))
                for which in range(2):
                    if which == 0:
                        o(lambda e: e.tensor_scalar(out=tb[:], in0=ang[:], scalar1=0.5 * PI, scalar2=None, op0=ALU.add))
                    else:
                        o(lambda e: e.tensor_copy(out=tb[:], in_=ang[:]))
                    o(lambda e: e.tensor_scalar(out=prow2[:], in0=tb[:], scalar1=1.0 / (2 * PI), scalar2=MAGIC, op0=ALU.mult, op1=ALU.add))
                    o(lambda e: e.tensor_scalar(out=prow2[:], in0=prow2[:], scalar1=-MAGIC, scalar2=None, op0=ALU.add))
                    o(lambda e: e.scalar_tensor_tensor(out=tb[:], in0=prow2[:], scalar=-2 * PI, in1=tb[:], op0=ALU.mult, op1=ALU.add))
                    o(lambda e: e.tensor_scalar(out=tb[:], in0=tb[:], scalar1=PI, scalar2=-PI, op0=ALU.min, op1=ALU.max))
                    fw.op(fw.act, lambda e: e.activation(out=tb[:], in_=tb[:], func=AF.Sin), reads=[b], writes=[b])
                    if which == 1:
                        o(lambda e: e.tensor_scalar(out=tb[:], in0=tb[:], scalar1=col[:, 2:3], scalar2=None, op0=ALU.mult))
                    fw.dma(fw.sp, p.ropeT[2 * kind + which], tb[:], reads=[b], sbuf_side=b)

    def pass_alloc(self, sc):
        p, fw = self, self.fw
        p.ws = WStream(fw, 4, sc)
        p.xt = fw.sbuf("xt", [128, DC, 512], F32, sc)
        p.bxt = bufs(DC, "xt")
        p.ht = fw.sbuf("ht", [128, DC, 512], BF16, sc)
        p.bht = bufs(DC, "ht")

    def modulation(self, i):
        p, fw = self, self.fw
        with self.fw.scope() as sc:
            ws = p.ws = WStream(fw, 4, sc)
            cs = fw.sbuf("cs", [128, DC, 2], F32, sc)
            sc_b = fw.sbuf("scb", [128, DC, 2], BF16, sc)
            adab = fw.sbuf("adab", [128, 144], F32, sc)
            bl = Buf("modl")
            fw.dma(fw.sp, cs[:], p.cT, writes=[bl], sbuf_side=bl)
            fw.dma(fw.sp, adab[:], p.ada_bT[i], writes=[bl], sbuf_side=bl)
            fw.op(fw.act, lambda e: e.activation(out=sc_b[:], in_=cs[:], func=AF.Silu), reads=[bl], writes=[bl])
            wb = p.wbuf["ada%d" % i]
            for blk in range(36):
                ws.enqueue(lambda t, b, blk=blk: fw.dma(fw.sp, t[:, :].rearrange("p (kc n) -> p kc n", kc=DC), p.adaS[i][blk],
                                                         reads=[wb], writes=[b], sbuf_side=b))
            bank, bbk = p.banks[0], p.bb[0]
            for blk in range(36):
                t, b = ws.take()
                tv = t[:, :].rearrange("p (kc n) -> p kc n", kc=DC)
                for c4 in range(4):
                    cc = blk * 4 + c4
                    fw.group(fw.pe, [
                        (lambda e, kc=kc, c4=c4, cc=cc: e.matmul(bank[:, 2 * cc:2 * cc + 2], lhsT=tv[:, kc, c4 * 128:(c4 + 1) * 128],
                                                                 rhs=sc_b[:, kc, :], start=(kc == 0), stop=(kc == DC - 1)))
                        for kc in range(DC)], reads=[b, bl], writes=[bbk])
            fw.op(fw.dve, lambda e: e.tensor_tensor(out=p.mT[:], in0=bank[:, 0:288].rearrange("p (c t) -> p c t", t=2),
                                                    in1=adab[:].unsqueeze(2).broadcast_to([128, 144, 2]), op=ALU.add),
                  reads=[bbk, bl], writes=[p.bmT])
            m4 = p.mT[:].rearrange("p (k c) t -> p k c t", c=DC)
            for k in range(3):
                g = p.ngT[:, i, k, :].unsqueeze(2).broadcast_to([128, DC, 2])
                fw.op(fw.dve, lambda e, k=k: e.tensor_scalar(out=p.Acol[:, k], in0=m4[:, 3 * k + 1], scalar1=1.0, scalar2=None, op0=ALU.add),
                      reads=[p.bmT], writes=[p.bcols])
                fw.op(fw.dve, lambda e, k=k, g=g: e.tensor_tensor(out=p.Acol[:, k], in0=p.Acol[:, k], in1=g, op=ALU.mult),
                      reads=[p.bcols, p.bsmall], writes=[p.bcols])
                fw.op(fw.dve, lambda e, k=k: e.tensor_scalar(out=p.Gcol[:, k], in0=m4[:, 3 * k + 2], scalar1=(1.0 if k == 1 else 0.5),
                                                            scalar2=None, op0=ALU.mult), reads=[p.bmT, p.bcols], writes=[p.bcols])
            if "mdbg" in p.debug:
                fw.dma(fw.act, p.mdbg, p.mT[:], reads=[p.bmT], sbuf_side=p.bmT)
            fw.barrier()

    def rmsnorm_mod(self, i, k, c, N):
        p, fw = self, self.fw
        ssb, bss = p.banks[6], p.bb[6]
        for kc in range(DC):
            s, bs = p.sq[kc % 2], p.bsq[kc % 2]
            fw.op(fw.act, lambda e, kc=kc, s=s: e.activation(out=s[:, :N], in_=p.xt[:, kc, :N], func=AF.Square),
                  reads=[p.bxt[kc]], writes=[bs])
            fw.op(fw.pe, lambda e, kc=kc, s=s: e.matmul(ssb[:, :N], lhsT=p.ones[:], rhs=s[:, :N], start=(kc == 0), stop=(kc == DC - 1)),
                  reads=[bs, p.bconst], writes=[bss])
        fw.op(fw.act, lambda e: e.activation(out=p.rs[:, :N], in_=ssb[:, :N], func=AF.Sqrt, scale=1.0 / D, bias=p.epsc[:]),
              reads=[bss, p.bconst], writes=[p.brs])
        fw.op(fw.dve, lambda e: e.reciprocal(out=p.rs[:, :N], in_=p.rs[:, :N]), reads=[p.brs], writes=[p.brs])
        m4 = p.mT[:].rearrange("p (k c) t -> p k c t", c=DC)
        for kc in range(DC):
            t, bt = p.tf[kc % 2], p.btf[kc % 2]
            fw.op(fw.dve, lambda e, kc=kc, t=t: e.tensor_tensor(out=t[:, :N], in0=p.xt[:, kc, :N], in1=p.rs[:, :N], op=ALU.mult),
                  reads=[p.bxt[kc], p.brs], writes=[bt])
            fw.op(fw.act, lambda e, kc=kc, t=t: e.activation(out=p.ht[:, kc, :N], in_=t[:, :N], func=AF.Identity,
                                                             scale=p.Acol[:, k, kc, c:c + 1], bias=m4[:, 3 * k, kc, c:c + 1]),
                  reads=[bt, p.bcols, p.bmT], writes=[p.bht[kc]])

    def enqueue_ffn(self, i, f):
        p, fw, ws = self, self.fw, self.ws
        for jp in range(FC // 2):
            ws.enqueue(lambda t, b, jp=jp: fw.dma(fw.sp, t[:, :].rearrange("p (g kc n) -> p g kc n", g=2, kc=DC),
                                                   p.wguS[i][f][jp], reads=[p.wbuf["wgu%d%d_%d" % (i, f, jp // 6)]], writes=[b], sbuf_side=b))
        for dc in range(DC):
            ws.enqueue(lambda t, b, dc=dc: fw.dma(fw.sp, t[:, 0:FC * 128].rearrange("p (j n) -> p j n", j=FC),
                                                   p.wdS[i][f][dc], reads=[p.wbuf["wd%d%d_%d" % (i, f, dc // 8)]], writes=[b], sbuf_side=b))

    def ffn(self, i, f, k, c, N, act, bact):
        p, fw, ws = self, self.fw, self.ws
        for jp in range(FC // 2):
            t, b = ws.take()
            tv = t[:, :].rearrange("p (g kc n) -> p g kc n", g=2, kc=DC)
            for jj in range(2):
                j = 2 * jp + jj
                gb, bgb = p.banks[j % 2], p.bb[j % 2]
                ub, bub = p.banks[2 + j % 2], p.bb[2 + j % 2]
                fw.group(fw.pe, [(lambda e, kc=kc, jj=jj, gb=gb: e.matmul(gb[:, :N], lhsT=tv[:, 0, kc, jj * 128:(jj + 1) * 128], rhs=p.ht[:, kc, :N],
                                                                         start=(kc == 0), stop=(kc == DC - 1))) for kc in range(DC)],
                         reads=[b] + p.bht, writes=[bgb])
                fw.group(fw.pe, [(lambda e, kc=kc, jj=jj, ub=ub: e.matmul(ub[:, :N], lhsT=tv[:, 1, kc, jj * 128:(jj + 1) * 128], rhs=p.ht[:, kc, :N],
                                                                         start=(kc == 0), stop=(kc == DC - 1))) for kc in range(DC)],
                         reads=[b] + p.bht, writes=[bub])
                st, bst = p.tf[2 + j % 2], p.btf[2 + j % 2]
                fw.op(fw.act, lambda e, gb=gb, st=st: e.activation(out=st[:, :N], in_=gb[:, :N], func=AF.Silu), reads=[bgb], writes=[bst])
                fw.op(fw.dve, lambda e, ub=ub, st=st, j=j: e.tensor_tensor(out=act[:, j, :N], in0=st[:, :N], in1=ub[:, :N], op=ALU.mult),
                      reads=[bst, bub], writes=[bact[j]])
        for dc in range(DC):
            t, b = ws.take()
            tv = t[:, 0:FC * 128].rearrange("p (j n) -> p j n", j=FC)
            yb, byb = p.banks[4 + dc % 2], p.bb[4 + dc % 2]
            fw.group(fw.pe, [(lambda e, j=j, yb=yb: e.matmul(yb[:, :N], lhsT=tv[:, j, :], rhs=act[:, j, :N], start=(j == 0), stop=(j == FC - 1)))
                             for j in range(FC)], reads=[b] + bact, writes=[byb])
            fw.op(fw.dve, lambda e, dc=dc, yb=yb: e.scalar_tensor_tensor(out=p.xt[:, dc, :N], in0=yb[:, :N], scalar=p.Gcol[:, k, dc, c:c + 1],
                                                                        in1=p.xt[:, dc, :N], op0=ALU.mult, op1=ALU.add),
                  reads=[byb, p.bcols, p.bxt[dc]], writes=[p.bxt[dc]])

    def load_x_tile(self, i, t0, N, is_ctx):
        p, fw = self, self.fw
        if i > 0:
            bl = p.bxt
            fw.dma(fw.sp, p.xt[:, :, :N], p.xT.rearrange("(kc q) t -> q kc t", q=128)[:, :, t0:t0 + N], writes=bl, sbuf_side=bl[0])
            return
        with self.fw.scope() as sc:
            xtok = [fw.sbuf("xtok", [128, D], F32, sc) for _ in range(2)]
            bxk = bufs(2, "xtok")
            nb = 0
            for blk in range(N // 128):
                src = p.ctx[blk * 128:(blk + 1) * 128, :] if is_ctx else p.x[t0 - CTX + blk * 128: t0 - CTX + (blk + 1) * 128, :]
                xk, bk = xtok[blk % 2], bxk[blk % 2]
                fw.dma(fw.sp, xk[:], src, writes=[bk], sbuf_side=bk)
                for dg in range(4):
                    bank, bbk = p.banks[nb % 4], p.bb[nb % 4]
                    nb += 1
                    fw.group(fw.pe, [(lambda e, q=q, dg=dg, bank=bank, xk=xk: e.transpose(out=bank[:, q * 128:(q + 1) * 128],
                                                                                      in_=xk[:, (dg * 4 + q) * 128:(dg * 4 + q + 1) * 128],
                                                                                      identity=p.ident[:])) for q in range(4)],
                             reads=[bk, p.bconst], writes=[bbk])
                    eng = fw.act if dg % 2 == 0 else fw.dve
                    outap = p.xt[:, dg * 4:(dg + 1) * 4, blk * 128:(blk + 1) * 128]
                    inap = bank[:, :].rearrange("p (q n) -> p q n", q=4)
                    if eng is fw.act:
                        fw.op(eng, lambda e, outap=outap, inap=inap: e.activation(out=outap, in_=inap, func=AF.Copy),
                              reads=[bbk], writes=p.bxt[dg * 4:(dg + 1) * 4])
                    else:
                        fw.op(eng, lambda e, outap=outap, inap=inap: e.tensor_copy(out=outap, in_=inap),
                              reads=[bbk], writes=p.bxt[dg * 4:(dg + 1) * 4])
            fw.barrier()

    def pass_a(self, i):
        with self.fw.scope() as sc:
            self.pass_alloc(sc)
            self._pass_a(i)

    def _pass_a(self, i):
        p, fw, ws = self, self.fw, self.ws
        wbin = p.wbuf["win%d" % i]
        wbv = p.wbuf["wv%d" % i]
        for (t0, N, is_ctx) in TILES:
            p.enqueue_ffn(i, 0)
            for g in range(10):
                if g in (2, 4, 6, 8):
                    continue
                ws.enqueue(lambda t, b, g=g: fw.dma(fw.sp, t[:, :].rearrange("p (c kc n) -> p c kc n", c=4, kc=DC), p.winS[i][g],
                                                     reads=[wbin], writes=[b], sbuf_side=b))
            ws.enqueue(lambda t, b: fw.dma(fw.sp, t[:, :].rearrange("p (kc n) -> p kc n", kc=DC), p.wvdS[i],
                                           reads=[wbv], writes=[b], sbuf_side=b))
            ws.enqueue(lambda t, b: fw.dma(fw.sp, t[:, 0:DC * 256].rearrange("p (kc n) -> p kc n", kc=DC), p.wvgS[i],
                                           reads=[wbv], writes=[b], sbuf_side=b))
        for (t0, N, is_ctx) in TILES[:p.ntiles]:
            c = 1 if is_ctx else 0
            p.load_x_tile(i, t0, N, is_ctx)
            with self.fw.scope() as sc:
                act = fw.sbuf("act", [128, FC, 512], BF16, sc)
                bact = bufs(FC, "act")
                p.rmsnorm_mod(i, 0, c, N)
                p.ffn(i, 0, 0, c, N, act, bact)
                fw.barrier()
            fw.dma(fw.act, p.xT.rearrange("(kc q) t -> q kc t", q=128)[:, :, t0:t0 + N], p.xt[:, :, :N], reads=p.bxt, sbuf_side=p.bxt[0])
            if p.stop == "ffn":
                continue
            p.rmsnorm_mod(i, 1, c, N)
            with self.fw.scope() as sc:
                p.in_proj(i, t0, N, is_ctx, sc)
                fw.barrier()

    def in_proj(self, i, t0, N, is_ctx, sc):
        p, fw, ws = self, self.fw, self.ws
        ust = fw.sbuf("ust", [128, 4, 512], F32, sc)
        usb = fw.sbuf("usb", [128, 4, 512], BF16, sc)
        qst = [fw.sbuf("qst", [128, 4, 512], BF16, sc) for _ in range(2)]
        bqst = [Buf("qst0"), Buf("qst1")]
        vsd = fw.sbuf("vsd", [128, 4, 512], BF16, sc)
        vsg = fw.sbuf("vsg", [128, 4, 256], BF16, sc)
        bu, bub, bvd, bvg = Buf("ust"), Buf("usb"), Buf("vsd"), Buf("vsg")
        rp = fw.sbuf("rp", [128, 4, 512], F32, sc)
        rq = fw.sbuf("rq", [128, 4, 512], F32, sc)
        brp, brq = Buf("rp"), Buf("rq")
        if not is_ctx:
            l0 = t0 - CTX
            fw.dma(fw.sp, rp[:, :, :N], p.ropeT.rearrange("k q t -> q k t")[:, :, l0:l0 + N], writes=[brp], sbuf_side=brp)
            for n, (tb, gi) in enumerate([(0, 1), (1, 2), (0, 3), (1, 4)]):
                fw.op(fw.dve, lambda e, n=n, tb=tb, gi=gi: e.tensor_scalar(out=rq[:, n, :N], in0=rp[:, tb, :N], scalar1=p.hg[:, i, gi:gi + 1],
                                                                           scalar2=None, op0=ALU.mult), reads=[brp, p.bsmall], writes=[brq])
        nbank = [0]
        nqb = [0]
        qbf = [fw.sbuf("qbf", [128, 512], BF16, sc) for _ in range(2)]
        bqbf = bufs(2, "qbf")

        def mm_chunk(tv, ci):
            bank, bbk = p.banks[nbank[0] % 6], p.bb[nbank[0] % 6]
            nbank[0] += 1
            return bank, bbk, [(lambda e, kc=kc: e.matmul(bank[:, :N], lhsT=tv[:, ci, kc, :], rhs=p.ht[:, kc, :N], start=(kc == 0),
                                                          stop=(kc == DC - 1))) for kc in range(DC)]

        def view(t):
            return t[:, :].rearrange("p (c kc n) -> p c kc n", c=4, kc=DC)
        t, b = ws.take()
        tv = view(t)
        for ci in range(4):
            bank, bbk, mms = mm_chunk(tv, ci)
            fw.group(fw.pe, mms, reads=[b] + p.bht, writes=[bbk])
            fw.op(fw.act, lambda e, ci=ci, bank=bank: e.activation(out=ust[:, ci, :N], in_=bank[:, :N], func=AF.Copy), reads=[bbk], writes=[bu])
            fw.op(fw.dve, lambda e, ci=ci: e.tensor_copy(out=usb[:, ci, :N], in_=ust[:, ci, :N]), reads=[bu], writes=[bub])
        fw.dma(fw.act, p.uT.rearrange("(c q) t -> q c t", q=128)[:, :, t0:t0 + N], ust[:, :, :N], reads=[bu], sbuf_side=bu)
        fw.dma(fw.act, p.uTb.rearrange("(c q) t -> q c t", q=128)[:, :, t0:t0 + N], usb[:, :, :N], reads=[bub], sbuf_side=bub)

        def qk_group(dst, dst_c0, nchunks, kind, sidx, gain_main, tabs):
            st, bs_ = qst[sidx], bqst[sidx]
            tm, bm = ws.take()
            tvm = view(tm)
            for ci in range(nchunks):
                bankA, bbA, mmsA = mm_chunk(tvm, ci)
                fw.group(fw.pe, mmsA, reads=[bm] + p.bht, writes=[bbA])
                if not is_ctx:
                    qb_, bqb = qbf[nqb[0] % 2], bqbf[nqb[0] % 2]
                    nqb[0] += 1
                    fw.op(fw.act, lambda e: e.activation(out=qb_[:, :N], in_=bankA[:, :N], func=AF.Copy), reads=[bbA], writes=[bqb])
                    bankB, bbB = p.banks[nbank[0] % 6], p.bb[nbank[0] % 6]
                    nbank[0] += 1
                    fw.op(fw.pe, lambda e: e.matmul(bankB[:, :N], lhsT=p.perm[:, (1 if kind == 'd' else 0), :], rhs=qb_[:, :N], start=True, stop=True),
                          reads=[bqb, p.bconst], writes=[bbB])
                if kind != 'd':
                    s, bs = p.sq[ci % 2], p.bsq[ci % 2]
                    ssb, bss = p.banks[6 + ci % 2], p.bb[6 + ci % 2]
                    fw.op(fw.act, lambda e, s=s, bankA=bankA: e.activation(out=s[:, :N], in_=bankA[:, :N], func=AF.Square), reads=[bbA], writes=[bs])
                    fw.op(fw.pe, lambda e, s=s, ssb=ssb: e.matmul(ssb[:, :N], lhsT=p.ones[:], rhs=s[:, :N], start=True, stop=True),
                          reads=[bs, p.bconst], writes=[bss])
                    fw.op(fw.act, lambda e, ssb=ssb: e.activation(out=p.rs[:, :N], in_=ssb[:, :N], func=AF.Sqrt, scale=1.0 / 128, bias=p.epsc[:]),
                          reads=[bss, p.bconst], writes=[p.brs])
                    fw.op(fw.dve, lambda e: e.reciprocal(out=p.rs[:, :N], in_=p.rs[:, :N]), reads=[p.brs], writes=[p.brs])
                if is_ctx:
                    if kind == 'd':
                        fw.op(fw.dve, lambda e, ci=ci, bankA=bankA: e.tensor_copy(out=st[:, ci, :N], in_=bankA[:, :N]), reads=[bbA], writes=[bs_])
                    else:
                        fw.op(fw.dve, lambda e, ci=ci, bankA=bankA: e.scalar_tensor_tensor(out=st[:, ci, :N], in0=bankA[:, :N],
                                                                                         scalar=p.hg[:, i, gain_main:gain_main + 1],
                                                                                         in1=p.rs[:, :N], op0=ALU.mult, op1=ALU.mult),
                              reads=[bbA, p.brs, p.bsmall], writes=[bs_])
                else:
                    tabt, btab = (rp, brp) if kind == 'd' else (rq, brq)
                    t1, b1 = p.tf[0], p.btf[0]
                    t2, b2 = p.tf[1], p.btf[1]
                    fw.op(fw.dve, lambda e, bankA=bankA, t1=t1: e.tensor_tensor(out=t1[:, :N], in0=bankA[:, :N], in1=tabt[:, tabs[0], :N], op=ALU.mult),
                          reads=[bbA, btab], writes=[b1])
                    fw.op(fw.dve, lambda e, bankB=bankB, t2=t2: e.tensor_tensor(out=t2[:, :N], in0=bankB[:, :N], in1=tabt[:, tabs[1], :N], op=ALU.mult),
                          reads=[bbB, btab], writes=[b2])
                    if kind == 'd':
                        fw.op(fw.dve, lambda e, ci=ci: e.tensor_tensor(out=st[:, ci, :N], in0=t1[:, :N], in1=t2[:, :N], op=ALU.add),
                              reads=[b1, b2], writes=[bs_])
                    else:
                        fw.op(fw.dve, lambda e: e.tensor_tensor(out=t1[:, :N], in0=t1[:, :N], in1=t2[:, :N], op=ALU.add),
                              reads=[b1, b2], writes=[b1])
                        fw.op(fw.dve, lambda e, ci=ci: e.tensor_tensor(out=st[:, ci, :N], in0=t1[:, :N], in1=p.rs[:, :N], op=ALU.mult),
                              reads=[b1, p.brs], writes=[bs_])
            fw.dma(fw.act, dst.rearrange("(c q) t -> q c t", q=128)[:, dst_c0:dst_c0 + nchunks, t0:t0 + N], st[:, 0:nchunks, :N],
                   reads=[bs_], sbuf_side=bs_)
        qk_group(p.qd, 0, 4, 'd', 0, None, (2, 3))
        qk_group(p.kd, 0, 4, 'd', 1, None, (2, 3))
        qk_group(p.qg, 0, 4, 'g', 0, 1, (0, 1))
        qk_group(p.qg, 4, 4, 'g', 1, 1, (0, 1))
        qk_group(p.kg, 0, 2, 'gk', 0, 3, (2, 3))
        tvd, bvdw = ws.take()
        tvg, bvgw = ws.take()
        tvdv = tvd[:, :].rearrange("p (kc n) -> p kc n", kc=DC)
        tvgv = tvg[:, 0:DC * 256].rearrange("p (kc n) -> p kc n", kc=DC)
        for blk in range(N // 128):
            bank, bbk = p.banks[blk % 2], p.bb[blk % 2]
            fw.group(fw.pe, [(lambda e, kc=kc, bank=bank, blk=blk: e.matmul(bank[:, :], lhsT=p.ht[:, kc, blk * 128:(blk + 1) * 128], rhs=tvdv[:, kc, :],
                                                                          start=(kc == 0), stop=(kc == DC - 1))) for kc in range(DC)],
                     reads=[bvdw] + p.bht, writes=[bbk])
            fw.op(fw.act, lambda e, blk=blk, bank=bank: e.activation(out=vsd[:, blk, :], in_=bank[:, :], func=AF.Copy), reads=[bbk], writes=[bvd])
            bank2, bbk2 = p.banks[2 + blk % 2], p.bb[2 + blk % 2]
            fw.group(fw.pe, [(lambda e, kc=kc, bank2=bank2, blk=blk: e.matmul(bank2[:, 0:256], lhsT=p.ht[:, kc, blk * 128:(blk + 1) * 128], rhs=tvgv[:, kc, :],
                                                                            start=(kc == 0), stop=(kc == DC - 1))) for kc in range(DC)],
                     reads=[bvgw] + p.bht, writes=[bbk2])
            fw.op(fw.dve, lambda e, blk=blk, bank2=bank2: e.tensor_copy(out=vsg[:, blk, :], in_=bank2[:, 0:256]), reads=[bbk2], writes=[bvg])
        nbk = N // 128
        b0 = t0 // 128
        for hh in range(4):
            fw.dma(fw.act, p.vd[hh, :, b0:b0 + nbk, :], vsd[:, 0:nbk, hh * 128:(hh + 1) * 128], reads=[bvd], sbuf_side=bvd)
        for hh in range(2):
            fw.dma(fw.act, p.vg[hh, :, b0:b0 + nbk, :], vsg[:, 0:nbk, hh * 128:(hh + 1) * 128], reads=[bvg], sbuf_side=bvg)

    def s5(self, i):
        p, fw = self, self.fw
        MAGIC = 12582912.0
        with fw.scope() as sc:
            dv, pl, ac = fw.dve, fw.pool, fw.act
            bs = Buf("s5setup")
            lamc = fw.sbuf("lamc", [128, 2, 3, 16], F32, sc)
            fw.dma(fw.sp, lamc[:], p.lamT[i].rearrange("d q k c -> q d k c"), writes=[bs], sbuf_side=bs)
            Uf = fw.sbuf("Uf", [128, 128], BF16, sc)
            Ur = fw.sbuf("Ur", [128, 128], BF16, sc)
            with fw.scope() as sc0:
                U32 = fw.sbuf("U32", [128, 128], F32, sc0)
                for (Ux, pat, cm) in ((Uf, 1, -1), (Ur, -1, 1)):
                    fw.op(pl, lambda e: e.memset(U32[:], 1.0), reads=[bs], writes=[bs])
                    fw.op(pl, lambda e: e.affine_select(out=U32[:], in_=U32[:], pattern=[[pat, 128]], compare_op=ALU.is_ge, fill=0.0, base=0,
                                                        channel_multiplier=cm), reads=[bs], writes=[bs])
                    fw.op(pl, lambda e: e.tensor_copy(out=Ux[:], in_=U32[:]), reads=[bs], writes=[bs])
            PCt = [fw.sbuf("PCt", [128, 2048], F32, sc) for _ in range(2)]
            PSt = [fw.sbuf("PSt", [128, 2048], F32, sc) for _ in range(2)]
            QC = [fw.sbuf("QC", [128, 16, 128], F32, sc) for _ in range(2)]
            QS = [fw.sbuf("QS", [128, 16, 128], F32, sc) for _ in range(2)]
            RB = [fw.sbuf("RB", [128, 4, 1024], BF16, sc) for _ in range(2)]
            CT = [fw.sbuf("CT", [128, 16, 2, 128], BF16, sc) for _ in range(2)]
            carry = fw.sbuf("carry", [128, 2, 16, 2], F32, sc)
            btab = Buf("s5tab")

            def o(eng, fn):
                fw.op(eng, fn, reads=[bs, p.bconst], writes=[bs])
            with fw.scope() as sc2:
                cols = fw.sbuf("cols", [128, 2, 12, 16], F32, sc2)
                kv = fw.sbuf("kv", [128, 128], F32, sc2)
                ang = fw.sbuf("ang", [128, 16, 128], F32, sc2)
                nn = fw.sbuf("nn", [128, 16, 128], F32, sc2)
                tc_ = fw.sbuf("tc", [128, 16, 128], F32, sc2)
                ts_ = fw.sbuf("ts", [128, 16, 128], F32, sc2)
                mg = fw.sbuf("mg", [128, 16, 128], F32, sc2)
                xb = fw.sbuf("xb", [128, 2, 16, 128], F32, sc2)
                xw = fw.sbuf("xw", [128, 2, 128], F32, sc2)

                def wrap_sin(dst, src, shift):
                    o(dv, lambda e: e.tensor_scalar(out=dst, in0=src, scalar1=shift, scalar2=None, op0=ALU.add))
                    o(dv, lambda e: e.tensor_scalar(out=nn[:], in0=dst, scalar1=1.0 / (2 * PI), scalar2=MAGIC, op0=ALU.mult, op1=ALU.add))
                    o(dv, lambda e: e.tensor_scalar(out=nn[:], in0=nn[:], scalar1=-MAGIC, scalar2=-2 * PI, op0=ALU.add, op1=ALU.mult))
                    o(dv, lambda e: e.tensor_tensor(out=dst, in0=dst, in1=nn[:], op=ALU.add))
                    o(dv, lambda e: e.tensor_scalar(out=dst, in0=dst, scalar1=PI, scalar2=-PI, op0=ALU.min, op1=ALU.max))
                    o(ac, lambda e: e.activation(out=dst, in_=dst, func=AF.Sin))

                def gen_table(d, base, step):
                    o(pl, lambda e: e.iota(kv[:], pattern=[[step, 128]], base=base, channel_multiplier=0, allow_small_or_imprecise_dtypes=True))
                    for q in range(16):
                        o(dv, lambda e, q=q: e.tensor_scalar(out=ang[:, q, :], in0=kv[:], scalar1=cols[:, d, 2, q:q + 1], scalar2=None, op0=ALU.mult))
                        o(ac, lambda e, q=q: e.activation(out=mg[:, q, :], in_=kv[:], func=AF.Exp, scale=cols[:, d, 1, q:q + 1]))
                    wrap_sin(ts_[:], ang[:], 0.0)
                    wrap_sin(tc_[:], ang[:], 0.5 * PI)
                    o(dv, lambda e: e.tensor_tensor(out=ts_[:], in0=ts_[:], in1=mg[:], op=ALU.mult))
                    o(dv, lambda e: e.tensor_tensor(out=tc_[:], in0=tc_[:], in1=mg[:], op=ALU.mult))

                for d in range(2):
                    c_ = lambda k: cols[:, d, k, :]
                    o(ac, lambda e: e.activation(out=c_(0), in_=lamc[:, d, 2, :], func=AF.Exp))
                    o(dv, lambda e: e.tensor_tensor(out=c_(1), in0=lamc[:, d, 0, :], in1=c_(0), op=ALU.mult))
                    o(dv, lambda e: e.tensor_tensor(out=c_(2), in0=lamc[:, d, 1, :], in1=c_(0), op=ALU.mult))
                    o(ac, lambda e: e.activation(out=c_(9), in_=c_(1), func=AF.Exp))
                    for (dst, shift) in ((4, 0.0), (3, 0.5 * PI)):
                        o(dv, lambda e: e.tensor_scalar(out=c_(11), in0=c_(2), scalar1=shift, scalar2=None, op0=ALU.add))
                        o(dv, lambda e: e.tensor_scalar(out=c_(10), in0=c_(11), scalar1=1.0 / (2 * PI), scalar2=MAGIC, op0=ALU.mult, op1=ALU.add))
                        o(dv, lambda e: e.tensor_scalar(out=c_(10), in0=c_(10), scalar1=-MAGIC, scalar2=-2 * PI, op0=ALU.add, op1=ALU.mult))
                        o(dv, lambda e: e.tensor_tensor(out=c_(11), in0=c_(11), in1=c_(10), op=ALU.add))
                        o(dv, lambda e: e.tensor_scalar(out=c_(11), in0=c_(11), scalar1=PI, scalar2=-PI, op0=ALU.min, op1=ALU.max))
                        o(ac, lambda e, dst=dst: e.activation(out=c_(dst), in_=c_(11), func=AF.Sin))
                        o(dv, lambda e, dst=dst: e.tensor_tensor(out=c_(dst), in0=c_(dst), in1=c_(9), op=ALU.mult))
                    o(dv, lambda e: e.tensor_tensor(out=c_(10), in0=c_(9), in1=c_(9), op=ALU.mult))
                    o(dv, lambda e: e.reciprocal(out=c_(10), in_=c_(10)))
                    o(dv, lambda e: e.tensor_tensor(out=c_(7), in0=c_(3), in1=c_(10), op=ALU.mult))
                    o(dv, lambda e: e.scalar_tensor_tensor(out=c_(8), in0=c_(4), scalar=-1.0, in1=c_(10), op0=ALU.mult, op1=ALU.mult))
                    o(dv, lambda e: e.tensor_tensor(out=c_(10), in0=lamc[:, d, 0, :], in1=lamc[:, d, 0, :], op=ALU.mult))
                    o(dv, lambda e: e.tensor_tensor(out=c_(11), in0=lamc[:, d, 1, :], in1=lamc[:, d, 1, :], op=ALU.mult))
                    o(dv, lambda e: e.tensor_tensor(out=c_(10), in0=c_(10), in1=c_(11), op=ALU.add))
                    o(dv, lambda e: e.reciprocal(out=c_(10), in_=c_(10)))
                    o(dv, lambda e: e.tensor_scalar(out=c_(9), in0=c_(3), scalar1=-1.0, scalar2=None, op0=ALU.add))
                    o(dv, lambda e: e.tensor_tensor(out=c_(5), in0=c_(9), in1=lamc[:, d, 0, :], op=ALU.mult))
                    o(dv, lambda e: e.tensor_tensor(out=c_(11), in0=c_(4), in1=lamc[:, d, 1, :], op=ALU.mult))
                    o(dv, lambda e: e.tensor_tensor(out=c_(5), in0=c_(5), in1=c_(11), op=ALU.add))
                    o(dv, lambda e: e.tensor_tensor(out=c_(5), in0=c_(5), in1=c_(10), op=ALU.mult))
                    o(dv, lambda e: e.tensor_tensor(out=c_(6), in0=c_(4), in1=lamc[:, d, 0, :], op=ALU.mult))
                    o(dv, lambda e: e.tensor_tensor(out=c_(11), in0=c_(9), in1=lamc[:, d, 1, :], op=ALU.mult))
                    o(dv, lambda e: e.tensor_tensor(out=c_(6), in0=c_(6), in1=c_(11), op=ALU.subtract))
                    o(dv, lambda e: e.tensor_tensor(out=c_(6), in0=c_(6), in1=c_(10), op=ALU.mult))
                    fw.dma(fw.sp, xb[:], p.XB[i, d].rearrange("r q c f -> c r q f"), reads=[bs], writes=[bs], sbuf_side=bs)
                    for q in range(16):
                        fc, ql = q // 4, q % 4
                        g_re, g_im = cols[:, d, 5, q:q + 1], cols[:, d, 6, q:q + 1]
                        o(dv, lambda e: e.tensor_scalar(out=xw[:, 0, :], in0=xb[:, 1, q, :], scalar1=g_im, scalar2=None, op0=ALU.mult))
                        o(dv, lambda e: e.scalar_tensor_tensor(out=xw[:, 0, :], in0=xb[:, 0, q, :], scalar=g_re, in1=xw[:, 0, :], op0=ALU.mult, op1=ALU.subtract))
                        o(dv, lambda e: e.tensor_scalar(out=xw[:, 1, :], in0=xb[:, 0, q, :], scalar1=g_im, scalar2=None, op0=ALU.mult))
                        o(dv, lambda e: e.scalar_tensor_tensor(out=xw[:, 1, :], in0=xb[:, 1, q, :], scalar=g_re, in1=xw[:, 1, :], op0=ALU.mult, op1=ALU.add))
                        bank, bbk = p.banks[q % 2], p.bb[q % 2]
                        fw.group(fw.pe, [(lambda e, ri=ri: e.transpose(out=bank[:, ri * 128:(ri + 1) * 128], in_=xw[:, ri, :], identity=p.ident[:]))
                                         for ri in range(2)], reads=[bs, p.bconst], writes=[bbk])
                        for ri in range(2):
                            fw.op(ac, lambda e, ri=ri: e.activation(out=RB[d][:, fc, ri * 512 + ql * 128: ri * 512 + (ql + 1) * 128],
                                                                    in_=bank[:, ri * 128:(ri + 1) * 128], func=AF.Copy),
                                  reads=[bbk], writes=[btab])
                    fw.dma(fw.sp, xb[:], p.YC[i, d].rearrange("r q c f -> c r q f"), reads=[bs], writes=[bs], sbuf_side=bs)
                    for q in range(16):
                        ai_re, ai_im = cols[:, d, 7, q:q + 1], cols[:, d, 8, q:q + 1]
                        o(dv, lambda e: e.tensor_scalar(out=xw[:, 0, :], in0=xb[:, 1, q, :], scalar1=ai_im, scalar2=None, op0=ALU.mult))
                        fw.op(dv, lambda e: e.scalar_tensor_tensor(out=CT[d][:, q, 0, :], in0=xb[:, 0, q, :], scalar=ai_re, in1=xw[:, 0, :], op0=ALU.mult,
                                                                   op1=ALU.subtract), reads=[bs], writes=[btab])
                        o(dv, lambda e: e.tensor_scalar(out=xw[:, 1, :], in0=xb[:, 0, q, :], scalar1=ai_im, scalar2=-1.0, op0=ALU.mult, op1=ALU.mult))
                        o(dv, lambda e: e.tensor_scalar(out=xw[:, 0, :], in0=xb[:, 1, q, :], scalar1=ai_re, scalar2=None, op0=ALU.mult))
                        fw.op(dv, lambda e: e.tensor_tensor(out=CT[d][:, q, 1, :], in0=xw[:, 1, :], in1=xw[:, 0, :], op=ALU.subtract),
                              reads=[bs], writes=[btab])
                    if d == 0:
                        gen_table(d, 1, 1)
                    else:
                        gen_table(d, 128, -1)
                    fw.op(dv, lambda e: e.tensor_copy(out=QC[d][:], in_=tc_[:]), reads=[bs], writes=[btab])
                    fw.op(dv, lambda e: e.tensor_copy(out=QS[d][:], in_=ts_[:]), reads=[bs], writes=[btab])
                    if d == 0:
                        gen_table(d, 0, -1)
                    else:
                        gen_table(d, -127, 1)
                    nbk = 0
                    for (src, dst) in ((tc_, PCt[d]), (ts_, PSt[d])):
                        for qg in range(4):
                            bank, bbk = p.banks[nbk % 4], p.bb[nbk % 4]
                            nbk += 1
                            fw.group(fw.pe, [(lambda e, k=k: e.transpose(out=bank[:, k * 128:(k + 1) * 128], in_=src[:, qg * 4 + k, :], identity=p.ident[:]))
                                             for k in range(4)], reads=[bs, p.bconst], writes=[bbk])
                            eng = ac if qg % 2 == 0 else dv
                            if eng is ac:
                                fw.op(eng, lambda e: e.activation(out=dst[:, qg * 512:(qg + 1) * 512], in_=bank[:, :], func=AF.Copy), reads=[bbk], writes=[btab])
                            else:
                                fw.op(eng, lambda e: e.tensor_copy(out=dst[:, qg * 512:(qg + 1) * 512], in_=bank[:, :]), reads=[bbk], writes=[btab])
            if "s5dbg" in p.debug:
                for nm, tl in (("dPC", PCt[0]), ("dPS", PSt[0]), ("dQC", QC[0]), ("dQS", QS[0]), ("dRB", RB[0]), ("dCT", CT[0]),
                               ("dPC1", PCt[1]), ("dQC1", QC[1])):
                    dd = p.nc.dram_tensor(nm, list(tl.shape), tl.dtype, kind="ExternalOutput").ap()
                    fw.dma(fw.sp, dd, tl[:], reads=[btab], sbuf_side=btab)
            with fw.scope() as scp:
                ub = [fw.sbuf("ub", [128, 4, 128], BF16, scp) for _ in range(2)]
                bub = bufs(2, "ub")
                Z = [fw.sbuf("Z", [128, 2, 512], BF16, scp) for _ in range(8)]
                bZ = bufs(8, "Z")
                P4a = [fw.sbuf("P4", [128, 512], F32, scp) for _ in range(8)]
                bP4a = bufs(8, "P4")
                Hh = [fw.sbuf("Hh", [128, 2, 128], BF16, scp) for _ in range(32)]
                bH = bufs(32, "Hh")
                Pp = [fw.sbuf("Pp", [128, 4, 128], F32, scp) for _ in range(4)]
                bPp = bufs(4, "Pp")
                yst = [fw.sbuf("yst", [128, 4, 128], F32, scp) for _ in range(2)]
                byst = bufs(2, "yst")
                bcars = [[Buf("carry") for _ in range(16)] for _ in range(2)]
                uview = p.uTb.rearrange("(c q) t -> q c t", q=128)
                for d in range(2):
                    order = list(range(NB)) if d == 0 else [1, 0] + list(range(NB - 1, 1, -1))
                    U = Uf if d == 0 else Ur
                    L = 127 if d == 0 else 0
                    ydst = (p.yF if d == 0 else p.yR).rearrange("(c q) t -> q c t", q=128)
                    fw.op(dv, lambda e: e.memset(carry[:, d], 0.0), writes=bcars[d])

                    def stage_a(n):
                        blk = order[n]
                        u_, bu_ = ub[n % 2], bub[n % 2]
                        fw.dma(fw.sp, u_[:], uview[:, :, blk * 128:(blk + 1) * 128], writes=[bu_], sbuf_side=bu_)
                        for fc in range(4):
                            bre, bbre = p.banks[0], p.bb[0]
                            bim, bbim = p.banks[1], p.bb[1]
                            fw.op(fw.pe, lambda e: e.matmul(bre[:, :], lhsT=u_[:, fc, :], rhs=RB[d][:, fc, 0:512], start=True, stop=True),
                                  reads=[bu_, btab], writes=[bbre])
                            fw.op(fw.pe, lambda e: e.matmul(bim[:, :], lhsT=u_[:, fc, :], rhs=RB[d][:, fc, 512:1024], start=True, stop=True),
                                  reads=[bu_, btab], writes=[bbim])
                            pc_ = PCt[d][:, fc * 512:(fc + 1) * 512]
                            ps_ = PSt[d][:, fc * 512:(fc + 1) * 512]
                            z, bz = Z[4 * (n % 2) + fc], bZ[4 * (n % 2) + fc]
                            P4, bP4 = P4a[4 * (fc % 2):4 * (fc % 2) + 4], bP4a[4 * (fc % 2):4 * (fc % 2) + 4]
                            fw.op(dv, lambda e: e.tensor_tensor(out=P4[0][:], in0=bre[:, :], in1=pc_, op=ALU.mult), reads=[bbre, btab], writes=[bP4[0]])
                            fw.op(dv, lambda e: e.tensor_tensor(out=P4[1][:], in0=bim[:, :], in1=ps_, op=ALU.mult), reads=[bbim, btab], writes=[bP4[1]])
                            fw.op(pl, lambda e: e.tensor_tensor(out=z[:, 0, :], in0=P4[0][:], in1=P4[1][:], op=ALU.subtract), reads=[bP4[0], bP4[1]], writes=[bz])
                            fw.op(dv, lambda e: e.tensor_tensor(out=P4[2][:], in0=bim[:, :], in1=pc_, op=ALU.mult), reads=[bbim, btab], writes=[bP4[2]])
                            fw.op(dv, lambda e: e.tensor_tensor(out=P4[3][:], in0=bre[:, :], in1=ps_, op=ALU.mult), reads=[bbre, btab], writes=[bP4[3]])
                            fw.op(pl, lambda e: e.tensor_tensor(out=z[:, 1, :], in0=P4[2][:], in1=P4[3][:], op=ALU.add), reads=[bP4[2], bP4[3]], writes=[bz])

                    def stage_b(n):
                        for q in range(16):
                            fc, ql = q // 4, q % 4
                            z, bz = Z[4 * (n % 2) + fc], bZ[4 * (n % 2) + fc]
                            gb, bgb = p.banks[2 + q % 4], p.bb[2 + q % 4]
                            fw.group(fw.pe, [(lambda e, ri=ri: e.matmul(gb[:, ri * 128:(ri + 1) * 128], lhsT=z[:, ri, ql * 128:(ql + 1) * 128], rhs=U[:],
                                                                         start=True, stop=True)) for ri in range(2)],
                                     reads=[bz, bs], writes=[bgb])
                            pp, bpp = Pp[q % 4], bPp[q % 4]
                            cre, cim = carry[:, d, q, 0:1], carry[:, d, q, 1:2]
                            gre, gim = gb[:, 0:128], gb[:, 128:256]
                            qc_, qs_ = QC[d][:, q, :], QS[d][:, q, :]
                            bcar = bcars[d][q]
                            rd = [bgb, btab, bcar]
                            fw.op(dv, lambda e: e.scalar_tensor_tensor(out=pp[:, 0, :], in0=gre, scalar=cre, in1=qc_, op0=ALU.add, op1=ALU.mult), reads=rd, writes=[bpp])
                            fw.op(dv, lambda e: e.scalar_tensor_tensor(out=pp[:, 1, :], in0=gim, scalar=cim, in1=qs_, op0=ALU.add, op1=ALU.mult), reads=rd, writes=[bpp])
                            fw.op(dv, lambda e: e.scalar_tensor_tensor(out=pp[:, 2, :], in0=gim, scalar=cim, in1=qc_, op0=ALU.add, op1=ALU.mult), reads=rd, writes=[bpp])
                            fw.op(dv, lambda e: e.scalar_tensor_tensor(out=pp[:, 3, :], in0=gre, scalar=cre, in1=qs_, op0=ALU.add, op1=ALU.mult), reads=rd, writes=[bpp])
                            h, bh = Hh[16 * (n % 2) + q], bH[16 * (n % 2) + q]
                            fw.op(pl, lambda e: e.tensor_tensor(out=h[:, 0, :], in0=pp[:, 0, :], in1=pp[:, 1, :], op=ALU.subtract), reads=[bpp], writes=[bh])
                            fw.op(pl, lambda e: e.tensor_tensor(out=h[:, 1, :], in0=pp[:, 2, :], in1=pp[:, 3, :], op=ALU.add), reads=[bpp], writes=[bh])
                            fw.op(dv, lambda e: e.tensor_tensor(out=cre, in0=pp[:, 0, L:L + 1], in1=pp[:, 1, L:L + 1], op=ALU.subtract), reads=[bpp, bcar], writes=[bcar])
                            fw.op(dv, lambda e: e.tensor_tensor(out=cim, in0=pp[:, 2, L:L + 1], in1=pp[:, 3, L:L + 1], op=ALU.add), reads=[bpp, bcar], writes=[bcar])

                    def stage_c(n):
                        blk = order[n]
                        ybank, bybank = p.banks[6 + n % 2], p.bb[6 + n % 2]
                        ys, bys = yst[n % 2], byst[n % 2]
                        for fc in range(4):
                            fw.group(fw.pe, [(lambda e, ql=ql, ri=ri: e.matmul(ybank[:, fc * 128:(fc + 1) * 128], lhsT=CT[d][:, 4 * fc + ql, ri, :],
                                                                               rhs=Hh[16 * (n % 2) + 4 * fc + ql][:, ri, :], start=(ql == 0 and ri == 0),
                                                                               stop=(ql == 3 and ri == 1))) for ql in range(4) for ri in range(2)],
                                     reads=[btab] + [bH[16 * (n % 2) + 4 * fc + ql] for ql in range(4)], writes=[bybank])
                        fw.op(ac, lambda e: e.activation(out=ys[:], in_=ybank[:, :].rearrange("p (c t) -> p c t", c=4), func=AF.Copy), reads=[bybank], writes=[bys])
                        fw.dma(fw.act, ydst[:, :, blk * 128:(blk + 1) * 128], ys[:], reads=[bys], sbuf_side=bys)

                    stage_a(0)
                    for n in range(len(order)):
                        if n + 1 < len(order):
                            stage_a(n + 1)
                        stage_b(n)
                        stage_c(n)
            fw.barrier()
            gw = fw.sbuf("gw", [128, 4, 512], BF16, sc)
            bgw = Buf("gw")
            fw.dma(fw.sp, gw[:], p.gluS[i], reads=[p.wbuf["glu%d" % i]], writes=[bgw], sbuf_side=bgw)
            A_ = [fw.sbuf("tA", [128, 4, 512], F32, sc) for _ in range(3)]
            bA = bufs(3, "tA")
            y3 = fw.sbuf("y3", [128, 4, 512], F32, sc)
            y3b = fw.sbuf("y3b", [128, 4, 512], BF16, sc)
            so = fw.sbuf("so", [128, 4, 512], BF16, sc)
            by3, by3b, bso = Buf("y3"), Buf("y3b"), Buf("so")
            for (t0, N, is_ctx) in TILES:
                vF = p.yF.rearrange("(c q) t -> q c t", q=128)[:, :, t0:t0 + N]
                vR = p.yR.rearrange("(c q) t -> q c t", q=128)[:, :, t0:t0 + N]
                vU = p.uT.rearrange("(c q) t -> q c t", q=128)[:, :, t0:t0 + N]
                fw.dma(fw.sp, A_[0][:, :, :N], vF, writes=[bA[0]], sbuf_side=bA[0])
                fw.dma(fw.sp, A_[1][:, :, :N], vR, writes=[bA[1]], sbuf_side=bA[1])
                fw.dma(fw.sp, A_[2][:, :, :N], vU, writes=[bA[2]], sbuf_side=bA[2])
                fw.op(pl, lambda e: e.tensor_tensor(out=A_[0][:, :, :N], in0=A_[0][:, :, :N], in1=A_[1][:, :, :N], op=ALU.add), reads=[bA[0], bA[1]], writes=[bA[0]])
                for fc in range(4):
                    fw.op(dv, lambda e: e.scalar_tensor_tensor(out=A_[0][:, fc, :N], in0=A_[2][:, fc, :N], scalar=p.s5d[:, i, 0, fc:fc + 1],
                                                               in1=A_[0][:, fc, :N], op0=ALU.mult, op1=ALU.add), reads=[bA[0], bA[2], p.bsmall], writes=[bA[0]])
                y2 = A_[0]
                fw.op(ac, lambda e: e.activation(out=A_[1][:, :, :N], in_=y2[:, :, :N], func=AF.Square), reads=[bA[0]], writes=[bA[1]])
                fw.op(dv, lambda e: e.tensor_scalar(out=A_[1][:, :, :N], in0=A_[1][:, :, :N], scalar1=0.044715, scalar2=1.0, op0=ALU.mult, op1=ALU.add),
                      reads=[bA[1]], writes=[bA[1]])
                fw.op(dv, lambda e: e.tensor_tensor(out=A_[1][:, :, :N], in0=A_[1][:, :, :N], in1=y2[:, :, :N], op=ALU.mult), reads=[bA[0], bA[1]], writes=[bA[1]])
                fw.op(ac, lambda e: e.activation(out=A_[1][:, :, :N], in_=A_[1][:, :, :N], func=AF.Sigmoid, scale=2.0 * math.sqrt(2.0 / PI)),
                      reads=[bA[1]], writes=[bA[1]])
                fw.op(dv, lambda e: e.tensor_tensor(out=y3[:, :, :N], in0=A_[1][:, :, :N], in1=y2[:, :, :N], op=ALU.mult), reads=[bA[0], bA[1]], writes=[by3])
                fw.op(pl, lambda e: e.tensor_copy(out=y3b[:, :, :N], in_=y3[:, :, :N]), reads=[by3], writes=[by3b])
                for fo in range(4):
                    bank, bbk = p.banks[fo % 2], p.bb[fo % 2]
                    fw.group(fw.pe, [(lambda e, fi=fi: e.matmul(bank[:, :N], lhsT=gw[:, fi, fo * 128:(fo + 1) * 128], rhs=y3b[:, fi, :N],
                                                                start=(fi == 0), stop=(fi == 3))) for fi in range(4)],
                             reads=[bgw, by3b], writes=[bbk])
                    fw.op(ac, lambda e: e.activation(out=A_[2][:, fo, :N], in_=bank[:, :N], func=AF.Sigmoid, bias=p.s5d[:, i, 1, fo:fo + 1]),
                          reads=[bbk, p.bsmall], writes=[bA[2]])
                fw.op(dv, lambda e: e.tensor_tensor(out=so[:, :, :N], in0=A_[2][:, :, :N], in1=y3[:, :, :N], op=ALU.mult), reads=[bA[2], by3], writes=[bso])
                fw.dma(fw.act, p.s5T.rearrange("(c q) t -> q c t", q=128)[:, :, t0:t0 + N], so[:, :, :N], reads=[bso], sbuf_side=bso)

    def lam_prep(self, i):
        p, fw = self, self.fw
        lam_init = 0.8 - 0.6 * math.exp(-0.3 * i)
        with fw.scope() as sc:
            dl = fw.sbuf("dl", [1, 256], F32, sc)
            pr = fw.sbuf("pr", [1, 128], F32, sc)
            sm = fw.sbuf("sm", [1, 4], F32, sc)
            one1 = fw.sbuf("one1", [1, 128], F32, sc)
            b = Buf("lam")
            fw.dma(fw.sp, dl[:], p.dlam[i], writes=[b], sbuf_side=b)
            fw.op(fw.dve, lambda e: e.memset(one1[:], 1.0), reads=[b], writes=[b])
            dl4 = dl[:, :].rearrange("p (k c) -> p k c", k=4)
            fw.op(fw.dve, lambda e: e.tensor_tensor(out=pr[:, 0:64], in0=dl4[:, 0, :], in1=dl4[:, 1, :], op=ALU.mult), reads=[b], writes=[b])
            fw.op(fw.dve, lambda e: e.tensor_tensor(out=pr[:, 64:128], in0=dl4[:, 2, :], in1=dl4[:, 3, :], op=ALU.mult), reads=[b], writes=[b])
            fw.op(fw.dve, lambda e: e.reduce_sum(out=sm[:, 0:2], in_=pr[:, :].rearrange("p (k c) -> p k c", k=2), axis=mybir.AxisListType.X),
                  reads=[b], writes=[b])
            fw.op(fw.act, lambda e: e.activation(out=sm[:, 0:2], in_=sm[:, 0:2], func=AF.Exp), reads=[b], writes=[b])
            fw.op(fw.dve, lambda e: e.tensor_tensor(out=sm[:, 2:3], in0=sm[:, 1:2], in1=sm[:, 0:1], op=ALU.subtract), reads=[b], writes=[b])
            fw.op(fw.dve, lambda e: e.tensor_scalar(out=sm[:, 2:3], in0=sm[:, 2:3], scalar1=-lam_init, scalar2=None, op0=ALU.add), reads=[b], writes=[b])
            bank, bbk = p.banks[0], p.bb[0]
            fw.op(fw.pe, lambda e: e.matmul(bank[:, 0:1], lhsT=one1[:], rhs=sm[:, 2:3], start=True, stop=True), reads=[b], writes=[bbk])
            fw.op(fw.dve, lambda e: e.tensor_copy(out=p.lamc[:, 0:1], in_=bank[:, 0:1]), reads=[bbk], writes=[p.blamc])
            fw.op(fw.dve, lambda e: e.tensor_scalar(out=p.lamc[:, 1:2], in0=p.hg[:, i, 0:1], scalar1=1.0 - lam_init, scalar2=None, op0=ALU.mult),
                  reads=[p.bsmall, p.blamc], writes=[p.blamc])

    def pass_b(self, i, last):
        p, fw = self, self.fw
        p.lam_prep(i)
        with fw.scope() as sc:
            p.pass_alloc(sc)
            ws = p.ws
            mix = fw.sbuf("mix", [128, DC, 512], BF16, sc)
            bmix = bufs(DC, "mix")
            tiles = [t for t in TILES if not (last and t[2])]
            wbo = p.wbuf["wout%d" % i]
            for _ in tiles:
                for dg in range(4):
                    ws.enqueue(lambda t, b, dg=dg: fw.dma(fw.sp, t[:, :].rearrange("p (c kc n) -> p c kc n", c=4, kc=DC), p.woutS[i][dg],
                                                           reads=[wbo], writes=[b], sbuf_side=b))
                p.enqueue_ffn(i, 1)
            for (t0, N, is_ctx) in tiles:
                c = 1 if is_ctx else 0
                fw.dma(fw.sp, p.xt[:, :, :N], p.xT.rearrange("(kc q) t -> q kc t", q=128)[:, :, t0:t0 + N], writes=p.bxt, sbuf_side=p.bxt[0])
                fw.dma(fw.sp, mix[:, 0:4, :N], p.s5T.rearrange("(c q) t -> q c t", q=128)[:, :, t0:t0 + N], writes=bmix[0:4], sbuf_side=bmix[0])
                with fw.scope() as sc2:
                    p.attention(i, t0, N, is_ctx, mix, bmix, sc2)
                for dg in range(4):
                    t, b = ws.take()
                    tv = t[:, :].rearrange("p (c kc n) -> p c kc n", c=4, kc=DC)
                    for ci in range(4):
                        dc = dg * 4 + ci
                        yb, byb = p.banks[4 + dc % 2], p.bb[4 + dc % 2]
                        fw.group(fw.pe, [(lambda e, kc=kc: e.matmul(yb[:, :N], lhsT=tv[:, ci, kc, :], rhs=mix[:, kc, :N], start=(kc == 0), stop=(kc == DC - 1)))
                                         for kc in range(DC)], reads=[b] + bmix, writes=[byb])
                        fw.op(fw.dve, lambda e: e.scalar_tensor_tensor(out=p.xt[:, dc, :N], in0=yb[:, :N], scalar=p.Gcol[:, 1, dc, c:c + 1],
                                                                      in1=p.xt[:, dc, :N], op0=ALU.mult, op1=ALU.add),
                              reads=[byb, p.bcols, p.bxt[dc]], writes=[p.bxt[dc]])
                with fw.scope() as sc2:
                    act = fw.sbuf("act", [128, FC, 512], BF16, sc2)
                    bact = bufs(FC, "act")
                    p.rmsnorm_mod(i, 2, c, N)
                    p.ffn(i, 1, 2, c, N, act, bact)
                if not last:
                    fw.dma(fw.act, p.xT.rearrange("(kc q) t -> q kc t", q=128)[:, :, t0:t0 + N], p.xt[:, :, :N], reads=p.bxt, sbuf_side=p.bxt[0])
                else:
                    with fw.scope() as sc2:
                        p.final_out(t0, N, sc2)

    def final_out(self, t0, N, sc):
        p, fw = self, self.fw
        ssb, bss = p.banks[6], p.bb[6]
        for kc in range(DC):
            s_, bs = p.sq[kc % 2], p.bsq[kc % 2]
            fw.op(fw.act, lambda e: e.activation(out=s_[:, :N], in_=p.xt[:, kc, :N], func=AF.Square), reads=[p.bxt[kc]], writes=[bs])
            fw.op(fw.pe, lambda e: e.matmul(ssb[:, :N], lhsT=p.ones[:], rhs=s_[:, :N], start=(kc == 0), stop=(kc == DC - 1)),
                  reads=[bs, p.bconst], writes=[bss])
        fw.op(fw.act, lambda e: e.activation(out=p.rs[:, :N], in_=ssb[:, :N], func=AF.Sqrt, scale=1.0 / D, bias=p.epsc[:]),
              reads=[bss, p.bconst], writes=[p.brs])
        fw.op(fw.dve, lambda e: e.reciprocal(out=p.rs[:, :N], in_=p.rs[:, :N]), reads=[p.brs], writes=[p.brs])
        for kc in range(DC):
            fw.op(fw.dve, lambda e: e.scalar_tensor_tensor(out=p.xt[:, kc, :N], in0=p.xt[:, kc, :N], scalar=p.fgT[:, kc:kc + 1], in1=p.rs[:, :N],
                                                          op0=ALU.mult, op1=ALU.mult), reads=[p.bxt[kc], p.brs, p.bsmall], writes=[p.bxt[kc]])
        otok = [fw.sbuf("otok", [128, D], F32, sc) for _ in range(2)]
        bot = bufs(2, "otok")
        nb = 0
        for blk in range(N // 128):
            ot, bo = otok[blk % 2], bot[blk % 2]
            for dg in range(4):
                bank, bbk = p.banks[nb % 4], p.bb[nb % 4]
                nb += 1
                fw.group(fw.pe, [(lambda e, q=q: e.transpose(out=bank[:, q * 128:(q + 1) * 128], in_=p.xt[:, dg * 4 + q, blk * 128:(blk + 1) * 128],
                                                             identity=p.ident[:])) for q in range(4)],
                         reads=p.bxt[dg * 4:(dg + 1) * 4] + [p.bconst], writes=[bbk])
                if dg % 2 == 0:
                    fw.op(fw.act, lambda e: e.activation(out=ot[:, dg * 512:(dg + 1) * 512], in_=bank[:, :], func=AF.Copy), reads=[bbk], writes=[bo])
                else:
                    fw.op(fw.dve, lambda e: e.tensor_copy(out=ot[:, dg * 512:(dg + 1) * 512], in_=bank[:, :]), reads=[bbk], writes=[bo])
            r0 = t0 - CTX + blk * 128
            fw.dma(fw.act, p.out[r0:r0 + 128, :], ot[:], reads=[bo], sbuf_side=bo)

    def attention(self, i, t0, N, is_ctx, mix, bmix, sc):
        p, fw = self, self.fw
        nkb = 2 if is_ctx else NB
        nk = nkb * 128
        qd = fw.sbuf("qd", [128, 4, 512], BF16, sc)
        qg = fw.sbuf("qg", [128, 8, 512], BF16, sc)
        bq = Buf("q")
        fw.dma(fw.sp, qd[:, :, :N], p.qd.rearrange("(c q) t -> q c t", q=128)[:, :, t0:t0 + N], writes=[bq], sbuf_side=bq)
        bq2 = Buf("q2")
        fw.dma(fw.sp, qg[:, :, :N], p.qg.rearrange("(c q) t -> q c t", q=128)[:, :, t0:t0 + N], writes=[bq2], sbuf_side=bq2)
        KT = [fw.sbuf("KT", [128, S], BF16, sc) for _ in range(2)]
        VV = [fw.sbuf("VV", [128, NB, 128], BF16, sc) for _ in range(2)]
        bkv = bufs(2, "kv")
        Pt = [fw.sbuf("Pt", [128, 2, 512], BF16, sc) for _ in range(3)]
        bPt = bufs(3, "Pt")
        Ps2 = [fw.sbuf("Ps2", [128, 512], BF16, sc) for _ in range(2)]
        bPs2 = bufs(2, "Ps2")
        rc = [fw.sbuf("rc", [128, 512], F32, sc) for _ in range(2)]
        brc = bufs(2, "rc")
        units = [("d", h) for h in range(4)] + [("g", kv) for kv in range(2)]

        def load_unit(u):
            kind, idx = units[u]
            k, b = u % 2, bkv[u % 2]
            ksrc = p.kd if kind == "d" else p.kg
            vsrc = p.vd if kind == "d" else p.vg
            fw.dma(fw.sp, KT[k][:, 0:nk], ksrc[idx * 128:(idx + 1) * 128, 0:nk], writes=[b], sbuf_side=b)
            fw.dma(fw.sp, VV[k][:, 0:nkb, :], vsrc[idx, :, 0:nkb, :], writes=[b], sbuf_side=b)
        npt = [0]

        def softmax_av(kt, vv, bkvu, qap, pbase, K, scale, obank, bob, dbank, bdb, readsq):
            npair = nkb // 2

            def s_pair(pi):
                sp, bsp0, bsp1 = p.bpair[pi % 2], p.bb[2 * (pi % 2)], p.bb[2 * (pi % 2) + 1]
                fw.group(fw.pe, [(lambda e, j=j: e.matmul(sp[:, j, :N], lhsT=kt[pbase:pbase + K, (2 * pi + j) * 128:(2 * pi + j + 1) * 128], rhs=qap,
                                                          start=True, stop=True)) for j in range(2)],
                         reads=[bkvu, readsq], writes=[bsp0, bsp1])
            s_pair(0)
            for pi in range(npair):
                if pi + 1 < npair:
                    s_pair(pi + 1)
                sp, bsp0, bsp1 = p.bpair[pi % 2], p.bb[2 * (pi % 2)], p.bb[2 * (pi % 2) + 1]
                pt, bpt = Pt[npt[0] % 3], bPt[npt[0] % 3]
                npt[0] += 1
                fw.op(fw.act, lambda e: e.activation(out=pt[:, :, :N], in_=sp[:, :, :N], func=AF.Exp, scale=scale), reads=[bsp0, bsp1], writes=[bpt])
                fw.group(fw.pe, [(lambda e, j=j: e.matmul(obank[:, :N], lhsT=vv[:, 2 * pi + j, :], rhs=pt[:, j, :N], start=(pi == 0 and j == 0),
                                                          stop=(pi == npair - 1 and j == 1))) for j in range(2)],
                         reads=[bkvu, bpt], writes=[bob])
                ps2, bps2 = Ps2[pi % 2], bPs2[pi % 2]
                fw.op(fw.dve, lambda e: e.tensor_tensor(out=ps2[:, :N], in0=pt[:, 0, :N], in1=pt[:, 1, :N], op=ALU.add), reads=[bpt], writes=[bps2])
                fw.op(fw.pe, lambda e: e.matmul(dbank[:, :N], lhsT=p.ones[:], rhs=ps2[:, :N], start=(pi == 0), stop=(pi == npair - 1)),
                      reads=[bps2, p.bconst], writes=[bdb])
        load_unit(0)
        for u, (kind, idx) in enumerate(units):
            if u + 1 < len(units):
                load_unit(u + 1)
            kt, vv, bkvu = KT[u % 2], VV[u % 2], bkv[u % 2]
            if kind == "d":
                h = idx
                for m in range(2):
                    softmax_av(kt, vv, bkvu, qd[64 * m:64 * m + 64, h, :N], 64 * m, 64, 0.125,
                               p.banks[4 + m], p.bb[4 + m], p.banks[6 + m], p.bb[6 + m], bq)
                t1, b1 = p.tf[0], p.btf[0]
                t2, b2 = p.tf[1], p.btf[1]
                for m, (tt, bt) in enumerate(((t1, b1), (t2, b2))):
                    fw.op(fw.dve, lambda e: e.reciprocal(out=rc[m][:, :N], in_=p.banks[6 + m][:, :N]), reads=[p.bb[6 + m]], writes=[brc[m]])
                    fw.op(fw.dve, lambda e: e.tensor_tensor(out=tt[:, :N], in0=p.banks[4 + m][:, :N], in1=rc[m][:, :N], op=ALU.mult),
                          reads=[p.bb[4 + m], brc[m]], writes=[bt])
                fw.op(fw.dve, lambda e: e.scalar_tensor_tensor(out=t1[:, :N], in0=t2[:, :N], scalar=p.lamc[:, 0:1], in1=t1[:, :N], op0=ALU.mult, op1=ALU.add),
                      reads=[b1, b2, p.blamc], writes=[b1])
                s_, bs = p.sq[h % 2], p.bsq[h % 2]
                ssb, bss = p.banks[0], p.bb[0]
                fw.op(fw.act, lambda e: e.activation(out=s_[:, :N], in_=t1[:, :N], func=AF.Square), reads=[b1], writes=[bs])
                fw.op(fw.pe, lambda e: e.matmul(ssb[:, :N], lhsT=p.ones[:], rhs=s_[:, :N], start=True, stop=True), reads=[bs, p.bconst], writes=[bss])
                fw.op(fw.act, lambda e: e.activation(out=p.rs[:, :N], in_=ssb[:, :N], func=AF.Sqrt, scale=1.0 / 128, bias=p.epsc[:]),
                      reads=[bss, p.bconst], writes=[p.brs])
                fw.op(fw.dve, lambda e: e.reciprocal(out=p.rs[:, :N], in_=p.rs[:, :N]), reads=[p.brs], writes=[p.brs])
                fw.op(fw.dve, lambda e: e.scalar_tensor_tensor(out=mix[:, 4 + h, :N], in0=t1[:, :N], scalar=p.lamc[:, 1:2], in1=p.rs[:, :N],
                                                              op0=ALU.mult, op1=ALU.mult), reads=[b1, p.brs, p.blamc], writes=[bmix[4 + h]])
            else:
                kv = idx
                for r in range(4):
                    h = kv * 4 + r
                    ob, bob = p.banks[4 + r % 2], p.bb[4 + r % 2]
                    db, bdb = p.banks[6 + r % 2], p.bb[6 + r % 2]
                    softmax_av(kt, vv, bkvu, qg[:, h, :N], 0, 128, 128.0 ** -0.5, ob, bob, db, bdb, bq2)
                    fw.op(fw.dve, lambda e: e.reciprocal(out=rc[r % 2][:, :N], in_=db[:, :N]), reads=[bdb], writes=[brc[r % 2]])
                    fw.op(fw.dve, lambda e: e.tensor_tensor(out=mix[:, 8 + h, :N], in0=ob[:, :N], in1=rc[r % 2][:, :N], op=ALU.mult),
                          reads=[bob, brc[r % 2]], writes=[bmix[8 + h]])

    def finish(self):
        self.fw.barrier()


def build(debug=None, upto="all", conv="all", ntiles=9, stop=None):
    p = Prog(debug)
    p.ntiles = ntiles
    p.stop = stop
    p.declare()
    fw = p.fw
    with fw.stack:
        p.setup()
        p.conv_filter = None if conv == "all" else conv.split(",")
        p.rope_tables()
        p.convert_layer(0)
        p.convert_layer(1)
        for i in range(2):
            if upto == "rope":
                break
            p.modulation(i)
            if upto == "mod":
                break
            p.pass_a(i)
            if upto == "A":
                break
            p.s5(i)
            if upto == "S5":
                break
            p.pass_b(i, i == 1)
            if upto == "B":
                break
        p.finish()
    return p


def host_inputs(inp):
    f = np.float32
    g = {k: np.asarray(v) for k, v in inp.items()}
    common = {}
    common["ada_w"] = np.ascontiguousarray(g["ada_w"], f)
    common["ada_bT"] = np.ascontiguousarray(g["ada_b"].reshape(2, 144, 128).transpose(0, 2, 1), f)
    common["norm_gT"] = np.ascontiguousarray(g["norm_g"].reshape(2, 3, DC, 128).transpose(0, 3, 1, 2), f)
    common["final_gT"] = np.ascontiguousarray(g["final_g"].reshape(DC, 128).T, f)
    common["ffn_w_gate"] = np.ascontiguousarray(g["ffn_w_gate"], f)
    common["ffn_w_up"] = np.ascontiguousarray(g["ffn_w_up"], f)
    common["ffn_w_down"] = np.ascontiguousarray(g["ffn_w_down"], f)
    w_in = np.ascontiguousarray(g["w_in"], f)
    common["w_in"] = w_in
    i64 = np.arange(512) ^ 16
    i128 = np.arange(1024) ^ 32
    k128 = np.arange(256) ^ 32
    common["w_in_rot"] = np.ascontiguousarray(np.concatenate([w_in[:, :, 512 + i64], w_in[:, :, 1024 + i64], w_in[:, :, 2048 + i128],
                                                              w_in[:, :, 3072 + k128]], axis=2), f)
    common["w_out"] = np.ascontiguousarray(g["w_out"], f)
    lam = np.stack([g["s5_lam_re"], g["s5_lam_im"], np.broadcast_to(g["s5_log_dt"][..., None], g["s5_lam_re"].shape)], axis=0)
    lamT = lam.reshape(3, 2, 2, 16, 128).transpose(1, 2, 4, 0, 3)
    common["lamT"] = np.ascontiguousarray(lamT, f)
    XB = np.zeros((2, 2, 2, 16, 128, 128), f)
    YC = np.zeros((2, 2, 2, 16, 128, 128), f)
    for ri, (bsrc, csrc) in enumerate([(g["s5_b_re"], g["s5_c_re"]), (g["s5_b_im"], g["s5_c_im"])]):
        for q in range(16):
            for e in range(2):
                gg = 2 * q + e
                gl = gg % 8
                XB[:, :, ri, q, e * 64:(e + 1) * 64, gl * 16:(gl + 1) * 16] = bsrc[:, :, gg]
                YC[:, :, ri, q, e * 64:(e + 1) * 64, gl * 16:(gl + 1) * 16] = csrc[:, :, gg].transpose(0, 1, 3, 2)
    common["XB"] = XB
    common["YC"] = YC
    common["s5dT"] = np.ascontiguousarray(np.stack([g["s5_d"].reshape(2, 4, 128), g["s5_glu_b"].reshape(2, 4, 128)], axis=1).transpose(0, 3, 1, 2), f)
    common["s5_glu_w"] = np.ascontiguousarray(g["s5_glu_w"], f)
    common["dlam"] = np.ascontiguousarray(g["diff_lam"].reshape(2, 1, 256), f)
    p32 = np.arange(128) ^ 32
    common["hgT"] = np.ascontiguousarray(np.stack([g["diff_subln_g"], g["gqa_q_g"], g["gqa_q_g"][:, p32], g["gqa_k_g"], g["gqa_k_g"][:, p32]], axis=2), f)
    maps = []
    for b in range(NCORES):
        m = dict(common)
        m["x"] = np.ascontiguousarray(g["x"][b], f)
        m["ctx"] = np.ascontiguousarray(g["ctx"][b], f)
        m["cT"] = np.ascontiguousarray(np.stack([g["c"][b].reshape(DC, 128).T, g["c_ctx"].reshape(DC, 128).T], axis=2), f)
        maps.append(m)
    idle = dict(common)
    for k in ("ada_w", "ffn_w_gate", "ffn_w_up", "ffn_w_down", "w_in", "w_in_rot", "w_out", "s5_glu_w", "XB", "YC"):
        idle[k] = np.zeros_like(common[k])
    idle["x"] = np.zeros_like(maps[0]["x"])
    idle["ctx"] = np.zeros_like(maps[0]["ctx"])
    idle["cT"] = np.zeros_like(maps[0]["cT"])
    full = []
    for b in range(NCORES):
        full.append(maps[b])
        full.append(idle)
    return full


_CACHE = {}


def kernel(**inputs):
    maps = host_inputs(inputs)
    if "p" not in _CACHE:
        _CACHE["p"] = build()
    p = _CACHE["p"]
    res = run_bass_kernel_spmd(p.nc, maps, core_ids=list(range(2 * NCORES)))
    return np.stack([np.asarray(res.results[2 * b]["out"], np.float32) for b in range(NCORES)], axis=0)
```

```python
import math
import numpy as np
import concourse.bass as bass
import concourse.mybir as mybir
from concourse.bass_utils import run_bass_kernel_spmd
from contextlib import ExitStack

F32 = mybir.dt.float32
BF16 = mybir.dt.bfloat16
I32 = mybir.dt.int32
ALU = mybir.AluOpType
AF = mybir.ActivationFunctionType

D = 2048
DC = 16
FF = 5632
FC = 44
LAT = 4096
CTX = 256
S = 4352
NB = 34
NCORES = 4
TILES = [(0, 256, True)] + [(256 + 512 * i, 512, False) for i in range(8)]
EPS = 1e-6
PI = math.pi


class Eng:
    def __init__(self, fw, name, h, is_pe=False):
        self.fw, self.name, self.h, self.is_pe = fw, name, h, is_pe
        self.nsem = 0
        self.newsem()
        self.seen = {}

    def newsem(self):
        self.sem = self.fw.stack.enter_context(self.fw.nc.semaphore("s_%s%d" % (self.name, self.nsem)))
        self.nsem += 1
        self.seq = 0


class Buf:
    __slots__ = ("name", "w", "r", "dsem", "dval", "excl")

    def __init__(self, name="", excl=False):
        self.name = name
        self.excl = excl
        self.w = None
        self.r = {}
        self.dsem = None
        self.dval = 0


def bufs(n, name=""):
    return [Buf(name + str(i)) for i in range(n)]


class FW:
    def __init__(self, nc):
        self.nc = nc
        self.stack = ExitStack()
        self.pe = Eng(self, "pe", nc.tensor, True)
        self.act = Eng(self, "act", nc.scalar)
        self.dve = Eng(self, "dve", nc.vector)
        self.pool = Eng(self, "pool", nc.gpsimd)
        self.sp = Eng(self, "sp", nc.sync)
        self.engs = [self.pe, self.act, self.dve, self.pool, self.sp]
        self.inflight = []
        self.nd = 0
        self.uid = 0
        self.sem_pool = []
        self.scopes = []

    def push_scope(self):
        self.scopes.append([])

    def pop_scope(self):
        for b in self.scopes.pop():
            if b.dsem is not None:
                self.sem_pool.append([b.dsem, b.dval])
                b.dsem = None

    def scope(self):
        return _Scope(self)

    def name(self, s):
        self.uid += 1
        return "%s_%d" % (s, self.uid)

    def sbuf(self, name, shape, dt, stack=None):
        return (stack or self.stack).enter_context(self.nc.sbuf_tensor(self.name(name), shape, dt))

    def psum(self, name, shape, dt):
        return self.stack.enter_context(self.nc.psum_tensor(self.name(name), shape, dt))

    def _wait(self, eng, dep):
        sem, val, ename = dep
        key = id(sem)
        if eng.seen.get(key, 0) >= val:
            return
        if eng.is_pe and ename == "pe":
            return
        eng.h.wait_ge(sem, val)
        eng.seen[key] = val

    def _deps(self, eng, reads, writes):
        for b in reads:
            if b.w is not None:
                self._wait(eng, b.w)
            if b.excl:
                for d in b.r.values():
                    self._wait(eng, d)
        for b in writes:
            if b.w is not None and b.w[2] != eng.name:
                self._wait(eng, b.w)
            for d in b.r.values():
                if d[2] != eng.name:
                    self._wait(eng, d)

    def _mark(self, d, key, reads, writes):
        for b in writes:
            b.w = d
            b.r = {}
        for b in reads:
            if b not in writes:
                b.r[key] = d

    def op(self, eng, fn, reads=(), writes=()):
        self._deps(eng, reads, writes)
        if eng.seq >= 30000:
            eng.newsem()
        ins = fn(eng.h)
        eng.seq += 1
        ins.then_inc(eng.sem, 1)
        self._mark((eng.sem, eng.seq, eng.name), eng.name, reads, writes)

    def group(self, eng, fns, reads=(), writes=()):
        self._deps(eng, reads, writes)
        if eng.seq >= 30000:
            eng.newsem()
        ins = None
        for f in fns:
            ins = f(eng.h)
        eng.seq += 1
        ins.then_inc(eng.sem, 1)
        self._mark((eng.sem, eng.seq, eng.name), eng.name, reads, writes)

    def dma(self, eng, out, in_, reads=(), writes=(), sbuf_side=None, track=True, **kw):
        self._deps(eng, reads, writes)
        ins = eng.h.dma_start(out=out, in_=in_, **kw)
        tgt = sbuf_side
        if tgt.dsem is None:
            if self.sem_pool and self.sem_pool[0][1] < 20000:
                tgt.dsem, tgt.dval = self.sem_pool.pop(0)
            else:
                tgt.dsem = self.stack.enter_context(self.nc.semaphore("d%d" % self.nd))
                self.nd += 1
                tgt.dval = 0
            if self.scopes:
                self.scopes[-1].append(tgt)
        tgt.dval += 16
        ins.then_inc(tgt.dsem, 16)
        d = (tgt.dsem, tgt.dval, "dma")
        self._mark(d, "dma%d" % id(tgt), reads, writes)
        if track:
            self.inflight.append(d)
        return d

    def barrier(self):
        deps = [(e.sem, e.seq, e.name) for e in self.engs if e.seq > 0] + self.inflight
        for e in self.engs:
            for d in deps:
                if d[2] != e.name:
                    self._wait(e, d)
        self.inflight = []


class _Scope:
    def __init__(self, fw):
        self.fw = fw
        self.es = ExitStack()

    def __enter__(self):
        self.fw.push_scope()
        self.es.__enter__()
        return self.es

    def __exit__(self, *a):
        if a[0] is None:
            self.fw.barrier()
            self.fw.pop_scope()
        return self.es.__exit__(*a)


class WStream:
    SLOT = 8192

    def __init__(self, fw, nslots, sc=None):
        self.fw = fw
        self.t = [fw.sbuf("ring", [128, self.SLOT], BF16, sc) for _ in range(nslots)]
        self.b = bufs(nslots, "ring")
        self.q = []
        self.issued = 0
        self.taken = 0

    def enqueue(self, fn):
        self.q.append(fn)

    def take(self):
        n = len(self.t)
        lim = min(self.taken + n - 1, len(self.q))
        while self.issued < lim:
            k = self.issued % n
            self.q[self.issued](self.t[k], self.b[k])
            self.issued += 1
        k = self.taken % n
        assert self.taken < self.issued
        self.taken += 1
        return self.t[k], self.b[k]


class Prog:
    def __init__(self, debug=None):
        self.debug = debug or ()
        self.nc = nc = bass.Bass("TRN2", target_bir_lowering=False)
        self.fw = fw = FW(nc)
        self.ext_in = {}
        self.ext_out = {}

    def din(self, name, shape, dt=F32):
        t = self.nc.dram_tensor(name, list(shape), dt, kind="ExternalInput").ap()
        self.ext_in[name] = t
        return t

    def dscratch(self, name, shape, dt):
        kind = "ExternalOutput" if name in self.debug else "Internal"
        t = self.nc.dram_tensor(name, list(shape), dt, kind=kind).ap()
        return t

    def declare(self):
        p = self
        p.x = p.din("x", [LAT, D])
        p.ctx = p.din("ctx", [CTX, D])
        p.cT = p.din("cT", [128, DC, 2])
        p.ada_w = p.din("ada_w", [2, D, 9 * D])
        p.ada_bT = p.din("ada_bT", [2, 128, 144])
        p.norm_gT = p.din("norm_gT", [2, 128, 3, DC])
        p.final_gT = p.din("final_gT", [128, DC])
        p.w_gate = p.din("ffn_w_gate", [2, 2, D, FF])
        p.w_up = p.din("ffn_w_up", [2, 2, D, FF])
        p.w_down = p.din("ffn_w_down", [2, 2, FF, D])
        p.w_in = p.din("w_in", [2, D, 3584])
        p.w_in_rot = p.din("w_in_rot", [2, D, 2304])
        p.w_out = p.din("w_out", [2, D, D])
        p.lamT = p.din("lamT", [2, 2, 128, 3, 16])
        p.XB = p.din("XB", [2, 2, 2, 16, 128, 128])
        p.YC = p.din("YC", [2, 2, 2, 16, 128, 128])
        p.s5dT = p.din("s5dT", [2, 128, 2, 4])
        p.glu_w = p.din("s5_glu_w", [2, 512, 512])
        p.dlam = p.din("dlam", [2, 1, 256])
        p.hgT = p.din("hgT", [2, 128, 5])
        p.out = p.nc.dram_tensor("out", [LAT, D], F32, kind="ExternalOutput").ap()
        p.xT = p.dscratch("xT", [D, S], F32)
        p.wguS = [[p.dscratch("wgu%d%d" % (i, f), [FC // 2, 128, 2, DC, 256], BF16) for f in range(2)] for i in range(2)]
        p.wdS = [[p.dscratch("wd%d%d" % (i, f), [DC, 128, FC, 128], BF16) for f in range(2)] for i in range(2)]
        p.winS = [p.dscratch("win%d" % i, [10, 128, 4, DC, 128], BF16) for i in range(2)]
        p.wvdS = [p.dscratch("wvd%d" % i, [128, DC, 512], BF16) for i in range(2)]
        p.wvgS = [p.dscratch("wvg%d" % i, [128, DC, 256], BF16) for i in range(2)]
        p.woutS = [p.dscratch("wout%d" % i, [4, 128, 4, DC, 128], BF16) for i in range(2)]
        p.adaS = [p.dscratch("ada%d" % i, [36, 128, DC, 512], BF16) for i in range(2)]
        p.gluS = [p.dscratch("glu%d" % i, [128, 4, 512], BF16) for i in range(2)]
        p.uT = p.dscratch("uT", [512, S], F32)
        p.uTb = p.dscratch("uTb", [512, S], BF16)
        p.qd = p.dscratch("qd", [512, S], BF16)
        p.kd = p.dscratch("kd", [512, S], BF16)
        p.vd = p.dscratch("vd", [4, 128, NB, 128], BF16)
        p.qg = p.dscratch("qg", [1024, S], BF16)
        p.kg = p.dscratch("kg", [256, S], BF16)
        p.vg = p.dscratch("vg", [2, 128, NB, 128], BF16)
        p.yF = p.dscratch("yF", [512, S], F32)
        p.yR = p.dscratch("yR", [512, S], F32)
        p.s5T = p.dscratch("s5T", [512, S], BF16)
        p.ropeT = p.dscratch("ropeT", [4, 128, LAT], F32)
        p.mdbg = p.dscratch("mdbg", [128, 144, 2], F32)
        p.wbuf = {}

    def conv(self, key, out, in_):
        fw = self.fw
        if self.conv_filter is not None and not any(key.startswith(k) for k in self.conv_filter):
            return
        b = self.wbuf.setdefault(key, Buf(key))
        fw.dma(fw.pool, out, in_, writes=[b], sbuf_side=b, track=False)

    def convert_layer(self, i):
        p = self
        for blk in range(36):
            p.conv("ada%d" % i, p.adaS[i][blk], p.ada_w[i][:, blk * 512:(blk + 1) * 512].rearrange("(kc p) n -> p kc n", p=128))
        self.convert_ffn(i, 0)
        wi, wr = p.w_in[i], p.w_in_rot[i]

        def colchunk(src, c0):
            return src[:, c0:c0 + 128].rearrange("(kc p) n -> p kc n", p=128)
        groups = [
            [(wi, 0), (wi, 128), (wi, 256), (wi, 384)],
            [(wi, 512 + 128 * k) for k in range(4)],
            [(wr, 0 + 128 * k) for k in range(4)],
            [(wi, 1024 + 128 * k) for k in range(4)],
            [(wr, 512 + 128 * k) for k in range(4)],
            [(wi, 2048 + 128 * k) for k in range(4)],
            [(wr, 1024 + 128 * k) for k in range(4)],
            [(wi, 2560 + 128 * k) for k in range(4)],
            [(wr, 1536 + 128 * k) for k in range(4)],
            [(wi, 3072), (wi, 3200), (wr, 2048), (wr, 2176)],
        ]
        for g, lst in enumerate(groups):
            if g in (2, 4, 6, 8):
                continue
            for ci, (src, c0) in enumerate(lst):
                if g == 9 and ci >= 2:
                    continue
                p.conv("win%d" % i, p.winS[i][g, :, ci], colchunk(src, c0))
        p.conv("wv%d" % i, p.wvdS[i], wi[:, 1536:2048].rearrange("(kc p) n -> p kc n", p=128))
        p.conv("wv%d" % i, p.wvgS[i], wi[:, 3328:3584].rearrange("(kc p) n -> p kc n", p=128))
        p.conv("glu%d" % i, p.gluS[i], p.glu_w[i].rearrange("(fi p) n -> p fi n", p=128))
        for dg in range(4):
            for ci in range(4):
                dc = dg * 4 + ci
                p.conv("wout%d" % i, p.woutS[i][dg, :, ci], p.w_out[i][:, dc * 128:(dc + 1) * 128].rearrange("(kc p) n -> p kc n", p=128))
        self.convert_ffn(i, 1)

    def convert_ffn(self, i, f):
        p = self
        for jp in range(FC // 2):
            p.conv("wgu%d%d_%d" % (i, f, jp // 6), p.wguS[i][f][jp, :, 0], p.w_gate[i, f][:, jp * 256:(jp + 1) * 256].rearrange("(kc p) n -> p kc n", p=128))
            p.conv("wgu%d%d_%d" % (i, f, jp // 6), p.wguS[i][f][jp, :, 1], p.w_up[i, f][:, jp * 256:(jp + 1) * 256].rearrange("(kc p) n -> p kc n", p=128))
        for dc in range(DC):
            for h2 in range(2):
                p.conv("wd%d%d_%d" % (i, f, dc // 8), p.wdS[i][f][dc, :, h2 * 22:(h2 + 1) * 22],
                       p.w_down[i, f][h2 * 2816:(h2 + 1) * 2816, dc * 128:(dc + 1) * 128].rearrange("(j p) n -> p j n", p=128))

    def setup(self):
        p, fw, nc = self, self.fw, self.nc
        p.bpair = [fw.psum("bpair", [128, 2, 512], F32) for _ in range(4)]
        p.banks = [p.bpair[k // 2][:, k % 2, :] for k in range(8)]
        p.bb = [Buf("bank%d" % k, excl=True) for k in range(8)]
        p.ones = fw.sbuf("ones", [128, 128], BF16)
        p.bconst = Buf("const")
        p.ident = fw.sbuf("ident", [128, 128], F32)
        p.ones32 = fw.sbuf("ones32", [128, 128], F32)
        p.perm = fw.sbuf("perm", [128, 2, 128], BF16)
        p.epsc = fw.sbuf("epsc", [128, 1], F32)
        p.negpi = fw.sbuf("negpi", [128, 1], F32)
        p.mT = fw.sbuf("mT", [128, 144, 2], F32)
        p.bmT = Buf("mT")
        p.Acol = fw.sbuf("Acol", [128, 3, DC, 2], F32)
        p.Gcol = fw.sbuf("Gcol", [128, 3, DC, 2], F32)
        p.bcols = Buf("cols")
        p.ngT = fw.sbuf("ngT", [128, 2, 3, DC], F32)
        p.fgT = fw.sbuf("fgT", [128, DC], F32)
        p.hg = fw.sbuf("hg", [128, 2, 5], F32)
        p.s5d = fw.sbuf("s5d", [128, 2, 2, 4], F32)
        p.sq = [fw.sbuf("sq", [128, 512], BF16) for _ in range(2)]
        p.bsq = bufs(2, "sq")
        p.tf = [fw.sbuf("tf", [128, 512], F32) for _ in range(4)]
        p.btf = bufs(4, "tf")
        p.rs = fw.sbuf("rs", [128, 512], F32)
        p.brs = Buf("rs")
        p.lamc = fw.sbuf("lamc", [128, 4], F32)
        p.blamc = Buf("lamc")
        bc = p.bconst
        fw.op(fw.pool, lambda e: e.memset(p.ones[:], 1.0), writes=[bc])
        fw.op(fw.pool, lambda e: e.memset(p.ones32[:], 1.0), writes=[bc])
        fw.op(fw.pool, lambda e: e.memset(p.ident[:], 0.0), writes=[bc])
        fw.op(fw.pool, lambda e: e.affine_select(out=p.ident[:], in_=p.ident[:], pattern=[[-1, 128]], compare_op=ALU.not_equal,
                                                  fill=1.0, base=0, channel_multiplier=1), reads=[bc], writes=[bc])
        for b_ in range(4):
            fw.op(fw.pool, lambda e, b_=b_: e.tensor_copy(out=p.perm[:, 0, 32 * b_:32 * b_ + 32], in_=p.ident[:, 32 * (b_ ^ 1):32 * (b_ ^ 1) + 32]),
                  reads=[bc], writes=[bc])
        for b_ in range(8):
            fw.op(fw.pool, lambda e, b_=b_: e.tensor_copy(out=p.perm[:, 1, 16 * b_:16 * b_ + 16], in_=p.ident[:, 16 * (b_ ^ 1):16 * (b_ ^ 1) + 16]),
                  reads=[bc], writes=[bc])
        fw.op(fw.pool, lambda e: e.memset(p.epsc[:], EPS), writes=[bc])
        fw.op(fw.pool, lambda e: e.memset(p.negpi[:], -PI), writes=[bc])
        bl = Buf("smallloads")
        fw.dma(fw.sp, p.ngT[:], p.norm_gT.rearrange("i p k c -> p i k c"), writes=[bl], sbuf_side=bl)
        fw.dma(fw.sp, p.fgT[:], p.final_gT, writes=[bl], sbuf_side=bl)
        fw.dma(fw.sp, p.hg[:], p.hgT.rearrange("i p k -> p i k"), writes=[bl], sbuf_side=bl)
        fw.dma(fw.sp, p.s5d[:], p.s5dT.rearrange("i p a k -> p i a k"), writes=[bl], sbuf_side=bl)
        p.bsmall = bl

    def rope_tables(self):
        p, fw = self, self.fw
        with self.fw.scope() as sc:
            di = fw.sbuf("di", [128, 1], F32, sc)
            col = fw.sbuf("col", [128, 8], F32, sc)
            prow = fw.sbuf("prow", [128, LAT], F32, sc)
            pcol = fw.sbuf("pcol", [128, LAT], F32, sc)
            ang = fw.sbuf("ang", [128, LAT], F32, sc)
            tb = fw.sbuf("tb", [128, LAT], F32, sc)
            prow2 = fw.sbuf("prow2", [128, LAT], F32, sc)
            b = Buf("rope")
            fw.op(fw.pool, lambda e: e.iota(prow[:].rearrange("p (r c) -> p r c", c=64), pattern=[[1, 64], [0, 64]], base=0,
                                            channel_multiplier=0, allow_small_or_imprecise_dtypes=True), writes=[b])
            fw.op(fw.pool, lambda e: e.iota(pcol[:].rearrange("p (r c) -> p r c", c=64), pattern=[[0, 64], [1, 64]], base=0,
                                            channel_multiplier=0, allow_small_or_imprecise_dtypes=True), reads=[b], writes=[b])
            fw.op(fw.pool, lambda e: e.iota(di[:], pattern=[[0, 1]], base=0, channel_multiplier=1,
                                            allow_small_or_imprecise_dtypes=True), reads=[b], writes=[b])
            dv = fw.dve

            def o(fn):
                fw.op(dv, fn, reads=[b], writes=[b])
            MAGIC = 12582912.0
            o(lambda e: e.tensor_single_scalar(out=col[:, 3:4], in_=di[:], scalar=63.5, op=ALU.is_gt))
            o(lambda e: e.scalar_tensor_tensor(out=col[:, 6:7], in0=col[:, 3:4], scalar=-64.0, in1=di[:], op0=ALU.mult, op1=ALU.add))
            o(lambda e: e.tensor_single_scalar(out=col[:, 4:5], in_=col[:, 6:7], scalar=31.5, op=ALU.is_gt))
            o(lambda e: e.scalar_tensor_tensor(out=col[:, 7:8], in0=col[:, 4:5], scalar=-32.0, in1=col[:, 6:7], op0=ALU.mult, op1=ALU.add))
            o(lambda e: e.tensor_single_scalar(out=col[:, 5:6], in_=col[:, 7:8], scalar=15.5, op=ALU.is_gt))
            o(lambda e: e.scalar_tensor_tensor(out=col[:, 6:7], in0=col[:, 5:6], scalar=-16.0, in1=col[:, 7:8], op0=ALU.mult, op1=ALU.add))
            for kind in range(2):
                half = 32 if kind == 0 else 16
                jcol = col[:, 7:8] if kind == 0 else col[:, 6:7]
                rowb = col[:, 3:4] if kind == 0 else col[:, 4:5]
                sgnb = col[:, 4:5] if kind == 0 else col[:, 5:6]
                fw.op(fw.act, lambda e: e.activation(out=col[:, 0:1], in_=jcol, func=AF.Exp, scale=-math.log(10000.0) / half),
                      reads=[b], writes=[b])
                o(lambda e: e.tensor_scalar(out=col[:, 1:2], in0=rowb, scalar1=-1.0, scalar2=1.0, op0=ALU.mult, op1=ALU.add))
                o(lambda e: e.tensor_scalar(out=col[:, 2:3], in0=sgnb, scalar1=2.0, scalar2=-1.0, op0=ALU.mult, op1=ALU.add))
                o(lambda e: e.tensor_tensor(out=ang[:], in0=prow[:], in1=pcol[:], op=ALU.subtract))
                o(lambda e: e.scalar_tensor_tensor(out=ang[:], in0=ang[:], scalar=col[:, 1:2], in1=pcol[:], op0=ALU.mult, op1=ALU.add))
                o(lambda e: e.tensor_scalar(out=ang[:], in0=ang[:], scalar1=col[:, 0:1], scalar2=None, op0=ALU.mult))
                for which in range(2):
                    if which == 0:
                        o(lambda e: e.tensor_scalar(out=tb[:], in0=ang[:], scalar1=0.5 * PI, scalar2=None, op0=ALU.add))
                    else:
                        o(lambda e: e.tensor_copy(out=tb[:], in_=ang[:]))
                    o(lambda e: e.tensor_scalar(out=prow2[:], in0=tb[:], scalar1=1.0 / (2 * PI), scalar2=MAGIC, op0=ALU.mult, op1=ALU.add))
                    o(lambda e: e.tensor_scalar(out=prow2[:], in0=prow2[:], scalar1=-MAGIC, scalar2=None, op0=ALU.add))
                    o(lambda e: e.scalar_tensor_tensor(out=tb[:], in0=prow2[:], scalar=-2 * PI, in1=tb[:], op0=ALU.mult, op1=ALU.add))
                    o(lambda e: e.tensor_scalar(out=tb[:], in0=tb[:], scalar1=PI, scalar2=-PI, op0=ALU.min, op1=ALU.max))
                    fw.op(fw.act, lambda e: e.activation(out=tb[:], in_=tb[:], func=AF.Sin), reads=[b], writes=[b])
                    if which == 1:
                        o(lambda e: e.tensor_scalar(out=tb[:], in0=tb[:], scalar1=col[:, 2:3], scalar2=None, op0=ALU.mult))
                    fw.dma(fw.sp, p.ropeT[2 * kind + which], tb[:], reads=[b], sbuf_side=b)

    def pass_alloc(self, sc):
        p, fw = self, self.fw
        p.ws = WStream(fw, 4, sc)
        p.xt = fw.sbuf("xt", [128, DC, 512], F32, sc)
        p.bxt = bufs(DC, "xt")
        p.ht = fw.sbuf("ht", [128, DC, 512], BF16, sc)
        p.bht = bufs(DC, "ht")

    def modulation(self, i):
        p, fw = self, self.fw
        with self.fw.scope() as sc:
            ws = p.ws = WStream(fw, 4, sc)
            cs = fw.sbuf("cs", [128, DC, 2], F32, sc)
            sc_b = fw.sbuf("scb", [128, DC, 2], BF16, sc)
            adab = fw.sbuf("adab", [128, 144], F32, sc)
            bl = Buf("modl")
            fw.dma(fw.sp, cs[:], p.cT, writes=[bl], sbuf_side=bl)
            fw.dma(fw.sp, adab[:], p.ada_bT[i], writes=[bl], sbuf_side=bl)
            fw.op(fw.act, lambda e: e.activation(out=sc_b[:], in_=cs[:], func=AF.Silu), reads=[bl], writes=[bl])
            wb = p.wbuf["ada%d" % i]
            for blk in range(36):
                ws.enqueue(lambda t, b, blk=blk: fw.dma(fw.sp, t[:, :].rearrange("p (kc n) -> p kc n", kc=DC), p.adaS[i][blk],
                                                         reads=[wb], writes=[b], sbuf_side=b))
            bank, bbk = p.banks[0], p.bb[0]
            for blk in range(36):
                t, b = ws.take()
                tv = t[:, :].rearrange("p (kc n) -> p kc n", kc=DC)
                for c4 in range(4):
                    cc = blk * 4 + c4
                    fw.group(fw.pe, [
                        (lambda e, kc=kc, c4=c4, cc=cc: e.matmul(bank[:, 2 * cc:2 * cc + 2], lhsT=tv[:, kc, c4 * 128:(c4 + 1) * 128],
                                                                 rhs=sc_b[:, kc, :], start=(kc == 0), stop=(kc == DC - 1)))
                        for kc in range(DC)], reads=[b, bl], writes=[bbk])
            fw.op(fw.dve, lambda e: e.tensor_tensor(out=p.mT[:], in0=bank[:, 0:288].rearrange("p (c t) -> p c t", t=2),
                                                    in1=adab[:].unsqueeze(2).broadcast_to([128, 144, 2]), op=ALU.add),
                  reads=[bbk, bl], writes=[p.bmT])
            m4 = p.mT[:].rearrange("p (k c) t -> p k c t", c=DC)
            for k in range(3):
                g = p.ngT[:, i, k, :].unsqueeze(2).broadcast_to([128, DC, 2])
                fw.op(fw.dve, lambda e, k=k: e.tensor_scalar(out=p.Acol[:, k], in0=m4[:, 3 * k + 1], scalar1=1.0, scalar2=None, op0=ALU.add),
                      reads=[p.bmT], writes=[p.bcols])
                fw.op(fw.dve, lambda e, k=k, g=g: e.tensor_tensor(out=p.Acol[:, k], in0=p.Acol[:, k], in1=g, op=ALU.mult),
                      reads=[p.bcols, p.bsmall], writes=[p.bcols])
                fw.op(fw.dve, lambda e, k=k: e.tensor_scalar(out=p.Gcol[:, k], in0=m4[:, 3 * k + 2], scalar1=(1.0 if k == 1 else 0.5),
                                                            scalar2=None, op0=ALU.mult), reads=[p.bmT, p.bcols], writes=[p.bcols])
            if "mdbg" in p.debug:
                fw.dma(fw.act, p.mdbg, p.mT[:], reads=[p.bmT], sbuf_side=p.bmT)
            fw.barrier()

    def rmsnorm_mod(self, i, k, c, N):
        p, fw = self, self.fw
        ssb, bss = p.banks[6], p.bb[6]
        for kc in range(DC):
            s, bs = p.sq[kc % 2], p.bsq[kc % 2]
            fw.op(fw.act, lambda e, kc=kc, s=s: e.activation(out=s[:, :N], in_=p.xt[:, kc, :N], func=AF.Square),
                  reads=[p.bxt[kc]], writes=[bs])
            fw.op(fw.pe, lambda e, kc=kc, s=s: e.matmul(ssb[:, :N], lhsT=p.ones[:], rhs=s[:, :N], start=(kc == 0), stop=(kc == DC - 1)),
                  reads=[bs, p.bconst], writes=[bss])
        fw.op(fw.act, lambda e: e.activation(out=p.rs[:, :N], in_=ssb[:, :N], func=AF.Sqrt, scale=1.0 / D, bias=p.epsc[:]),
              reads=[bss, p.bconst], writes=[p.brs])
        fw.op(fw.dve, lambda e: e.reciprocal(out=p.rs[:, :N], in_=p.rs[:, :N]), reads=[p.brs], writes=[p.brs])
        m4 = p.mT[:].rearrange("p (k c) t -> p k c t", c=DC)
        for kc in range(DC):
            t, bt = p.tf[kc % 2], p.btf[kc % 2]
            fw.op(fw.dve, lambda e, kc=kc, t=t: e.tensor_tensor(out=t[:, :N], in0=p.xt[:, kc, :N], in1=p.rs[:, :N], op=ALU.mult),
                  reads=[p.bxt[kc], p.brs], writes=[bt])
            fw.op(fw.act, lambda e, kc=kc, t=t: e.activation(out=p.ht[:, kc, :N], in_=t[:, :N], func=AF.Identity,
                                                             scale=p.Acol[:, k, kc, c:c + 1], bias=m4[:, 3 * k, kc, c:c + 1]),
                  reads=[bt, p.bcols, p.bmT], writes=[p.bht[kc]])

    def enqueue_ffn(self, i, f):
        p, fw, ws = self, self.fw, self.ws
        for jp in range(FC // 2):
            ws.enqueue(lambda t, b, jp=jp: fw.dma(fw.sp, t[:, :].rearrange("p (g kc n) -> p g kc n", g=2, kc=DC),
                                                   p.wguS[i][f][jp], reads=[p.wbuf["wgu%d%d_%d" % (i, f, jp // 6)]], writes=[b], sbuf_side=b))
        for dc in range(DC):
            ws.enqueue(lambda t, b, dc=dc: fw.dma(fw.sp, t[:, 0:FC * 128].rearrange("p (j n) -> p j n", j=FC),
                                                   p.wdS[i][f][dc], reads=[p.wbuf["wd%d%d_%d" % (i, f, dc // 8)]], writes=[b], sbuf_side=b))

    def ffn(self, i, f, k, c, N, act, bact):
        p, fw, ws = self, self.fw, self.ws
        for jp in range(FC // 2):
            t, b = ws.take()
            tv = t[:, :].rearrange("p (g kc n) -> p g kc n", g=2, kc=DC)
            for jj in range(2):
                j = 2 * jp + jj
                gb, bgb = p.banks[j % 2], p.bb[j % 2]
                ub, bub = p.banks[2 + j % 2], p.bb[2 + j % 2]
                fw.group(fw.pe, [(lambda e, kc=kc, jj=jj, gb=gb: e.matmul(gb[:, :N], lhsT=tv[:, 0, kc, jj * 128:(jj + 1) * 128], rhs=p.ht[:, kc, :N],
                                                                         start=(kc == 0), stop=(kc == DC - 1))) for kc in range(DC)],
                         reads=[b] + p.bht, writes=[bgb])
                fw.group(fw.pe, [(lambda e, kc=kc, jj=jj, ub=ub: e.matmul(ub[:, :N], lhsT=tv[:, 1, kc, jj * 128:(jj + 1) * 128], rhs=p.ht[:, kc, :N],
                                                                         start=(kc == 0), stop=(kc == DC - 1))) for kc in range(DC)],
                         reads=[b] + p.bht, writes=[bub])
                st, bst = p.tf[2 + j % 2], p.btf[2 + j % 2]
                fw.op(fw.act, lambda e, gb=gb, st=st: e.activation(out=st[:, :N], in_=gb[:, :N], func=AF.Silu), reads=[bgb], writes=[bst])
                fw.op(fw.dve, lambda e, ub=ub, st=st, j=j: e.tensor_tensor(out=act[:, j, :N], in0=st[:, :N], in1=ub[:, :N], op=ALU.mult),
                      reads=[bst, bub], writes=[bact[j]])
        for dc in range(DC):
            t, b = ws.take()
            tv = t[:, 0:FC * 128].rearrange("p (j n) -> p j n", j=FC)
            yb, byb = p.banks[4 + dc % 2], p.bb[4 + dc % 2]
            fw.group(fw.pe, [(lambda e, j=j, yb=yb: e.matmul(yb[:, :N], lhsT=tv[:, j, :], rhs=act[:, j, :N], start=(j == 0), stop=(j == FC - 1)))
                             for j in range(FC)], reads=[b] + bact, writes=[byb])
            fw.op(fw.dve, lambda e, dc=dc, yb=yb: e.scalar_tensor_tensor(out=p.xt[:, dc, :N], in0=yb[:, :N], scalar=p.Gcol[:, k, dc, c:c + 1],
                                                                        in1=p.xt[:, dc, :N], op0=ALU.mult, op1=ALU.add),
                  reads=[byb, p.bcols, p.bxt[dc]], writes=[p.bxt[dc]])

    def load_x_tile(self, i, t0, N, is_ctx):
        p, fw = self, self.fw
        if i > 0:
            bl = p.bxt
            fw.dma(fw.sp, p.xt[:, :, :N], p.xT.rearrange("(kc q) t -> q kc t", q=128)[:, :, t0:t0 + N], writes=bl, sbuf_side=bl[0])
            return
        with self.fw.scope() as sc:
            xtok = [fw.sbuf("xtok", [128, D], F32, sc) for _ in range(2)]
            bxk = bufs(2, "xtok")
            nb = 0
            for blk in range(N // 128):
                src = p.ctx[blk * 128:(blk + 1) * 128, :] if is_ctx else p.x[t0 - CTX + blk * 128: t0 - CTX + (blk + 1) * 128, :]
                xk, bk = xtok[blk % 2], bxk[blk % 2]
                fw.dma(fw.sp, xk[:], src, writes=[bk], sbuf_side=bk)
                for dg in range(4):
                    bank, bbk = p.banks[nb % 4], p.bb[nb % 4]
                    nb += 1
                    fw.group(fw.pe, [(lambda e, q=q, dg=dg, bank=bank, xk=xk: e.transpose(out=bank[:, q * 128:(q + 1) * 128],
                                                                                      in_=xk[:, (dg * 4 + q) * 128:(dg * 4 + q + 1) * 128],
                                                                                      identity=p.ident[:])) for q in range(4)],
                             reads=[bk, p.bconst], writes=[bbk])
                    eng = fw.act if dg % 2 == 0 else fw.dve
                    outap = p.xt[:, dg * 4:(dg + 1) * 4, blk * 128:(blk + 1) * 128]
                    inap = bank[:, :].rearrange("p (q n) -> p q n", q=4)
                    if eng is fw.act:
                        fw.op(eng, lambda e, outap=outap, inap=inap: e.activation(out=outap, in_=inap, func=AF.Copy),
                              reads=[bbk], writes=p.bxt[dg * 4:(dg + 1) * 4])
                    else:
                        fw.op(eng, lambda e, outap=outap, inap=inap: e.tensor_copy(out=outap, in_=inap),
                              reads=[bbk], writes=p.bxt[dg * 4:(dg + 1) * 4])
            fw.barrier()

    def pass_a(self, i):
        with self.fw.scope() as sc:
            self.pass_alloc(sc)
            self._pass_a(i)

    def _pass_a(self, i):
        p, fw, ws = self, self.fw, self.ws
        wbin = p.wbuf["win%d" % i]
        wbv = p.wbuf["wv%d" % i]
        for (t0, N, is_ctx) in TILES:
            p.enqueue_ffn(i, 0)
            for g in range(10):
                if g in (2, 4, 6, 8):
                    continue
                ws.enqueue(lambda t, b, g=g: fw.dma(fw.sp, t[:, :].rearrange("p (c kc n) -> p c kc n", c=4, kc=DC), p.winS[i][g],
                                                     reads=[wbin], writes=[b], sbuf_side=b))
            ws.enqueue(lambda t, b: fw.dma(fw.sp, t[:, :].rearrange("p (kc n) -> p kc n", kc=DC), p.wvdS[i],
                                           reads=[wbv], writes=[b], sbuf_side=b))
            ws.enqueue(lambda t, b: fw.dma(fw.sp, t[:, 0:DC * 256].rearrange("p (kc n) -> p kc n", kc=DC), p.wvgS[i],
                                           reads=[wbv], writes=[b], sbuf_side=b))
        for (t0, N, is_ctx) in TILES[:p.ntiles]:
            c = 1 if is_ctx else 0
            p.load_x_tile(i, t0, N, is_ctx)
            with self.fw.scope() as sc:
                act = fw.sbuf("act", [128, FC, 512], BF16, sc)
                bact = bufs(FC, "act")
                p.rmsnorm_mod(i, 0, c, N)
                p.ffn(i, 0, 0, c, N, act, bact)
                fw.barrier()
            fw.dma(fw.act, p.xT.rearrange("(kc q) t -> q kc t", q=128)[:, :, t0:t0 + N], p.xt[:, :, :N], reads=p.bxt, sbuf_side=p.bxt[0])
            if p.stop == "ffn":
                continue
            p.rmsnorm_mod(i, 1, c, N)
            with self.fw.scope() as sc:
                p.in_proj(i, t0, N, is_ctx, sc)
                fw.barrier()

    def in_proj(self, i, t0, N, is_ctx, sc):
        p, fw, ws = self, self.fw, self.ws
        ust = fw.sbuf("ust", [128, 4, 512], F32, sc)
        usb = fw.sbuf("usb", [128, 4, 512], BF16, sc)
        qst = [fw.sbuf("qst", [128, 4, 512], BF16, sc) for _ in range(2)]
        bqst = [Buf("qst0"), Buf("qst1")]
        vsd = fw.sbuf("vsd", [128, 4, 512], BF16, sc)
        vsg = fw.sbuf("vsg", [128, 4, 256], BF16, sc)
        bu, bub, bvd, bvg = Buf("ust"), Buf("usb"), Buf("vsd"), Buf("vsg")
        rp = fw.sbuf("rp", [128, 4, 512], F32, sc)
        rq = fw.sbuf("rq", [128, 4, 512], F32, sc)
        brp, brq = Buf("rp"), Buf("rq")
        if not is_ctx:
            l0 = t0 - CTX
            fw.dma(fw.sp, rp[:, :, :N], p.ropeT.rearrange("k q t -> q k t")[:, :, l0:l0 + N], writes=[brp], sbuf_side=brp)
            for n, (tb, gi) in enumerate([(0, 1), (1, 2), (0, 3), (1, 4)]):
                fw.op(fw.dve, lambda e, n=n, tb=tb, gi=gi: e.tensor_scalar(out=rq[:, n, :N], in0=rp[:, tb, :N], scalar1=p.hg[:, i, gi:gi + 1],
                                                                           scalar2=None, op0=ALU.mult), reads=[brp, p.bsmall], writes=[brq])
        nbank = [0]
        nqb = [0]
        qbf = [fw.sbuf("qbf", [128, 512], BF16, sc) for _ in range(2)]
        bqbf = bufs(2, "qbf")

        def mm_chunk(tv, ci):
            bank, bbk = p.banks[nbank[0] % 6], p.bb[nbank[0] % 6]
            nbank[0] += 1
            return bank, bbk, [(lambda e, kc=kc: e.matmul(bank[:, :N], lhsT=tv[:, ci, kc, :], rhs=p.ht[:, kc, :N], start=(kc == 0),
                                                          stop=(kc == DC - 1))) for kc in range(DC)]

        def view(t):
            return t[:, :].rearrange("p (c kc n) -> p c kc n", c=4, kc=DC)
        t, b = ws.take()
        tv = view(t)
        for ci in range(4):
            bank, bbk, mms = mm_chunk(tv, ci)
            fw.group(fw.pe, mms, reads=[b] + p.bht, writes=[bbk])
            fw.op(fw.act, lambda e, ci=ci, bank=bank: e.activation(out=ust[:, ci, :N], in_=bank[:, :N], func=AF.Copy), reads=[bbk], writes=[bu])
            fw.op(fw.dve, lambda e, ci=ci: e.tensor_copy(out=usb[:, ci, :N], in_=ust[:, ci, :N]), reads=[bu], writes=[bub])
        fw.dma(fw.act, p.uT.rearrange("(c q) t -> q c t", q=128)[:, :, t0:t0 + N], ust[:, :, :N], reads=[bu], sbuf_side=bu)
        fw.dma(fw.act, p.uTb.rearrange("(c q) t -> q c t", q=128)[:, :, t0:t0 + N], usb[:, :, :N], reads=[bub], sbuf_side=bub)

        def qk_group(dst, dst_c0, nchunks, kind, sidx, gain_main, tabs):
            st, bs_ = qst[sidx], bqst[sidx]
            tm, bm = ws.take()
            tvm = view(tm)
            for ci in range(nchunks):
                bankA, bbA, mmsA = mm_chunk(tvm, ci)
                fw.group(fw.pe, mmsA, reads=[bm] + p.bht, writes=[bbA])
                if not is_ctx:
                    qb_, bqb = qbf[nqb[0] % 2], bqbf[nqb[0] % 2]
                    nqb[0] += 1
                    fw.op(fw.act, lambda e: e.activation(out=qb_[:, :N], in_=bankA[:, :N], func=AF.Copy), reads=[bbA], writes=[bqb])
                    bankB, bbB = p.banks[nbank[0] % 6], p.bb[nbank[0] % 6]
                    nbank[0] += 1
                    fw.op(fw.pe, lambda e: e.matmul(bankB[:, :N], lhsT=p.perm[:, (1 if kind == 'd' else 0), :], rhs=qb_[:, :N], start=True, stop=True),
                          reads=[bqb, p.bconst], writes=[bbB])
                if kind != 'd':
                    s, bs = p.sq[ci % 2], p.bsq[ci % 2]
                    ssb, bss = p.banks[6 + ci % 2], p.bb[6 + ci % 2]
                    fw.op(fw.act, lambda e, s=s, bankA=bankA: e.activation(out=s[:, :N], in_=bankA[:, :N], func=AF.Square), reads=[bbA], writes=[bs])
                    fw.op(fw.pe, lambda e, s=s, ssb=ssb: e.matmul(ssb[:, :N], lhsT=p.ones[:], rhs=s[:, :N], start=True, stop=True),
                          reads=[bs, p.bconst], writes=[bss])
                    fw.op(fw.act, lambda e, ssb=ssb: e.activation(out=p.rs[:, :N], in_=ssb[:, :N], func=AF.Sqrt, scale=1.0 / 128, bias=p.epsc[:]),
                          reads=[bss, p.bconst], writes=[p.brs])
                    fw.op(fw.dve, lambda e: e.reciprocal(out=p.rs[:, :N], in_=p.rs[:, :N]), reads=[p.brs], writes=[p.brs])
                if is_ctx:
                    if kind == 'd':
                        fw.op(fw.dve, lambda e, ci=ci, bankA=bankA: e.tensor_copy(out=st[:, ci, :N], in_=bankA[:, :N]), reads=[bbA], writes=[bs_])
                    else:
                        fw.op(fw.dve, lambda e, ci=ci, bankA=bankA: e.scalar_tensor_tensor(out=st[:, ci, :N], in0=bankA[:, :N],
                                                                                         scalar=p.hg[:, i, gain_main:gain_main + 1],
                                                                                         in1=p.rs[:, :N], op0=ALU.mult, op1=ALU.mult),
                              reads=[bbA, p.brs, p.bsmall], writes=[bs_])
                else:
                    tabt, btab = (rp, brp) if kind == 'd' else (rq, brq)
                    t1, b1 = p.tf[0], p.btf[0]
                    t2, b2 = p.tf[1], p.btf[1]
                    fw.op(fw.dve, lambda e, bankA=bankA, t1=t1: e.tensor_tensor(out=t1[:, :N], in0=bankA[:, :N], in1=tabt[:, tabs[0], :N], op=ALU.mult),
                          reads=[bbA, btab], writes=[b1])
                    fw.op(fw.dve, lambda e, bankB=bankB, t2=t2: e.tensor_tensor(out=t2[:, :N], in0=bankB[:, :N], in1=tabt[:, tabs[1], :N], op=ALU.mult),
                          reads=[bbB, btab], writes=[b2])
                    if kind == 'd':
                        fw.op(fw.dve, lambda e, ci=ci: e.tensor_tensor(out=st[:, ci, :N], in0=t1[:, :N], in1=t2[:, :N], op=ALU.add),
                              reads=[b1, b2], writes=[bs_])
                    else:
                        fw.op(fw.dve, lambda e: e.tensor_tensor(out=t1[:, :N], in0=t1[:, :N], in1=t2[:, :N], op=ALU.add),
                              reads=[b1, b2], writes=[b1])
                        fw.op(fw.dve, lambda e, ci=ci: e.tensor_tensor(out=st[:, ci, :N], in0=t1[:, :N], in1=p.rs[:, :N], op=ALU.mult),
                              reads=[b1, p.brs], writes=[bs_])
            fw.dma(fw.act, dst.rearrange("(c q) t -> q c t", q=128)[:, dst_c0:dst_c0 + nchunks, t0:t0 + N], st[:, 0:nchunks, :N],
                   reads=[bs_], sbuf_side=bs_)
        qk_group(p.qd, 0, 4, 'd', 0, None, (2, 3))
        qk_group(p.kd, 0, 4, 'd', 1, None, (2, 3))
        qk_group(p.qg, 0, 4, 'g', 0, 1, (0, 1))
        qk_group(p.qg, 4, 4, 'g', 1, 1, (0, 1))
        qk_group(p.kg, 0, 2, 'gk', 0, 3, (2, 3))
        tvd, bvdw = ws.take()
        tvg, bvgw = ws.take()
        tvdv = tvd[:, :].rearrange("p (kc n) -> p kc n", kc=DC)
        tvgv = tvg[:, 0:DC * 256].rearrange("p (kc n) -> p kc n", kc=DC)
        for blk in range(N // 128):
            bank, bbk = p.banks[blk % 2], p.bb[blk % 2]
            fw.group(fw.pe, [(lambda e, kc=kc, bank=bank, blk=blk: e.matmul(bank[:, :], lhsT=p.ht[:, kc, blk * 128:(blk + 1) * 128], rhs=tvdv[:, kc, :],
                                                                          start=(kc == 0), stop=(kc == DC - 1))) for kc in range(DC)],
                     reads=[bvdw] + p.bht, writes=[bbk])
            fw.op(fw.act, lambda e, blk=blk, bank=bank: e.activation(out=vsd[:, blk, :], in_=bank[:, :], func=AF.Copy), reads=[bbk], writes=[bvd])
            bank2, bbk2 = p.banks[2 + blk % 2], p.bb[2 + blk % 2]
            fw.group(fw.pe, [(lambda e, kc=kc, bank2=bank2, blk=blk: e.matmul(bank2[:, 0:256], lhsT=p.ht[:, kc, blk * 128:(blk + 1) * 128], rhs=tvgv[:, kc, :],
                                                                            start=(kc == 0), stop=(kc == DC - 1))) for kc in range(DC)],
                     reads=[bvgw] + p.bht, writes=[bbk2])
            fw.op(fw.dve, lambda e, blk=blk, bank2=bank2: e.tensor_copy(out=vsg[:, blk, :], in_=bank2[:, 0:256]), reads=[bbk2], writes=[bvg])
        nbk = N // 128
        b0 = t0 // 128
        for hh in range(4):
            fw.dma(fw.act, p.vd[hh, :, b0:b0 + nbk, :], vsd[:, 0:nbk, hh * 128:(hh + 1) * 128], reads=[bvd], sbuf_side=bvd)
        for hh in range(2):
            fw.dma(fw.act, p.vg[hh, :, b0:b0 + nbk, :], vsg[:, 0:nbk, hh * 128:(hh + 1) * 128], reads=[bvg], sbuf_side=bvg)

    def s5(self, i):
        p, fw = self, self.fw
        MAGIC = 12582912.0
        with fw.scope() as sc:
            dv, pl, ac = fw.dve, fw.pool, fw.act
            bs = Buf("s5setup")
            lamc = fw.sbuf("lamc", [128, 2, 3, 16], F32, sc)
            fw.dma(fw.sp, lamc[:], p.lamT[i].rearrange("d q k c -> q d k c"), writes=[bs], sbuf_side=bs)
            Uf = fw.sbuf("Uf", [128, 128], BF16, sc)
            Ur = fw.sbuf("Ur", [128, 128], BF16, sc)
            with fw.scope() as sc0:
                U32 = fw.sbuf("U32", [128, 128], F32, sc0)
                for (Ux, pat, cm) in ((Uf, 1, -1), (Ur, -1, 1)):
                    fw.op(pl, lambda e: e.memset(U32[:], 1.0), reads=[bs], writes=[bs])
                    fw.op(pl, lambda e: e.affine_select(out=U32[:], in_=U32[:], pattern=[[pat, 128]], compare_op=ALU.is_ge, fill=0.0, base=0,
                                                        channel_multiplier=cm), reads=[bs], writes=[bs])
                    fw.op(pl, lambda e: e.tensor_copy(out=Ux[:], in_=U32[:]), reads=[bs], writes=[bs])
            PCt = [fw.sbuf("PCt", [128, 2048], F32, sc) for _ in range(2)]
            PSt = [fw.sbuf("PSt", [128, 2048], F32, sc) for _ in range(2)]
            QC = [fw.sbuf("QC", [128, 16, 128], F32, sc) for _ in range(2)]
            QS = [fw.sbuf("QS", [128, 16, 128], F32, sc) for _ in range(2)]
            RB = [fw.sbuf("RB", [128, 4, 1024], BF16, sc) for _ in range(2)]
            CT = [fw.sbuf("CT", [128, 16, 2, 128], BF16, sc) for _ in range(2)]
            carry = fw.sbuf("carry", [128, 2, 16, 2], F32, sc)
            btab = Buf("s5tab")

            def o(eng, fn):
                fw.op(eng, fn, reads=[bs, p.bconst], writes=[bs])
            with fw.scope() as sc2:
                cols = fw.sbuf("cols", [128, 2, 12, 16], F32, sc2)
                kv = fw.sbuf("kv", [128, 128], F32, sc2)
                ang = fw.sbuf("ang", [128, 16, 128], F32, sc2)
                nn = fw.sbuf("nn", [128, 16, 128], F32, sc2)
                tc_ = fw.sbuf("tc", [128, 16, 128], F32, sc2)
                ts_ = fw.sbuf("ts", [128, 16, 128], F32, sc2)
                mg = fw.sbuf("mg", [128, 16, 128], F32, sc2)
                xb = fw.sbuf("xb", [128, 2, 16, 128], F32, sc2)
                xw = fw.sbuf("xw", [128, 2, 128], F32, sc2)

                def wrap_sin(dst, src, shift):
                    o(dv, lambda e: e.tensor_scalar(out=dst, in0=src, scalar1=shift, scalar2=None, op0=ALU.add))
                    o(dv, lambda e: e.tensor_scalar(out=nn[:], in0=dst, scalar1=1.0 / (2 * PI), scalar2=MAGIC, op0=ALU.mult, op1=ALU.add))
                    o(dv, lambda e: e.tensor_scalar(out=nn[:], in0=nn[:], scalar1=-MAGIC, scalar2=-2 * PI, op0=ALU.add, op1=ALU.mult))
                    o(dv, lambda e: e.tensor_tensor(out=dst, in0=dst, in1=nn[:], op=ALU.add))
                    o(dv, lambda e: e.tensor_scalar(out=dst, in0=dst, scalar1=PI, scalar2=-PI, op0=ALU.min, op1=ALU.max))
                    o(ac, lambda e: e.activation(out=dst, in_=dst, func=AF.Sin))

                def gen_table(d, base, step):
                    o(pl, lambda e: e.iota(kv[:], pattern=[[step, 128]], base=base, channel_multiplier=0, allow_small_or_imprecise_dtypes=True))
                    for q in range(16):
                        o(dv, lambda e, q=q: e.tensor_scalar(out=ang[:, q, :], in0=kv[:], scalar1=cols[:, d, 2, q:q + 1], scalar2=None, op0=ALU.mult))
                        o(ac, lambda e, q=q: e.activation(out=mg[:, q, :], in_=kv[:], func=AF.Exp, scale=cols[:, d, 1, q:q + 1]))
                    wrap_sin(ts_[:], ang[:], 0.0)
                    wrap_sin(tc_[:], ang[:], 0.5 * PI)
                    o(dv, lambda e: e.tensor_tensor(out=ts_[:], in0=ts_[:], in1=mg[:], op=ALU.mult))
                    o(dv, lambda e: e.tensor_tensor(out=tc_[:], in0=tc_[:], in1=mg[:], op=ALU.mult))

                for d in range(2):
                    c_ = lambda k: cols[:, d, k, :]
                    o(ac, lambda e: e.activation(out=c_(0), in_=lamc[:, d, 2, :], func=AF.Exp))
                    o(dv, lambda e: e.tensor_tensor(out=c_(1), in0=lamc[:, d, 0, :], in1=c_(0), op=ALU.mult))
                    o(dv, lambda e: e.tensor_tensor(out=c_(2), in0=lamc[:, d, 1, :], in1=c_(0), op=ALU.mult))
                    o(ac, lambda e: e.activation(out=c_(9), in_=c_(1), func=AF.Exp))
                    for (dst, shift) in ((4, 0.0), (3, 0.5 * PI)):
                        o(dv, lambda e: e.tensor_scalar(out=c_(11), in0=c_(2), scalar1=shift, scalar2=None, op0=ALU.add))
                        o(dv, lambda e: e.tensor_scalar(out=c_(10), in0=c_(11), scalar1=1.0 / (2 * PI), scalar2=MAGIC, op0=ALU.mult, op1=ALU.add))
                        o(dv, lambda e: e.tensor_scalar(out=c_(10), in0=c_(10), scalar1=-MAGIC, scalar2=-2 * PI, op0=ALU.add, op1=ALU.mult))
                        o(dv, lambda e: e.tensor_tensor(out=c_(11), in0=c_(11), in1=c_(10), op=ALU.add))
                        o(dv, lambda e: e.tensor_scalar(out=c_(11), in0=c_(11), scalar1=PI, scalar2=-PI, op0=ALU.min, op1=ALU.max))
                        o(ac, lambda e, dst=dst: e.activation(out=c_(dst), in_=c_(11), func=AF.Sin))
                        o(dv, lambda e, dst=dst: e.tensor_tensor(out=c_(dst), in0=c_(dst), in1=c_(9), op=ALU.mult))
                    o(dv, lambda e: e.tensor_tensor(out=c_(10), in0=c_(9), in1=c_(9), op=ALU.mult))
                    o(dv, lambda e: e.reciprocal(out=c_(10), in_=c_(10)))
                    o(dv, lambda e: e.tensor_tensor(out=c_(7), in0=c_(3), in1=c_(10), op=ALU.mult))
                    o(dv, lambda e: e.scalar_tensor_tensor(out=c_(8), in0=c_(4), scalar=-1.0, in1=c_(10), op0=ALU.mult, op1=ALU.mult))
                    o(dv, lambda e: e.tensor_tensor(out=c_(10), in0=lamc[:, d, 0, :], in1=lamc[:, d, 0, :], op=ALU.mult))
                    o(dv, lambda e: e.tensor_tensor(out=c_(11), in0=lamc[:, d, 1, :], in1=lamc[:, d, 1, :], op=ALU.mult))
                    o(dv, lambda e: e.tensor_tensor(out=c_(10), in0=c_(10), in1=c_(11), op=ALU.add))
                    o(dv, lambda e: e.reciprocal(out=c_(10), in_=c_(10)))
                    o(dv, lambda e: e.tensor_scalar(out=c_(9), in0=c_(3), scalar1=-1.0, scalar2=None, op0=ALU.add))
                    o(dv, lambda e: e.tensor_tensor(out=c_(5), in0=c_(9), in1=lamc[:, d, 0, :], op=ALU.mult))
                    o(dv, lambda e: e.tensor_tensor(out=c_(11), in0=c_(4), in1=lamc[:, d, 1, :], op=ALU.mult))
                    o(dv, lambda e: e.tensor_tensor(out=c_(5), in0=c_(5), in1=c_(11), op=ALU.add))
                    o(dv, lambda e: e.tensor_tensor(out=c_(5), in0=c_(5), in1=c_(10), op=ALU.mult))
                    o(dv, lambda e: e.tensor_tensor(out=c_(6), in0=c_(4), in1=lamc[:, d, 0, :], op=ALU.mult))
                    o(dv, lambda e: e.tensor_tensor(out=c_(11), in0=c_(9), in1=lamc[:, d, 1, :], op=ALU.mult))
                    o(dv, lambda e: e.tensor_tensor(out=c_(6), in0=c_(6), in1=c_(11), op=ALU.subtract))
                    o(dv, lambda e: e.tensor_tensor(out=c_(6), in0=c_(6), in1=c_(10), op=ALU.mult))
                    fw.dma(fw.sp, xb[:], p.XB[i, d].rearrange("r q c f -> c r q f"), reads=[bs], writes=[bs], sbuf_side=bs)
                    for q in range(16):
                        fc, ql = q // 4, q % 4
                        g_re, g_im = cols[:, d, 5, q:q + 1], cols[:, d, 6, q:q + 1]
                        o(dv, lambda e: e.tensor_scalar(out=xw[:, 0, :], in0=xb[:, 1, q, :], scalar1=g_im, scalar2=None, op0=ALU.mult))
                        o(dv, lambda e: e.scalar_tensor_tensor(out=xw[:, 0, :], in0=xb[:, 0, q, :], scalar=g_re, in1=xw[:, 0, :], op0=ALU.mult, op1=ALU.subtract))
                        o(dv, lambda e: e.tensor_scalar(out=xw[:, 1, :], in0=xb[:, 0, q, :], scalar1=g_im, scalar2=None, op0=ALU.mult))
                        o(dv, lambda e: e.scalar_tensor_tensor(out=xw[:, 1, :], in0=xb[:, 1, q, :], scalar=g_re, in1=xw[:, 1, :], op0=ALU.mult, op1=ALU.add))
                        bank, bbk = p.banks[q % 2], p.bb[q % 2]
                        fw.group(fw.pe, [(lambda e, ri=ri: e.transpose(out=bank[:, ri * 128:(ri + 1) * 128], in_=xw[:, ri, :], identity=p.ident[:]))
                                         for ri in range(2)], reads=[bs, p.bconst], writes=[bbk])
                        for ri in range(2):
                            fw.op(ac, lambda e, ri=ri: e.activation(out=RB[d][:, fc, ri * 512 + ql * 128: ri * 512 + (ql + 1) * 128],
                                                                    in_=bank[:, ri * 128:(ri + 1) * 128], func=AF.Copy),
                                  reads=[bbk], writes=[btab])
                    fw.dma(fw.sp, xb[:], p.YC[i, d].rearrange("r q c f -> c r q f"), reads=[bs], writes=[bs], sbuf_side=bs)
                    for q in range(16):
                        ai_re, ai_im = cols[:, d, 7, q:q + 1], cols[:, d, 8, q:q + 1]
                        o(dv, lambda e: e.tensor_scalar(out=xw[:, 0, :], in0=xb[:, 1, q, :], scalar1=ai_im, scalar2=None, op0=ALU.mult))
                        fw.op(dv, lambda e: e.scalar_tensor_tensor(out=CT[d][:, q, 0, :], in0=xb[:, 0, q, :], scalar=ai_re, in1=xw[:, 0, :], op0=ALU.mult,
                                                                   op1=ALU.subtract), reads=[bs], writes=[btab])
                        o(dv, lambda e: e.tensor_scalar(out=xw[:, 1, :], in0=xb[:, 0, q, :], scalar1=ai_im, scalar2=-1.0, op0=ALU.mult, op1=ALU.mult))
                        o(dv, lambda e: e.tensor_scalar(out=xw[:, 0, :], in0=xb[:, 1, q, :], scalar1=ai_re, scalar2=None, op0=ALU.mult))
                        fw.op(dv, lambda e: e.tensor_tensor(out=CT[d][:, q, 1, :], in0=xw[:, 1, :], in1=xw[:, 0, :], op=ALU.subtract),
                              reads=[bs], writes=[btab])
                    if d == 0:
                        gen_table(d, 1, 1)
                    else:
                        gen_table(d, 128, -1)
                    fw.op(dv, lambda e: e.tensor_copy(out=QC[d][:], in_=tc_[:]), reads=[bs], writes=[btab])
                    fw.op(dv, lambda e: e.tensor_copy(out=QS[d][:], in_=ts_[:]), reads=[bs], writes=[btab])
                    if d == 0:
                        gen_table(d, 0, -1)
                    else:
                        gen_table(d, -127, 1)
                    nbk = 0
                    for (src, dst) in ((tc_, PCt[d]), (ts_, PSt[d])):
                        for qg in range(4):
                            bank, bbk = p.banks[nbk % 4], p.bb[nbk % 4]
                            nbk += 1
                            fw.group(fw.pe, [(lambda e, k=k: e.transpose(out=bank[:, k * 128:(k + 1) * 128], in_=src[:, qg * 4 + k, :], identity=p.ident[:]))
                                             for k in range(4)], reads=[bs, p.bconst], writes=[bbk])
                            eng = ac if qg % 2 == 0 else dv
                            if eng is ac:
                                fw.op(eng, lambda e: e.activation(out=dst[:, qg * 512:(qg + 1) * 512], in_=bank[:, :], func=AF.Copy), reads=[bbk], writes=[btab])
                            else:
                                fw.op(eng, lambda e: e.tensor_copy(out=dst[:, qg * 512:(qg + 1) * 512], in_=bank[:, :]), reads=[bbk], writes=[btab])
            if "s5dbg" in p.debug:
                for nm, tl in (("dPC", PCt[0]), ("dPS", PSt[0]), ("dQC", QC[0]), ("dQS", QS[0]), ("dRB", RB[0]), ("dCT", CT[0]),
                               ("dPC1", PCt[1]), ("dQC1", QC[1])):
                    dd = p.nc.dram_tensor(nm, list(tl.shape), tl.dtype, kind="ExternalOutput").ap()
                    fw.dma(fw.sp, dd, tl[:], reads=[btab], sbuf_side=btab)
            with fw.scope() as scp:
                ub = [fw.sbuf("ub", [128, 4, 128], BF16, scp) for _ in range(2)]
                bub = bufs(2, "ub")
                Z = [fw.sbuf("Z", [128, 2, 512], BF16, scp) for _ in range(8)]
                bZ = bufs(8, "Z")
                P4a = [fw.sbuf("P4", [128, 512], F32, scp) for _ in range(8)]
                bP4a = bufs(8, "P4")
                Hh = [fw.sbuf("Hh", [128, 2, 128], BF16, scp) for _ in range(32)]
                bH = bufs(32, "Hh")
                Pp = [fw.sbuf("Pp", [128, 4, 128], F32, scp) for _ in range(4)]
                bPp = bufs(4, "Pp")
                Gc = [fw.sbuf("Gc", [128, 2, 128], F32, scp) for _ in range(4)]
                bGc = bufs(4, "Gc")
                yst = [fw.sbuf("yst", [128, 4, 128], F32, scp) for _ in range(2)]
                byst = bufs(2, "yst")
                bcars = [[Buf("carry") for _ in range(16)] for _ in range(2)]
                uview = p.uTb.rearrange("(c q) t -> q c t", q=128)
                for d in range(2):
                    order = list(range(NB)) if d == 0 else [1, 0] + list(range(NB - 1, 1, -1))
                    U = Uf if d == 0 else Ur
                    L = 127 if d == 0 else 0
                    ydst = (p.yF if d == 0 else p.yR).rearrange("(c q) t -> q c t", q=128)
                    fw.op(dv, lambda e: e.memset(carry[:, d], 0.0), writes=bcars[d])

                    def stage_a(n):
                        blk = order[n]
                        u_, bu_ = ub[n % 2], bub[n % 2]
                        fw.dma(fw.sp, u_[:], uview[:, :, blk * 128:(blk + 1) * 128], writes=[bu_], sbuf_side=bu_)
                        for fc in range(4):
                            bre, bbre = p.banks[0], p.bb[0]
                            bim, bbim = p.banks[1], p.bb[1]
                            fw.op(fw.pe, lambda e: e.matmul(bre[:, :], lhsT=u_[:, fc, :], rhs=RB[d][:, fc, 0:512], start=True, stop=True),
                                  reads=[bu_, btab], writes=[bbre])
                            fw.op(fw.pe, lambda e: e.matmul(bim[:, :], lhsT=u_[:, fc, :], rhs=RB[d][:, fc, 512:1024], start=True, stop=True),
                                  reads=[bu_, btab], writes=[bbim])
                            pc_ = PCt[d][:, fc * 512:(fc + 1) * 512]
                            ps_ = PSt[d][:, fc * 512:(fc + 1) * 512]
                            z, bz = Z[4 * (n % 2) + fc], bZ[4 * (n % 2) + fc]
                            P4, bP4 = P4a[4 * (fc % 2):4 * (fc % 2) + 4], bP4a[4 * (fc % 2):4 * (fc % 2) + 4]
                            fw.op(dv, lambda e: e.tensor_tensor(out=P4[0][:], in0=bre[:, :], in1=pc_, op=ALU.mult), reads=[bbre, btab], writes=[bP4[0]])
                            fw.op(dv, lambda e: e.tensor_tensor(out=P4[1][:], in0=bim[:, :], in1=ps_, op=ALU.mult), reads=[bbim, btab], writes=[bP4[1]])
                            fw.op(pl, lambda e: e.tensor_tensor(out=z[:, 0, :], in0=P4[0][:], in1=P4[1][:], op=ALU.subtract), reads=[bP4[0], bP4[1]], writes=[bz])
                            fw.op(dv, lambda e: e.tensor_tensor(out=P4[2][:], in0=bim[:, :], in1=pc_, op=ALU.mult), reads=[bbim, btab], writes=[bP4[2]])
                            fw.op(dv, lambda e: e.tensor_tensor(out=P4[3][:], in0=bre[:, :], in1=ps_, op=ALU.mult), reads=[bbre, btab], writes=[bP4[3]])
                            fw.op(pl, lambda e: e.tensor_tensor(out=z[:, 1, :], in0=P4[2][:], in1=P4[3][:], op=ALU.add), reads=[bP4[2], bP4[3]], writes=[bz])

                    def stage_b(n):
                        for q in range(16):
                            fc, ql = q // 4, q % 4
                            z, bz = Z[4 * (n % 2) + fc], bZ[4 * (n % 2) + fc]
                            gb, bgb = p.banks[2 + q % 4], p.bb[2 + q % 4]
                            fw.group(fw.pe, [(lambda e, ri=ri: e.matmul(gb[:, ri * 128:(ri + 1) * 128], lhsT=z[:, ri, ql * 128:(ql + 1) * 128], rhs=U[:],
                                                                         start=True, stop=True)) for ri in range(2)],
                                     reads=[bz, bs], writes=[bgb])
                            pp, bpp = Pp[q % 4], bPp[q % 4]
                            cre, cim = carry[:, d, q, 0:1], carry[:, d, q, 1:2]
                            qc2 = QC[d][:, q, :].unsqueeze(1).broadcast_to([128, 2, 128])
                            qs2 = QS[d][:, q, :].unsqueeze(1).broadcast_to([128, 2, 128])
                            bcar = bcars[d][q]
                            gc, bgc = Gc[q % 4], bGc[q % 4]
                            fw.op(ac, lambda e: e.activation(out=gc[:, 0, :], in_=gb[:, 0:128], func=AF.Identity, bias=cre), reads=[bgb, bcar], writes=[bgc])
                            fw.op(ac, lambda e: e.activation(out=gc[:, 1, :], in_=gb[:, 128:256], func=AF.Identity, bias=cim), reads=[bgb, bcar], writes=[bgc])
                            fw.op(dv, lambda e: e.tensor_tensor(out=pp[:, 0:2, :], in0=gc[:, :, :], in1=qc2, op=ALU.mult), reads=[bgc, btab], writes=[bpp])
                            fw.op(dv, lambda e: e.tensor_tensor(out=pp[:, 2:4, :], in0=gc[:, :, :], in1=qs2, op=ALU.mult), reads=[bgc, btab], writes=[bpp])
                            h, bh = Hh[16 * (n % 2) + q], bH[16 * (n % 2) + q]
                            fw.op(pl, lambda e: e.tensor_tensor(out=h[:, 0, :], in0=pp[:, 0, :], in1=pp[:, 3, :], op=ALU.subtract), reads=[bpp], writes=[bh])
                            fw.op(pl, lambda e: e.tensor_tensor(out=h[:, 1, :], in0=pp[:, 1, :], in1=pp[:, 2, :], op=ALU.add), reads=[bpp], writes=[bh])
                            fw.op(dv, lambda e: e.tensor_tensor(out=cre, in0=pp[:, 0, L:L + 1], in1=pp[:, 3, L:L + 1], op=ALU.subtract), reads=[bpp, bcar], writes=[bcar])
                            fw.op(dv, lambda e: e.tensor_tensor(out=cim, in0=pp[:, 1, L:L + 1], in1=pp[:, 2, L:L + 1], op=ALU.add), reads=[bpp, bcar], writes=[bcar])

                    def stage_c(n):
                        blk = order[n]
                        ybank, bybank = p.banks[6 + n % 2], p.bb[6 + n % 2]
                        ys, bys = yst[n % 2], byst[n % 2]
                        for fc in range(4):
                            fw.group(fw.pe, [(lambda e, ql=ql, ri=ri: e.matmul(ybank[:, fc * 128:(fc + 1) * 128], lhsT=CT[d][:, 4 * fc + ql, ri, :],
                                                                               rhs=Hh[16 * (n % 2) + 4 * fc + ql][:, ri, :], start=(ql == 0 and ri == 0),
                                                                               stop=(ql == 3 and ri == 1))) for ql in range(4) for ri in range(2)],
                                     reads=[btab] + [bH[16 * (n % 2) + 4 * fc + ql] for ql in range(4)], writes=[bybank])
                        fw.op(ac, lambda e: e.activation(out=ys[:], in_=ybank[:, :].rearrange("p (c t) -> p c t", c=4), func=AF.Copy), reads=[bybank], writes=[bys])
                        fw.dma(fw.act, ydst[:, :, blk * 128:(blk + 1) * 128], ys[:], reads=[bys], sbuf_side=bys)

                    stage_a(0)
                    for n in range(len(order)):
                        if n + 1 < len(order):
                            stage_a(n + 1)
                        stage_b(n)
                        stage_c(n)
            fw.barrier()
            gw = fw.sbuf("gw", [128, 4, 512], BF16, sc)
            bgw = Buf("gw")
            fw.dma(fw.sp, gw[:], p.gluS[i], reads=[p.wbuf["glu%d" % i]], writes=[bgw], sbuf_side=bgw)
            A_ = [fw.sbuf("tA", [128, 4, 512], F32, sc) for _ in range(3)]
            bA = bufs(3, "tA")
            y3 = fw.sbuf("y3", [128, 4, 512], F32, sc)
            y3b = fw.sbuf("y3b", [128, 4, 512], BF16, sc)
            so = fw.sbuf("so", [128, 4, 512], BF16, sc)
            by3, by3b, bso = Buf("y3"), Buf("y3b"), Buf("so")
            for (t0, N, is_ctx) in TILES:
                vF = p.yF.rearrange("(c q) t -> q c t", q=128)[:, :, t0:t0 + N]
                vR = p.yR.rearrange("(c q) t -> q c t", q=128)[:, :, t0:t0 + N]
                vU = p.uT.rearrange("(c q) t -> q c t", q=128)[:, :, t0:t0 + N]
                fw.dma(fw.sp, A_[0][:, :, :N], vF, writes=[bA[0]], sbuf_side=bA[0])
                fw.dma(fw.sp, A_[1][:, :, :N], vR, writes=[bA[1]], sbuf_side=bA[1])
                fw.dma(fw.sp, A_[2][:, :, :N], vU, writes=[bA[2]], sbuf_side=bA[2])
                fw.op(pl, lambda e: e.tensor_tensor(out=A_[0][:, :, :N], in0=A_[0][:, :, :N], in1=A_[1][:, :, :N], op=ALU.add), reads=[bA[0], bA[1]], writes=[bA[0]])
                for fc in range(4):
                    fw.op(dv, lambda e: e.scalar_tensor_tensor(out=A_[0][:, fc, :N], in0=A_[2][:, fc, :N], scalar=p.s5d[:, i, 0, fc:fc + 1],
                                                               in1=A_[0][:, fc, :N], op0=ALU.mult, op1=ALU.add), reads=[bA[0], bA[2], p.bsmall], writes=[bA[0]])
                y2 = A_[0]
                fw.op(ac, lambda e: e.activation(out=A_[1][:, :, :N], in_=y2[:, :, :N], func=AF.Square), reads=[bA[0]], writes=[bA[1]])
                fw.op(dv, lambda e: e.tensor_scalar(out=A_[1][:, :, :N], in0=A_[1][:, :, :N], scalar1=0.044715, scalar2=1.0, op0=ALU.mult, op1=ALU.add),
                      reads=[bA[1]], writes=[bA[1]])
                fw.op(dv, lambda e: e.tensor_tensor(out=A_[1][:, :, :N], in0=A_[1][:, :, :N], in1=y2[:, :, :N], op=ALU.mult), reads=[bA[0], bA[1]], writes=[bA[1]])
                fw.op(ac, lambda e: e.activation(out=A_[1][:, :, :N], in_=A_[1][:, :, :N], func=AF.Sigmoid, scale=2.0 * math.sqrt(2.0 / PI)),
                      reads=[bA[1]], writes=[bA[1]])
                fw.op(dv, lambda e: e.tensor_tensor(out=y3[:, :, :N], in0=A_[1][:, :, :N], in1=y2[:, :, :N], op=ALU.mult), reads=[bA[0], bA[1]], writes=[by3])
                fw.op(pl, lambda e: e.tensor_copy(out=y3b[:, :, :N], in_=y3[:, :, :N]), reads=[by3], writes=[by3b])
                for fo in range(4):
                    bank, bbk = p.banks[fo % 2], p.bb[fo % 2]
                    fw.group(fw.pe, [(lambda e, fi=fi: e.matmul(bank[:, :N], lhsT=gw[:, fi, fo * 128:(fo + 1) * 128], rhs=y3b[:, fi, :N],
                                                                start=(fi == 0), stop=(fi == 3))) for fi in range(4)],
                             reads=[bgw, by3b], writes=[bbk])
                    fw.op(ac, lambda e: e.activation(out=A_[2][:, fo, :N], in_=bank[:, :N], func=AF.Sigmoid, bias=p.s5d[:, i, 1, fo:fo + 1]),
                          reads=[bbk, p.bsmall], writes=[bA[2]])
                fw.op(dv, lambda e: e.tensor_tensor(out=so[:, :, :N], in0=A_[2][:, :, :N], in1=y3[:, :, :N], op=ALU.mult), reads=[bA[2], by3], writes=[bso])
                fw.dma(fw.act, p.s5T.rearrange("(c q) t -> q c t", q=128)[:, :, t0:t0 + N], so[:, :, :N], reads=[bso], sbuf_side=bso)

    def lam_prep(self, i):
        p, fw = self, self.fw
        lam_init = 0.8 - 0.6 * math.exp(-0.3 * i)
        with fw.scope() as sc:
            dl = fw.sbuf("dl", [1, 256], F32, sc)
            pr = fw.sbuf("pr", [1, 128], F32, sc)
            sm = fw.sbuf("sm", [1, 4], F32, sc)
            one1 = fw.sbuf("one1", [1, 128], F32, sc)
            b = Buf("lam")
            fw.dma(fw.sp, dl[:], p.dlam[i], writes=[b], sbuf_side=b)
            fw.op(fw.dve, lambda e: e.memset(one1[:], 1.0), reads=[b], writes=[b])
            dl4 = dl[:, :].rearrange("p (k c) -> p k c", k=4)
            fw.op(fw.dve, lambda e: e.tensor_tensor(out=pr[:, 0:64], in0=dl4[:, 0, :], in1=dl4[:, 1, :], op=ALU.mult), reads=[b], writes=[b])
            fw.op(fw.dve, lambda e: e.tensor_tensor(out=pr[:, 64:128], in0=dl4[:, 2, :], in1=dl4[:, 3, :], op=ALU.mult), reads=[b], writes=[b])
            fw.op(fw.dve, lambda e: e.reduce_sum(out=sm[:, 0:2], in_=pr[:, :].rearrange("p (k c) -> p k c", k=2), axis=mybir.AxisListType.X),
                  reads=[b], writes=[b])
            fw.op(fw.act, lambda e: e.activation(out=sm[:, 0:2], in_=sm[:, 0:2], func=AF.Exp), reads=[b], writes=[b])
            fw.op(fw.dve, lambda e: e.tensor_tensor(out=sm[:, 2:3], in0=sm[:, 1:2], in1=sm[:, 0:1], op=ALU.subtract), reads=[b], writes=[b])
            fw.op(fw.dve, lambda e: e.tensor_scalar(out=sm[:, 2:3], in0=sm[:, 2:3], scalar1=-lam_init, scalar2=None, op0=ALU.add), reads=[b], writes=[b])
            bank, bbk = p.banks[0], p.bb[0]
            fw.op(fw.pe, lambda e: e.matmul(bank[:, 0:1], lhsT=one1[:], rhs=sm[:, 2:3], start=True, stop=True), reads=[b], writes=[bbk])
            fw.op(fw.dve, lambda e: e.tensor_copy(out=p.lamc[:, 0:1], in_=bank[:, 0:1]), reads=[bbk], writes=[p.blamc])
            fw.op(fw.dve, lambda e: e.tensor_scalar(out=p.lamc[:, 1:2], in0=p.hg[:, i, 0:1], scalar1=1.0 - lam_init, scalar2=None, op0=ALU.mult),
                  reads=[p.bsmall, p.blamc], writes=[p.blamc])

    def pass_b(self, i, last):
        p, fw = self, self.fw
        p.lam_prep(i)
        with fw.scope() as sc:
            p.pass_alloc(sc)
            ws = p.ws
            mix = fw.sbuf("mix", [128, DC, 512], BF16, sc)
            bmix = bufs(DC, "mix")
            tiles = [t for t in TILES if not (last and t[2])]
            wbo = p.wbuf["wout%d" % i]
            for _ in tiles:
                for dg in range(4):
                    ws.enqueue(lambda t, b, dg=dg: fw.dma(fw.sp, t[:, :].rearrange("p (c kc n) -> p c kc n", c=4, kc=DC), p.woutS[i][dg],
                                                           reads=[wbo], writes=[b], sbuf_side=b))
                p.enqueue_ffn(i, 1)
            for (t0, N, is_ctx) in tiles:
                c = 1 if is_ctx else 0
                fw.dma(fw.sp, p.xt[:, :, :N], p.xT.rearrange("(kc q) t -> q kc t", q=128)[:, :, t0:t0 + N], writes=p.bxt, sbuf_side=p.bxt[0])
                fw.dma(fw.sp, mix[:, 0:4, :N], p.s5T.rearrange("(c q) t -> q c t", q=128)[:, :, t0:t0 + N], writes=bmix[0:4], sbuf_side=bmix[0])
                with fw.scope() as sc2:
                    p.attention(i, t0, N, is_ctx, mix, bmix, sc2)
                for dg in range(4):
                    t, b = ws.take()
                    tv = t[:, :].rearrange("p (c kc n) -> p c kc n", c=4, kc=DC)
                    for ci in range(4):
                        dc = dg * 4 + ci
                        yb, byb = p.banks[4 + dc % 2], p.bb[4 + dc % 2]
                        fw.group(fw.pe, [(lambda e, kc=kc: e.matmul(yb[:, :N], lhsT=tv[:, ci, kc, :], rhs=mix[:, kc, :N], start=(kc == 0), stop=(kc == DC - 1)))
                                         for kc in range(DC)], reads=[b] + bmix, writes=[byb])
                        fw.op(fw.dve, lambda e: e.scalar_tensor_tensor(out=p.xt[:, dc, :N], in0=yb[:, :N], scalar=p.Gcol[:, 1, dc, c:c + 1],
                                                                      in1=p.xt[:, dc, :N], op0=ALU.mult, op1=ALU.add),
                              reads=[byb, p.bcols, p.bxt[dc]], writes=[p.bxt[dc]])
                with fw.scope() as sc2:
                    act = fw.sbuf("act", [128, FC, 512], BF16, sc2)
                    bact = bufs(FC, "act")
                    p.rmsnorm_mod(i, 2, c, N)
                    p.ffn(i, 1, 2, c, N, act, bact)
                if not last:
                    fw.dma(fw.act, p.xT.rearrange("(kc q) t -> q kc t", q=128)[:, :, t0:t0 + N], p.xt[:, :, :N], reads=p.bxt, sbuf_side=p.bxt[0])
                else:
                    with fw.scope() as sc2:
                        p.final_out(t0, N, sc2)

    def final_out(self, t0, N, sc):
        p, fw = self, self.fw
        ssb, bss = p.banks[6], p.bb[6]
        for kc in range(DC):
            s_, bs = p.sq[kc % 2], p.bsq[kc % 2]
            fw.op(fw.act, lambda e: e.activation(out=s_[:, :N], in_=p.xt[:, kc, :N], func=AF.Square), reads=[p.bxt[kc]], writes=[bs])
            fw.op(fw.pe, lambda e: e.matmul(ssb[:, :N], lhsT=p.ones[:], rhs=s_[:, :N], start=(kc == 0), stop=(kc == DC - 1)),
                  reads=[bs, p.bconst], writes=[bss])
        fw.op(fw.act, lambda e: e.activation(out=p.rs[:, :N], in_=ssb[:, :N], func=AF.Sqrt, scale=1.0 / D, bias=p.epsc[:]),
              reads=[bss, p.bconst], writes=[p.brs])
        fw.op(fw.dve, lambda e: e.reciprocal(out=p.rs[:, :N], in_=p.rs[:, :N]), reads=[p.brs], writes=[p.brs])
        for kc in range(DC):
            fw.op(fw.dve, lambda e: e.scalar_tensor_tensor(out=p.xt[:, kc, :N], in0=p.xt[:, kc, :N], scalar=p.fgT[:, kc:kc + 1], in1=p.rs[:, :N],
                                                          op0=ALU.mult, op1=ALU.mult), reads=[p.bxt[kc], p.brs, p.bsmall], writes=[p.bxt[kc]])
        otok = [fw.sbuf("otok", [128, D], F32, sc) for _ in range(2)]
        bot = bufs(2, "otok")
        nb = 0
        for blk in range(N // 128):
            ot, bo = otok[blk % 2], bot[blk % 2]
            for dg in range(4):
                bank, bbk = p.banks[nb % 4], p.bb[nb % 4]
                nb += 1
                fw.group(fw.pe, [(lambda e, q=q: e.transpose(out=bank[:, q * 128:(q + 1) * 128], in_=p.xt[:, dg * 4 + q, blk * 128:(blk + 1) * 128],
                                                             identity=p.ident[:])) for q in range(4)],
                         reads=p.bxt[dg * 4:(dg + 1) * 4] + [p.bconst], writes=[bbk])
                if dg % 2 == 0:
                    fw.op(fw.act, lambda e: e.activation(out=ot[:, dg * 512:(dg + 1) * 512], in_=bank[:, :], func=AF.Copy), reads=[bbk], writes=[bo])
                else:
                    fw.op(fw.dve, lambda e: e.tensor_copy(out=ot[:, dg * 512:(dg + 1) * 512], in_=bank[:, :]), reads=[bbk], writes=[bo])
            r0 = t0 - CTX + blk * 128
            fw.dma(fw.act, p.out[r0:r0 + 128, :], ot[:], reads=[bo], sbuf_side=bo)

    def attention(self, i, t0, N, is_ctx, mix, bmix, sc):
        p, fw = self, self.fw
        nkb = 2 if is_ctx else NB
        nk = nkb * 128
        qd = fw.sbuf("qd", [128, 4, 512], BF16, sc)
        qg = fw.sbuf("qg", [128, 8, 512], BF16, sc)
        bq = Buf("q")
        fw.dma(fw.sp, qd[:, :, :N], p.qd.rearrange("(c q) t -> q c t", q=128)[:, :, t0:t0 + N], writes=[bq], sbuf_side=bq)
        bq2 = Buf("q2")
        fw.dma(fw.sp, qg[:, :, :N], p.qg.rearrange("(c q) t -> q c t", q=128)[:, :, t0:t0 + N], writes=[bq2], sbuf_side=bq2)
        KT = [fw.sbuf("KT", [128, S], BF16, sc) for _ in range(2)]
        VV = [fw.sbuf("VV", [128, NB, 128], BF16, sc) for _ in range(2)]
        bkv = bufs(2, "kv")
        Pt = [fw.sbuf("Pt", [128, 2, 512], BF16, sc) for _ in range(3)]
        bPt = bufs(3, "Pt")
        Ps2 = [fw.sbuf("Ps2", [128, 512], BF16, sc) for _ in range(2)]
        bPs2 = bufs(2, "Ps2")
        rc = [fw.sbuf("rc", [128, 512], F32, sc) for _ in range(2)]
        brc = bufs(2, "rc")
        units = [("d", h) for h in range(4)] + [("g", kv) for kv in range(2)]

        def load_unit(u):
            kind, idx = units[u]
            k, b = u % 2, bkv[u % 2]
            ksrc = p.kd if kind == "d" else p.kg
            vsrc = p.vd if kind == "d" else p.vg
            fw.dma(fw.sp, KT[k][:, 0:nk], ksrc[idx * 128:(idx + 1) * 128, 0:nk], writes=[b], sbuf_side=b)
            fw.dma(fw.sp, VV[k][:, 0:nkb, :], vsrc[idx, :, 0:nkb, :], writes=[b], sbuf_side=b)
        npt = [0]

        def softmax_av(kt, vv, bkvu, qap, pbase, K, scale, obank, bob, dbank, bdb, readsq):
            npair = nkb // 2

            def s_pair(pi):
                sp, bsp0, bsp1 = p.bpair[pi % 2], p.bb[2 * (pi % 2)], p.bb[2 * (pi % 2) + 1]
                fw.group(fw.pe, [(lambda e, j=j: e.matmul(sp[:, j, :N], lhsT=kt[pbase:pbase + K, (2 * pi + j) * 128:(2 * pi + j + 1) * 128], rhs=qap,
                                                          start=True, stop=True)) for j in range(2)],
                         reads=[bkvu, readsq], writes=[bsp0, bsp1])
            s_pair(0)
            for pi in range(npair):
                if pi + 1 < npair:
                    s_pair(pi + 1)
                sp, bsp0, bsp1 = p.bpair[pi % 2], p.bb[2 * (pi % 2)], p.bb[2 * (pi % 2) + 1]
                pt, bpt = Pt[npt[0] % 3], bPt[npt[0] % 3]
                npt[0] += 1
                fw.op(fw.act, lambda e: e.activation(out=pt[:, :, :N], in_=sp[:, :, :N], func=AF.Exp, scale=scale), reads=[bsp0, bsp1], writes=[bpt])
                fw.group(fw.pe, [(lambda e, j=j: e.matmul(obank[:, :N], lhsT=vv[:, 2 * pi + j, :], rhs=pt[:, j, :N], start=(pi == 0 and j == 0),
                                                          stop=(pi == npair - 1 and j == 1))) for j in range(2)],
                         reads=[bkvu, bpt], writes=[bob])
                ps2, bps2 = Ps2[pi % 2], bPs2[pi % 2]
                fw.op(fw.dve, lambda e: e.tensor_tensor(out=ps2[:, :N], in0=pt[:, 0, :N], in1=pt[:, 1, :N], op=ALU.add), reads=[bpt], writes=[bps2])
                fw.op(fw.pe, lambda e: e.matmul(dbank[:, :N], lhsT=p.ones[:], rhs=ps2[:, :N], start=(pi == 0), stop=(pi == npair - 1)),
                      reads=[bps2, p.bconst], writes=[bdb])
        load_unit(0)
        for u, (kind, idx) in enumerate(units):
            if u + 1 < len(units):
                load_unit(u + 1)
            kt, vv, bkvu = KT[u % 2], VV[u % 2], bkv[u % 2]
            if kind == "d":
                h = idx
                for m in range(2):
                    softmax_av(kt, vv, bkvu, qd[64 * m:64 * m + 64, h, :N], 64 * m, 64, 0.125,
                               p.banks[4 + m], p.bb[4 + m], p.banks[6 + m], p.bb[6 + m], bq)
                t1, b1 = p.tf[0], p.btf[0]
                t2, b2 = p.tf[1], p.btf[1]
                for m, (tt, bt) in enumerate(((t1, b1), (t2, b2))):
                    fw.op(fw.dve, lambda e: e.reciprocal(out=rc[m][:, :N], in_=p.banks[6 + m][:, :N]), reads=[p.bb[6 + m]], writes=[brc[m]])
                    fw.op(fw.dve, lambda e: e.tensor_tensor(out=tt[:, :N], in0=p.banks[4 + m][:, :N], in1=rc[m][:, :N], op=ALU.mult),
                          reads=[p.bb[4 + m], brc[m]], writes=[bt])
                fw.op(fw.dve, lambda e: e.scalar_tensor_tensor(out=t1[:, :N], in0=t2[:, :N], scalar=p.lamc[:, 0:1], in1=t1[:, :N], op0=ALU.mult, op1=ALU.add),
                      reads=[b1, b2, p.blamc], writes=[b1])
                s_, bs = p.sq[h % 2], p.bsq[h % 2]
                ssb, bss = p.banks[0], p.bb[0]
                fw.op(fw.act, lambda e: e.activation(out=s_[:, :N], in_=t1[:, :N], func=AF.Square), reads=[b1], writes=[bs])
                fw.op(fw.pe, lambda e: e.matmul(ssb[:, :N], lhsT=p.ones[:], rhs=s_[:, :N], start=True, stop=True), reads=[bs, p.bconst], writes=[bss])
                fw.op(fw.act, lambda e: e.activation(out=p.rs[:, :N], in_=ssb[:, :N], func=AF.Sqrt, scale=1.0 / 128, bias=p.epsc[:]),
                      reads=[bss, p.bconst], writes=[p.brs])
                fw.op(fw.dve, lambda e: e.reciprocal(out=p.rs[:, :N], in_=p.rs[:, :N]), reads=[p.brs], writes=[p.brs])
                fw.op(fw.dve, lambda e: e.scalar_tensor_tensor(out=mix[:, 4 + h, :N], in0=t1[:, :N], scalar=p.lamc[:, 1:2], in1=p.rs[:, :N],
                                                              op0=ALU.mult, op1=ALU.mult), reads=[b1, p.brs, p.blamc], writes=[bmix[4 + h]])
            else:
                kv = idx
                for r in range(4):
                    h = kv * 4 + r
                    ob, bob = p.banks[4 + r % 2], p.bb[4 + r % 2]
                    db, bdb = p.banks[6 + r % 2], p.bb[6 + r % 2]
                    softmax_av(kt, vv, bkvu, qg[:, h, :N], 0, 128, 128.0 ** -0.5, ob, bob, db, bdb, bq2)
                    fw.op(fw.dve, lambda e: e.reciprocal(out=rc[r % 2][:, :N], in_=db[:, :N]), reads=[bdb], writes=[brc[r % 2]])
                    fw.op(fw.dve, lambda e: e.tensor_tensor(out=mix[:, 8 + h, :N], in0=ob[:, :N], in1=rc[r % 2][:, :N], op=ALU.mult),
                          reads=[bob, brc[r % 2]], writes=[bmix[8 + h]])

    def finish(self):
        self.fw.barrier()


def build(debug=None, upto="all", conv="all", ntiles=9, stop=None):
    p = Prog(debug)
    p.ntiles = ntiles
    p.stop = stop
    p.declare()
    fw = p.fw
    with fw.stack:
        p.setup()
        p.conv_filter = None if conv == "all" else conv.split(",")
        p.rope_tables()
        p.convert_layer(0)
        p.convert_layer(1)
        for i in range(2):
            if upto == "rope":
                break
            p.modulation(i)
            if upto == "mod":
                break
            p.pass_a(i)
            if upto == "A":
                break
            p.s5(i)
            if upto == "S5":
                break
            p.pass_b(i, i == 1)
            if upto == "B":
                break
        p.finish()
    return p


def host_inputs(inp):
    f = np.float32
    g = {k: np.asarray(v) for k, v in inp.items()}
    common = {}
    common["ada_w"] = np.ascontiguousarray(g["ada_w"], f)
    common["ada_bT"] = np.ascontiguousarray(g["ada_b"].reshape(2, 144, 128).transpose(0, 2, 1), f)
    common["norm_gT"] = np.ascontiguousarray(g["norm_g"].reshape(2, 3, DC, 128).transpose(0, 3, 1, 2), f)
    common["final_gT"] = np.ascontiguousarray(g["final_g"].reshape(DC, 128).T, f)
    common["ffn_w_gate"] = np.ascontiguousarray(g["ffn_w_gate"], f)
    common["ffn_w_up"] = np.ascontiguousarray(g["ffn_w_up"], f)
    common["ffn_w_down"] = np.ascontiguousarray(g["ffn_w_down"], f)
    w_in = np.ascontiguousarray(g["w_in"], f)
    common["w_in"] = w_in
    i64 = np.arange(512) ^ 16
    i128 = np.arange(1024) ^ 32
    k128 = np.arange(256) ^ 32
    common["w_in_rot"] = np.ascontiguousarray(np.concatenate([w_in[:, :, 512 + i64], w_in[:, :, 1024 + i64], w_in[:, :, 2048 + i128],
                                                              w_in[:, :, 3072 + k128]], axis=2), f)
    common["w_out"] = np.ascontiguousarray(g["w_out"], f)
    lam = np.stack([g["s5_lam_re"], g["s5_lam_im"], np.broadcast_to(g["s5_log_dt"][..., None], g["s5_lam_re"].shape)], axis=0)
    lamT = lam.reshape(3, 2, 2, 16, 128).transpose(1, 2, 4, 0, 3)
    common["lamT"] = np.ascontiguousarray(lamT, f)
    XB = np.zeros((2, 2, 2, 16, 128, 128), f)
    YC = np.zeros((2, 2, 2, 16, 128, 128), f)
    for ri, (bsrc, csrc) in enumerate([(g["s5_b_re"], g["s5_c_re"]), (g["s5_b_im"], g["s5_c_im"])]):
        for q in range(16):
            for e in range(2):
                gg = 2 * q + e
                gl = gg % 8
                XB[:, :, ri, q, e * 64:(e + 1) * 64, gl * 16:(gl + 1) * 16] = bsrc[:, :, gg]
                YC[:, :, ri, q, e * 64:(e + 1) * 64, gl * 16:(gl + 1) * 16] = csrc[:, :, gg].transpose(0, 1, 3, 2)
    common["XB"] = XB
    common["YC"] = YC
    common["s5dT"] = np.ascontiguousarray(np.stack([g["s5_d"].reshape(2, 4, 128), g["s5_glu_b"].reshape(2, 4, 128)], axis=1).transpose(0, 3, 1, 2), f)
    common["s5_glu_w"] = np.ascontiguousarray(g["s5_glu_w"], f)
    common["dlam"] = np.ascontiguousarray(g["diff_lam"].reshape(2, 1, 256), f)
    p32 = np.arange(128) ^ 32
    common["hgT"] = np.ascontiguousarray(np.stack([g["diff_subln_g"], g["gqa_q_g"], g["gqa_q_g"][:, p32], g["gqa_k_g"], g["gqa_k_g"][:, p32]], axis=2), f)
    maps = []
    for b in range(NCORES):
        m = dict(common)
        m["x"] = np.ascontiguousarray(g["x"][b], f)
        m["ctx"] = np.ascontiguousarray(g["ctx"][b], f)
        m["cT"] = np.ascontiguousarray(np.stack([g["c"][b].reshape(DC, 128).T, g["c_ctx"].reshape(DC, 128).T], axis=2), f)
        maps.append(m)
    idle = dict(common)
    for k in ("ada_w", "ffn_w_gate", "ffn_w_up", "ffn_w_down", "w_in", "w_in_rot", "w_out", "s5_glu_w", "XB", "YC"):
        idle[k] = np.zeros_like(common[k])
    idle["x"] = np.zeros_like(maps[0]["x"])
    idle["ctx"] = np.zeros_like(maps[0]["ctx"])
    idle["cT"] = np.zeros_like(maps[0]["cT"])
    full = []
    for b in range(NCORES):
        full.append(maps[b])
        full.append(idle)
    return full


_CACHE = {}


def kernel(**inputs):
    maps = host_inputs(inputs)
    if "p" not in _CACHE:
        _CACHE["p"] = build()
    p = _CACHE["p"]
    res = run_bass_kernel_spmd(p.nc, maps, core_ids=list(range(2 * NCORES)))
    return np.stack([np.asarray(res.results[2 * b]["out"], np.float32) for b in range(NCORES)], axis=0)
```

```python
import math
import numpy as np
import concourse.bass as bass
import concourse.mybir as mybir
from concourse.bass_utils import run_bass_kernel_spmd
from contextlib import ExitStack

F32 = mybir.dt.float32
BF16 = mybir.dt.bfloat16
I32 = mybir.dt.int32
ALU = mybir.AluOpType
AF = mybir.ActivationFunctionType

D = 2048
DC = 16
FF = 5632
FC = 44
LAT = 4096
CTX = 256
S = 4352
NB = 34
NCORES = 4
TILES = [(0, 256, True)] + [(256 + 512 * i, 512, False) for i in range(8)]
EPS = 1e-6
PI = math.pi


class Eng:
    def __init__(self, fw, name, h, is_pe=False):
        self.fw, self.name, self.h, self.is_pe = fw, name, h, is_pe
        self.nsem = 0
        self.newsem()
        self.seen = {}

    def newsem(self):
        self.sem = self.fw.stack.enter_context(self.fw.nc.semaphore("s_%s%d" % (self.name, self.nsem)))
        self.nsem += 1
        self.seq = 0


class Buf:
    __slots__ = ("name", "w", "r", "dsem", "dval", "excl")

    def __init__(self, name="", excl=False):
        self.name = name
        self.excl = excl
        self.w = None
        self.r = {}
        self.dsem = None
        self.dval = 0


def bufs(n, name=""):
    return [Buf(name + str(i)) for i in range(n)]


class FW:
    def __init__(self, nc):
        self.nc = nc
        self.stack = ExitStack()
        self.pe = Eng(self, "pe", nc.tensor, True)
        self.act = Eng(self, "act", nc.scalar)
        self.dve = Eng(self, "dve", nc.vector)
        self.pool = Eng(self, "pool", nc.gpsimd)
        self.sp = Eng(self, "sp", nc.sync)
        self.engs = [self.pe, self.act, self.dve, self.pool, self.sp]
        self.inflight = []
        self.nd = 0
        self.uid = 0
        self.sem_pool = []
        self.scopes = []

    def push_scope(self):
        self.scopes.append([])

    def pop_scope(self):
        for b in self.scopes.pop():
            if b.dsem is not None:
                self.sem_pool.append([b.dsem, b.dval])
                b.dsem = None

    def scope(self):
        return _Scope(self)

    def name(self, s):
        self.uid += 1
        return "%s_%d" % (s, self.uid)

    def sbuf(self, name, shape, dt, stack=None):
        return (stack or self.stack).enter_context(self.nc.sbuf_tensor(self.name(name), shape, dt))

    def psum(self, name, shape, dt):
        return self.stack.enter_context(self.nc.psum_tensor(self.name(name), shape, dt))

    def _wait(self, eng, dep):
        sem, val, ename = dep
        key = id(sem)
        if eng.seen.get(key, 0) >= val:
            return
        if eng.is_pe and ename == "pe":
            return
        eng.h.wait_ge(sem, val)
        eng.seen[key] = val

    def _deps(self, eng, reads, writes):
        for b in reads:
            if b.w is not None:
                self._wait(eng, b.w)
            if b.excl:
                for d in b.r.values():
                    self._wait(eng, d)
        for b in writes:
            if b.w is not None and b.w[2] != eng.name:
                self._wait(eng, b.w)
            for d in b.r.values():
                if d[2] != eng.name:
                    self._wait(eng, d)

    def _mark(self, d, key, reads, writes):
        for b in writes:
            b.w = d
            b.r = {}
        for b in reads:
            if b not in writes:
                b.r[key] = d

    def op(self, eng, fn, reads=(), writes=()):
        self._deps(eng, reads, writes)
        if eng.seq >= 30000:
            eng.newsem()
        ins = fn(eng.h)
        eng.seq += 1
        ins.then_inc(eng.sem, 1)
        self._mark((eng.sem, eng.seq, eng.name), eng.name, reads, writes)

    def group(self, eng, fns, reads=(), writes=()):
        self._deps(eng, reads, writes)
        if eng.seq >= 30000:
            eng.newsem()
        ins = None
        for f in fns:
            ins = f(eng.h)
        eng.seq += 1
        ins.then_inc(eng.sem, 1)
        self._mark((eng.sem, eng.seq, eng.name), eng.name, reads, writes)

    def dma(self, eng, out, in_, reads=(), writes=(), sbuf_side=None, track=True, **kw):
        self._deps(eng, reads, writes)
        ins = eng.h.dma_start(out=out, in_=in_, **kw)
        tgt = sbuf_side
        if tgt.dsem is None:
            if self.sem_pool and self.sem_pool[0][1] < 20000:
                tgt.dsem, tgt.dval = self.sem_pool.pop(0)
            else:
                tgt.dsem = self.stack.enter_context(self.nc.semaphore("d%d" % self.nd))
                self.nd += 1
                tgt.dval = 0
            if self.scopes:
                self.scopes[-1].append(tgt)
        tgt.dval += 16
        ins.then_inc(tgt.dsem, 16)
        d = (tgt.dsem, tgt.dval, "dma")
        self._mark(d, "dma%d" % id(tgt), reads, writes)
        if track:
            self.inflight.append(d)
        return d

    def barrier(self):
        deps = [(e.sem, e.seq, e.name) for e in self.engs if e.seq > 0] + self.inflight
        for e in self.engs:
            for d in deps:
                if d[2] != e.name:
                    self._wait(e, d)
        self.inflight = []


class _Scope:
    def __init__(self, fw):
        self.fw = fw
        self.es = ExitStack()

    def __enter__(self):
        self.fw.push_scope()
        self.es.__enter__()
        return self.es

    def __exit__(self, *a):
        if a[0] is None:
            self.fw.barrier()
            self.fw.pop_scope()
        return self.es.__exit__(*a)


class WStream:
    SLOT = 8192

    def __init__(self, fw, nslots, sc=None):
        self.fw = fw
        self.t = [fw.sbuf("ring", [128, self.SLOT], BF16, sc) for _ in range(nslots)]
        self.b = bufs(nslots, "ring")
        self.q = []
        self.issued = 0
        self.taken = 0

    def enqueue(self, fn):
        self.q.append(fn)

    def take(self):
        n = len(self.t)
        lim = min(self.taken + n - 1, len(self.q))
        while self.issued < lim:
            k = self.issued % n
            self.q[self.issued](self.t[k], self.b[k])
            self.issued += 1
        k = self.taken % n
        assert self.taken < self.issued
        self.taken += 1
        return self.t[k], self.b[k]


class Prog:
    def __init__(self, debug=None):
        self.debug = debug or ()
        self.nc = nc = bass.Bass("TRN2", target_bir_lowering=False)
        self.fw = fw = FW(nc)
        self.ext_in = {}
        self.ext_out = {}

    def din(self, name, shape, dt=F32):
        t = self.nc.dram_tensor(name, list(shape), dt, kind="ExternalInput").ap()
        self.ext_in[name] = t
        return t

    def dscratch(self, name, shape, dt):
        kind = "ExternalOutput" if name in self.debug else "Internal"
        t = self.nc.dram_tensor(name, list(shape), dt, kind=kind).ap()
        return t

    def declare(self):
        p = self
        p.x = p.din("x", [LAT, D])
        p.ctx = p.din("ctx", [CTX, D])
        p.cT = p.din("cT", [128, DC, 2])
        p.ada_w = p.din("ada_w", [2, D, 9 * D])
        p.ada_bT = p.din("ada_bT", [2, 128, 144])
        p.norm_gT = p.din("norm_gT", [2, 128, 3, DC])
        p.final_gT = p.din("final_gT", [128, DC])
        p.w_gate = p.din("ffn_w_gate", [2, 2, D, FF])
        p.w_up = p.din("ffn_w_up", [2, 2, D, FF])
        p.w_down = p.din("ffn_w_down", [2, 2, FF, D])
        p.w_in = p.din("w_in", [2, D, 3584])
        p.w_in_rot = p.din("w_in_rot", [2, D, 2304])
        p.w_out = p.din("w_out", [2, D, D])
        p.lamT = p.din("lamT", [2, 2, 128, 3, 16])
        p.XB = p.din("XB", [2, 2, 2, 16, 128, 128])
        p.YC = p.din("YC", [2, 2, 2, 16, 128, 128])
        p.s5dT = p.din("s5dT", [2, 128, 2, 4])
        p.glu_w = p.din("s5_glu_w", [2, 512, 512])
        p.dlam = p.din("dlam", [2, 1, 256])
        p.hgT = p.din("hgT", [2, 128, 5])
        p.out = p.nc.dram_tensor("out", [LAT, D], F32, kind="ExternalOutput").ap()
        p.xT = p.dscratch("xT", [D, S], F32)
        p.wguS = [[p.dscratch("wgu%d%d" % (i, f), [FC // 2, 128, 2, DC, 256], BF16) for f in range(2)] for i in range(2)]
        p.wdS = [[p.dscratch("wd%d%d" % (i, f), [DC, 128, FC, 128], BF16) for f in range(2)] for i in range(2)]
        p.winS = [p.dscratch("win%d" % i, [10, 128, 4, DC, 128], BF16) for i in range(2)]
        p.wvdS = [p.dscratch("wvd%d" % i, [128, DC, 512], BF16) for i in range(2)]
        p.wvgS = [p.dscratch("wvg%d" % i, [128, DC, 256], BF16) for i in range(2)]
        p.woutS = [p.dscratch("wout%d" % i, [4, 128, 4, DC, 128], BF16) for i in range(2)]
        p.adaS = [p.dscratch("ada%d" % i, [36, 128, DC, 512], BF16) for i in range(2)]
        p.gluS = [p.dscratch("glu%d" % i, [128, 4, 512], BF16) for i in range(2)]
        p.uT = p.dscratch("uT", [512, S], F32)
        p.uTb = p.dscratch("uTb", [512, S], BF16)
        p.qd = p.dscratch("qd", [512, S], BF16)
        p.kd = p.dscratch("kd", [512, S], BF16)
        p.vd = p.dscratch("vd", [4, 128, NB, 128], BF16)
        p.qg = p.dscratch("qg", [1024, S], BF16)
        p.kg = p.dscratch("kg", [256, S], BF16)
        p.vg = p.dscratch("vg", [2, 128, NB, 128], BF16)
        p.yF = p.dscratch("yF", [512, S], F32)
        p.yR = p.dscratch("yR", [512, S], F32)
        p.s5T = p.dscratch("s5T", [512, S], BF16)
        p.ropeT = p.dscratch("ropeT", [4, 128, LAT], F32)
        p.mdbg = p.dscratch("mdbg", [128, 144, 2], F32)
        p.wbuf = {}

    def conv(self, key, out, in_):
        fw = self.fw
        if self.conv_filter is not None and not any(key.startswith(k) for k in self.conv_filter):
            return
        b = self.wbuf.setdefault(key, Buf(key))
        fw.dma(fw.pool, out, in_, writes=[b], sbuf_side=b, track=False)

    def convert_layer(self, i):
        p = self
        for blk in range(36):
            p.conv("ada%d" % i, p.adaS[i][blk], p.ada_w[i][:, blk * 512:(blk + 1) * 512].rearrange("(kc p) n -> p kc n", p=128))
        self.convert_ffn(i, 0)
        wi, wr = p.w_in[i], p.w_in_rot[i]

        def colchunk(src, c0):
            return src[:, c0:c0 + 128].rearrange("(kc p) n -> p kc n", p=128)
        groups = [
            [(wi, 0), (wi, 128), (wi, 256), (wi, 384)],
            [(wi, 512 + 128 * k) for k in range(4)],
            [(wr, 0 + 128 * k) for k in range(4)],
            [(wi, 1024 + 128 * k) for k in range(4)],
            [(wr, 512 + 128 * k) for k in range(4)],
            [(wi, 2048 + 128 * k) for k in range(4)],
            [(wr, 1024 + 128 * k) for k in range(4)],
            [(wi, 2560 + 128 * k) for k in range(4)],
            [(wr, 1536 + 128 * k) for k in range(4)],
            [(wi, 3072), (wi, 3200), (wr, 2048), (wr, 2176)],
        ]
        for g, lst in enumerate(groups):
            if g in (2, 4, 6, 8):
                continue
            for ci, (src, c0) in enumerate(lst):
                if g == 9 and ci >= 2:
                    continue
                p.conv("win%d" % i, p.winS[i][g, :, ci], colchunk(src, c0))
        p.conv("wv%d" % i, p.wvdS[i], wi[:, 1536:2048].rearrange("(kc p) n -> p kc n", p=128))
        p.conv("wv%d" % i, p.wvgS[i], wi[:, 3328:3584].rearrange("(kc p) n -> p kc n", p=128))
        p.conv("glu%d" % i, p.gluS[i], p.glu_w[i].rearrange("(fi p) n -> p fi n", p=128))
        for dg in range(4):
            for ci in range(4):
                dc = dg * 4 + ci
                p.conv("wout%d" % i, p.woutS[i][dg, :, ci], p.w_out[i][:, dc * 128:(dc + 1) * 128].rearrange("(kc p) n -> p kc n", p=128))
        self.convert_ffn(i, 1)

    def convert_ffn(self, i, f):
        p = self
        for jp in range(FC // 2):
            p.conv("wgu%d%d_%d" % (i, f, jp // 6), p.wguS[i][f][jp, :, 0], p.w_gate[i, f][:, jp * 256:(jp + 1) * 256].rearrange("(kc p) n -> p kc n", p=128))
            p.conv("wgu%d%d_%d" % (i, f, jp // 6), p.wguS[i][f][jp, :, 1], p.w_up[i, f][:, jp * 256:(jp + 1) * 256].rearrange("(kc p) n -> p kc n", p=128))
        for dc in range(DC):
            for h2 in range(2):
                p.conv("wd%d%d_%d" % (i, f, dc // 8), p.wdS[i][f][dc, :, h2 * 22:(h2 + 1) * 22],
                       p.w_down[i, f][h2 * 2816:(h2 + 1) * 2816, dc * 128:(dc + 1) * 128].rearrange("(j p) n -> p j n", p=128))

    def setup(self):
        p, fw, nc = self, self.fw, self.nc
        p.bpair = [fw.psum("bpair", [128, 2, 512], F32) for _ in range(4)]
        p.banks = [p.bpair[k // 2][:, k % 2, :] for k in range(8)]
        p.bb = [Buf("bank%d" % k, excl=True) for k in range(8)]
        p.ones = fw.sbuf("ones", [128, 128], BF16)
        p.bconst = Buf("const")
        p.ident = fw.sbuf("ident", [128, 128], F32)
        p.ones32 = fw.sbuf("ones32", [128, 128], F32)
        p.perm = fw.sbuf("perm", [128, 2, 128], BF16)
        p.epsc = fw.sbuf("epsc", [128, 1], F32)
        p.negpi = fw.sbuf("negpi", [128, 1], F32)
        p.mT = fw.sbuf("mT", [128, 144, 2], F32)
        p.bmT = Buf("mT")
        p.Acol = fw.sbuf("Acol", [128, 3, DC, 2], F32)
        p.Gcol = fw.sbuf("Gcol", [128, 3, DC, 2], F32)
        p.bcols = Buf("cols")
        p.ngT = fw.sbuf("ngT", [128, 2, 3, DC], F32)
        p.fgT = fw.sbuf("fgT", [128, DC], F32)
        p.hg = fw.sbuf("hg", [128, 2, 5], F32)
        p.s5d = fw.sbuf("s5d", [128, 2, 2, 4], F32)
        p.sq = [fw.sbuf("sq", [128, 512], BF16) for _ in range(2)]
        p.bsq = bufs(2, "sq")
        p.tf = [fw.sbuf("tf", [128, 512], F32) for _ in range(4)]
        p.btf = bufs(4, "tf")
        p.rs = fw.sbuf("rs", [128, 512], F32)
        p.brs = Buf("rs")
        p.lamc = fw.sbuf("lamc", [128, 4], F32)
        p.blamc = Buf("lamc")
        bc = p.bconst
        fw.op(fw.pool, lambda e: e.memset(p.ones[:], 1.0), writes=[bc])
        fw.op(fw.pool, lambda e: e.memset(p.ones32[:], 1.0), writes=[bc])
        fw.op(fw.pool, lambda e: e.memset(p.ident[:], 0.0), writes=[bc])
        fw.op(fw.pool, lambda e: e.affine_select(out=p.ident[:], in_=p.ident[:], pattern=[[-1, 128]], compare_op=ALU.not_equal,
                                                  fill=1.0, base=0, channel_multiplier=1), reads=[bc], writes=[bc])
        for b_ in range(4):
            fw.op(fw.pool, lambda e, b_=b_: e.tensor_copy(out=p.perm[:, 0, 32 * b_:32 * b_ + 32], in_=p.ident[:, 32 * (b_ ^ 1):32 * (b_ ^ 1) + 32]),
                  reads=[bc], writes=[bc])
        for b_ in range(8):
            fw.op(fw.pool, lambda e, b_=b_: e.tensor_copy(out=p.perm[:, 1, 16 * b_:16 * b_ + 16], in_=p.ident[:, 16 * (b_ ^ 1):16 * (b_ ^ 1) + 16]),
                  reads=[bc], writes=[bc])
        fw.op(fw.pool, lambda e: e.memset(p.epsc[:], EPS), writes=[bc])
        fw.op(fw.pool, lambda e: e.memset(p.negpi[:], -PI), writes=[bc])
        bl = Buf("smallloads")
        fw.dma(fw.sp, p.ngT[:], p.norm_gT.rearrange("i p k c -> p i k c"), writes=[bl], sbuf_side=bl)
        fw.dma(fw.sp, p.fgT[:], p.final_gT, writes=[bl], sbuf_side=bl)
        fw.dma(fw.sp, p.hg[:], p.hgT.rearrange("i p k -> p i k"), writes=[bl], sbuf_side=bl)
        fw.dma(fw.sp, p.s5d[:], p.s5dT.rearrange("i p a k -> p i a k"), writes=[bl], sbuf_side=bl)
        p.bsmall = bl

    def rope_tables(self):
        p, fw = self, self.fw
        with self.fw.scope() as sc:
            di = fw.sbuf("di", [128, 1], F32, sc)
            col = fw.sbuf("col", [128, 8], F32, sc)
            prow = fw.sbuf("prow", [128, LAT], F32, sc)
            pcol = fw.sbuf("pcol", [128, LAT], F32, sc)
            ang = fw.sbuf("ang", [128, LAT], F32, sc)
            tb = fw.sbuf("tb", [128, LAT], F32, sc)
            prow2 = fw.sbuf("prow2", [128, LAT], F32, sc)
            b = Buf("rope")
            fw.op(fw.pool, lambda e: e.iota(prow[:].rearrange("p (r c) -> p r c", c=64), pattern=[[1, 64], [0, 64]], base=0,
                                            channel_multiplier=0, allow_small_or_imprecise_dtypes=True), writes=[b])
            fw.op(fw.pool, lambda e: e.iota(pcol[:].rearrange("p (r c) -> p r c", c=64), pattern=[[0, 64], [1, 64]], base=0,
                                            channel_multiplier=0, allow_small_or_imprecise_dtypes=True), reads=[b], writes=[b])
            fw.op(fw.pool, lambda e: e.iota(di[:], pattern=[[0, 1]], base=0, channel_multiplier=1,
                                            allow_small_or_imprecise_dtypes=True), reads=[b], writes=[b])
            dv = fw.dve

            def o(fn):
                fw.op(dv, fn, reads=[b], writes=[b])
            MAGIC = 12582912.0
            o(lambda e: e.tensor_single_scalar(out=col[:, 3:4], in_=di[:], scalar=63.5, op=ALU.is_gt))
            o(lambda e: e.scalar_tensor_tensor(out=col[:, 6:7], in0=col[:, 3:4], scalar=-64.0, in1=di[:], op0=ALU.mult, op1=ALU.add))
            o(lambda e: e.tensor_single_scalar(out=col[:, 4:5], in_=col[:, 6:7], scalar=31.5, op=ALU.is_gt))
            o(lambda e: e.scalar_tensor_tensor(out=col[:, 7:8], in0=col[:, 4:5], scalar=-32.0, in1=col[:, 6:7], op0=ALU.mult, op1=ALU.add))
            o(lambda e: e.tensor_single_scalar(out=col[:, 5:6], in_=col[:, 7:8], scalar=15.5, op=ALU.is_gt))
            o(lambda e: e.scalar_tensor_tensor(out=col[:, 6:7], in0=col[:, 5:6], scalar=-16.0, in1=col[:, 7:8], op0=ALU.mult, op1=ALU.add))
            for kind in range(2):
                half = 32 if kind == 0 else 16
                jcol = col[:, 7:8] if kind == 0 else col[:, 6:7]
                rowb = col[:, 3:4] if kind == 0 else col[:, 4:5]
                sgnb = col[:, 4:5] if kind == 0 else col[:, 5:6]
                fw.op(fw.act, lambda e: e.activation(out=col[:, 0:1], in_=jcol, func=AF.Exp, scale=-math.log(10000.0) / half),
                      reads=[b], writes=[b])
                o(lambda e: e.tensor_scalar(out=col[:, 1:2], in0=rowb, scalar1=-1.0, scalar2=1.0, op0=ALU.mult, op1=ALU.add))
                o(lambda e: e.tensor_scalar(out=col[:, 2:3], in0=sgnb, scalar1=2.0, scalar2=-1.0, op0=ALU.mult, op1=ALU.add))
                o(lambda e: e.tensor_tensor(out=ang[:], in0=prow[:], in1=pcol[:], op=ALU.subtract))
                o(lambda e: e.scalar_tensor_tensor(out=ang[:], in0=ang[:], scalar=col[:, 1:2], in1=pcol[:], op0=ALU.mult, op1=ALU.add))
                o(lambda e: e.tensor_scalar(out=ang[:], in0=ang[:], scalar1=col[:, 0:1], scalar2=None, op0=ALU.mult))
                for which in range(2):
                    if which == 0:
                        o(lambda e: e.tensor_scalar(out=tb[:], in0=ang[:], scalar1=0.5 * PI, scalar2=None, op0=ALU.add))
                    else:
                        o(lambda e: e.tensor_copy(out=tb[:], in_=ang[:]))
                    o(lambda e: e.tensor_scalar(out=prow2[:], in0=tb[:], scalar1=1.0 / (2 * PI), scalar2=MAGIC, op0=ALU.mult, op1=ALU.add))
                    o(lambda e: e.tensor_scalar(out=prow2[:], in0=prow2[:], scalar1=-MAGIC, scalar2=None, op0=ALU.add))
                    o(lambda e: e.scalar_tensor_tensor(out=tb[:], in0=prow2[:], scalar=-2 * PI, in1=tb[:], op0=ALU.mult, op1=ALU.add))
                    o(lambda e: e.tensor_scalar(out=tb[:], in0=tb[:], scalar1=PI, scalar2=-PI, op0=ALU.min, op1=ALU.max))
                    fw.op(fw.act, lambda e: e.activation(out=tb[:], in_=tb[:], func=AF.Sin), reads=[b], writes=[b])
                    if which == 1:
                        o(lambda e: e.tensor_scalar(out=tb[:], in0=tb[:], scalar1=col[:, 2:3], scalar2=None, op0=ALU.mult))
                    fw.dma(fw.sp, p.ropeT[2 * kind + which], tb[:], reads=[b], sbuf_side=b)

    def pass_alloc(self, sc):
        p, fw = self, self.fw
        p.ws = WStream(fw, 4, sc)
        p.xt = fw.sbuf("xt", [128, DC, 512], F32, sc)
        p.bxt = bufs(DC, "xt")
        p.ht = fw.sbuf("ht", [128, DC, 512], BF16, sc)
        p.bht = bufs(DC, "ht")

    def modulation(self, i):
        p, fw = self, self.fw
        with self.fw.scope() as sc:
            ws = p.ws = WStream(fw, 4, sc)
            cs = fw.sbuf("cs", [128, DC, 2], F32, sc)
            sc_b = fw.sbuf("scb", [128, DC, 2], BF16, sc)
            adab = fw.sbuf("adab", [128, 144], F32, sc)
            bl = Buf("modl")
            fw.dma(fw.sp, cs[:], p.cT, writes=[bl], sbuf_side=bl)
            fw.dma(fw.sp, adab[:], p.ada_bT[i], writes=[bl], sbuf_side=bl)
            fw.op(fw.act, lambda e: e.activation(out=sc_b[:], in_=cs[:], func=AF.Silu), reads=[bl], writes=[bl])
            wb = p.wbuf["ada%d" % i]
            for blk in range(36):
                ws.enqueue(lambda t, b, blk=blk: fw.dma(fw.sp, t[:, :].rearrange("p (kc n) -> p kc n", kc=DC), p.adaS[i][blk],
                                                         reads=[wb], writes=[b], sbuf_side=b))
            bank, bbk = p.banks[0], p.bb[0]
            for blk in range(36):
                t, b = ws.take()
                tv = t[:, :].rearrange("p (kc n) -> p kc n", kc=DC)
                for c4 in range(4):
                    cc = blk * 4 + c4
                    fw.group(fw.pe, [
                        (lambda e, kc=kc, c4=c4, cc=cc: e.matmul(bank[:, 2 * cc:2 * cc + 2], lhsT=tv[:, kc, c4 * 128:(c4 + 1) * 128],
                                                                 rhs=sc_b[:, kc, :], start=(kc == 0), stop=(kc == DC - 1)))
                        for kc in range(DC)], reads=[b, bl], writes=[bbk])
            fw.op(fw.dve, lambda e: e.tensor_tensor(out=p.mT[:], in0=bank[:, 0:288].rearrange("p (c t) -> p c t", t=2),
                                                    in1=adab[:].unsqueeze(2).broadcast_to([128, 144, 2]), op=ALU.add),
                  reads=[bbk, bl], writes=[p.bmT])
            m4 = p.mT[:].rearrange("p (k c) t -> p k c t", c=DC)
            for k in range(3):
                g = p.ngT[:, i, k, :].unsqueeze(2).broadcast_to([128, DC, 2])
                fw.op(fw.dve, lambda e, k=k: e.tensor_scalar(out=p.Acol[:, k], in0=m4[:, 3 * k + 1], scalar1=1.0, scalar2=None, op0=ALU.add),
                      reads=[p.bmT], writes=[p.bcols])
                fw.op(fw.dve, lambda e, k=k, g=g: e.tensor_tensor(out=p.Acol[:, k], in0=p.Acol[:, k], in1=g, op=ALU.mult),
                      reads=[p.bcols, p.bsmall], writes=[p.bcols])
                fw.op(fw.dve, lambda e, k=k: e.tensor_scalar(out=p.Gcol[:, k], in0=m4[:, 3 * k + 2], scalar1=(1.0 if k == 1 else 0.5),
                                                            scalar2=None, op0=ALU.mult), reads=[p.bmT, p.bcols], writes=[p.bcols])
            if "mdbg" in p.debug:
                fw.dma(fw.act, p.mdbg, p.mT[:], reads=[p.bmT], sbuf_side=p.bmT)
            fw.barrier()

    def rmsnorm_mod(self, i, k, c, N):
        p, fw = self, self.fw
        ssb, bss = p.banks[6], p.bb[6]
        for kc in range(DC):
            s, bs = p.sq[kc % 2], p.bsq[kc % 2]
            fw.op(fw.act, lambda e, kc=kc, s=s: e.activation(out=s[:, :N], in_=p.xt[:, kc, :N], func=AF.Square),
                  reads=[p.bxt[kc]], writes=[bs])
            fw.op(fw.pe, lambda e, kc=kc, s=s: e.matmul(ssb[:, :N], lhsT=p.ones[:], rhs=s[:, :N], start=(kc == 0), stop=(kc == DC - 1)),
                  reads=[bs, p.bconst], writes=[bss])
        fw.op(fw.act, lambda e: e.activation(out=p.rs[:, :N], in_=ssb[:, :N], func=AF.Sqrt, scale=1.0 / D, bias=p.epsc[:]),
              reads=[bss, p.bconst], writes=[p.brs])
        fw.op(fw.dve, lambda e: e.reciprocal(out=p.rs[:, :N], in_=p.rs[:, :N]), reads=[p.brs], writes=[p.brs])
        m4 = p.mT[:].rearrange("p (k c) t -> p k c t", c=DC)
        for kc in range(DC):
            t, bt = p.tf[kc % 2], p.btf[kc % 2]
            fw.op(fw.dve, lambda e, kc=kc, t=t: e.tensor_tensor(out=t[:, :N], in0=p.xt[:, kc, :N], in1=p.rs[:, :N], op=ALU.mult),
                  reads=[p.bxt[kc], p.brs], writes=[bt])
            fw.op(fw.act, lambda e, kc=kc, t=t: e.activation(out=p.ht[:, kc, :N], in_=t[:, :N], func=AF.Identity,
                                                             scale=p.Acol[:, k, kc, c:c + 1], bias=m4[:, 3 * k, kc, c:c + 1]),
                  reads=[bt, p.bcols, p.bmT], writes=[p.bht[kc]])

    def enqueue_ffn(self, i, f):
        p, fw, ws = self, self.fw, self.ws
        for jp in range(FC // 2):
            ws.enqueue(lambda t, b, jp=jp: fw.dma(fw.sp, t[:, :].rearrange("p (g kc n) -> p g kc n", g=2, kc=DC),
                                                   p.wguS[i][f][jp], reads=[p.wbuf["wgu%d%d_%d" % (i, f, jp // 6)]], writes=[b], sbuf_side=b))
        for dc in range(DC):
            ws.enqueue(lambda t, b, dc=dc: fw.dma(fw.sp, t[:, 0:FC * 128].rearrange("p (j n) -> p j n", j=FC),
                                                   p.wdS[i][f][dc], reads=[p.wbuf["wd%d%d_%d" % (i, f, dc // 8)]], writes=[b], sbuf_side=b))

    def ffn(self, i, f, k, c, N, act, bact):
        p, fw, ws = self, self.fw, self.ws
        for jp in range(FC // 2):
            t, b = ws.take()
            tv = t[:, :].rearrange("p (g kc n) -> p g kc n", g=2, kc=DC)
            for jj in range(2):
                j = 2 * jp + jj
                gb, bgb = p.banks[j % 2], p.bb[j % 2]
                ub, bub = p.banks[2 + j % 2], p.bb[2 + j % 2]
                fw.group(fw.pe, [(lambda e, kc=kc, jj=jj, gb=gb: e.matmul(gb[:, :N], lhsT=tv[:, 0, kc, jj * 128:(jj + 1) * 128], rhs=p.ht[:, kc, :N],
                                                                         start=(kc == 0), stop=(kc == DC - 1))) for kc in range(DC)],
                         reads=[b] + p.bht, writes=[bgb])
                fw.group(fw.pe, [(lambda e, kc=kc, jj=jj, ub=ub: e.matmul(ub[:, :N], lhsT=tv[:, 1, kc, jj * 128:(jj + 1) * 128], rhs=p.ht[:, kc, :N],
                                                                         start=(kc == 0), stop=(kc == DC - 1))) for kc in range(DC)],
                         reads=[b] + p.bht, writes=[bub])
                st, bst = p.tf[2 + j % 2], p.btf[2 + j % 2]
                fw.op(fw.act, lambda e, gb=gb, st=st: e.activation(out=st[:, :N], in_=gb[:, :N], func=AF.Silu), reads=[bgb], writes=[bst])
                fw.op(fw.dve, lambda e, ub=ub, st=st, j=j: e.tensor_tensor(out=act[:, j, :N], in0=st[:, :N], in1=ub[:, :N], op=ALU.mult),
                      reads=[bst, bub], writes=[bact[j]])
        for dc in range(DC):
            t, b = ws.take()
            tv = t[:, 0:FC * 128].rearrange("p (j n) -> p j n", j=FC)
            yb, byb = p.banks[4 + dc % 2], p.bb[4 + dc % 2]
            fw.group(fw.pe, [(lambda e, j=j, yb=yb: e.matmul(yb[:, :N], lhsT=tv[:, j, :], rhs=act[:, j, :N], start=(j == 0), stop=(j == FC - 1)))
                             for j in range(FC)], reads=[b] + bact, writes=[byb])
            fw.op(fw.dve, lambda e, dc=dc, yb=yb: e.scalar_tensor_tensor(out=p.xt[:, dc, :N], in0=yb[:, :N], scalar=p.Gcol[:, k, dc, c:c + 1],
                                                                        in1=p.xt[:, dc, :N], op0=ALU.mult, op1=ALU.add),
                  reads=[byb, p.bcols, p.bxt[dc]], writes=[p.bxt[dc]])

    def load_x_tile(self, i, t0, N, is_ctx):
        p, fw = self, self.fw
        if i > 0:
            bl = p.bxt
            fw.dma(fw.sp, p.xt[:, :, :N], p.xT.rearrange("(kc q) t -> q kc t", q=128)[:, :, t0:t0 + N], writes=bl, sbuf_side=bl[0])
            return
        with self.fw.scope() as sc:
            xtok = [fw.sbuf("xtok", [128, D], F32, sc) for _ in range(2)]
            bxk = bufs(2, "xtok")
            nb = 0
            for blk in range(N // 128):
                src = p.ctx[blk * 128:(blk + 1) * 128, :] if is_ctx else p.x[t0 - CTX + blk * 128: t0 - CTX + (blk + 1) * 128, :]
                xk, bk = xtok[blk % 2], bxk[blk % 2]
                fw.dma(fw.sp, xk[:], src, writes=[bk], sbuf_side=bk)
                for dg in range(4):
                    bank, bbk = p.banks[nb % 4], p.bb[nb % 4]
                    nb += 1
                    fw.group(fw.pe, [(lambda e, q=q, dg=dg, bank=bank, xk=xk: e.transpose(out=bank[:, q * 128:(q + 1) * 128],
                                                                                      in_=xk[:, (dg * 4 + q) * 128:(dg * 4 + q + 1) * 128],
                                                                                      identity=p.ident[:])) for q in range(4)],
                             reads=[bk, p.bconst], writes=[bbk])
                    eng = fw.act if dg % 2 == 0 else fw.dve
                    outap = p.xt[:, dg * 4:(dg + 1) * 4, blk * 128:(blk + 1) * 128]
                    inap = bank[:, :].rearrange("p (q n) -> p q n", q=4)
                    if eng is fw.act:
                        fw.op(eng, lambda e, outap=outap, inap=inap: e.activation(out=outap, in_=inap, func=AF.Copy),
                              reads=[bbk], writes=p.bxt[dg * 4:(dg + 1) * 4])
                    else:
                        fw.op(eng, lambda e, outap=outap, inap=inap: e.tensor_copy(out=outap, in_=inap),
                              reads=[bbk], writes=p.bxt[dg * 4:(dg + 1) * 4])
            fw.barrier()

    def pass_a(self, i):
        with self.fw.scope() as sc:
            self.pass_alloc(sc)
            self._pass_a(i)

    def _pass_a(self, i):
        p, fw, ws = self, self.fw, self.ws
        wbin = p.wbuf["win%d" % i]
        wbv = p.wbuf["wv%d" % i]
        for (t0, N, is_ctx) in TILES:
            p.enqueue_ffn(i, 0)
            for g in range(10):
                if g in (2, 4, 6, 8):
                    continue
                ws.enqueue(lambda t, b, g=g: fw.dma(fw.sp, t[:, :].rearrange("p (c kc n) -> p c kc n", c=4, kc=DC), p.winS[i][g],
                                                     reads=[wbin], writes=[b], sbuf_side=b))
            ws.enqueue(lambda t, b: fw.dma(fw.sp, t[:, :].rearrange("p (kc n) -> p kc n", kc=DC), p.wvdS[i],
                                           reads=[wbv], writes=[b], sbuf_side=b))
            ws.enqueue(lambda t, b: fw.dma(fw.sp, t[:, 0:DC * 256].rearrange("p (kc n) -> p kc n", kc=DC), p.wvgS[i],
                                           reads=[wbv], writes=[b], sbuf_side=b))
        for (t0, N, is_ctx) in TILES[:p.ntiles]:
            c = 1 if is_ctx else 0
            p.load_x_tile(i, t0, N, is_ctx)
            with self.fw.scope() as sc:
                act = fw.sbuf("act", [128, FC, 512], BF16, sc)
                bact = bufs(FC, "act")
                p.rmsnorm_mod(i, 0, c, N)
                p.ffn(i, 0, 0, c, N, act, bact)
                fw.barrier()
            fw.dma(fw.act, p.xT.rearrange("(kc q) t -> q kc t", q=128)[:, :, t0:t0 + N], p.xt[:, :, :N], reads=p.bxt, sbuf_side=p.bxt[0])
            if p.stop == "ffn":
                continue
            p.rmsnorm_mod(i, 1, c, N)
            with self.fw.scope() as sc:
                p.in_proj(i, t0, N, is_ctx, sc)
                fw.barrier()

    def in_proj(self, i, t0, N, is_ctx, sc):
        p, fw, ws = self, self.fw, self.ws
        ust = fw.sbuf("ust", [128, 4, 512], F32, sc)
        usb = fw.sbuf("usb", [128, 4, 512], BF16, sc)
        qst = [fw.sbuf("qst", [128, 4, 512], BF16, sc) for _ in range(2)]
        bqst = [Buf("qst0"), Buf("qst1")]
        vsd = fw.sbuf("vsd", [128, 4, 512], BF16, sc)
        vsg = fw.sbuf("vsg", [128, 4, 256], BF16, sc)
        bu, bub, bvd, bvg = Buf("ust"), Buf("usb"), Buf("vsd"), Buf("vsg")
        rp = fw.sbuf("rp", [128, 4, 512], F32, sc)
        rq = fw.sbuf("rq", [128, 4, 512], F32, sc)
        brp, brq = Buf("rp"), Buf("rq")
        if not is_ctx:
            l0 = t0 - CTX
            fw.dma(fw.sp, rp[:, :, :N], p.ropeT.rearrange("k q t -> q k t")[:, :, l0:l0 + N], writes=[brp], sbuf_side=brp)
            for n, (tb, gi) in enumerate([(0, 1), (1, 2), (0, 3), (1, 4)]):
                fw.op(fw.dve, lambda e, n=n, tb=tb, gi=gi: e.tensor_scalar(out=rq[:, n, :N], in0=rp[:, tb, :N], scalar1=p.hg[:, i, gi:gi + 1],
                                                                           scalar2=None, op0=ALU.mult), reads=[brp, p.bsmall], writes=[brq])
        nbank = [0]
        nqb = [0]
        qbf = [fw.sbuf("qbf", [128, 512], BF16, sc) for _ in range(2)]
        bqbf = bufs(2, "qbf")

        def mm_chunk(tv, ci):
            bank, bbk = p.banks[nbank[0] % 6], p.bb[nbank[0] % 6]
            nbank[0] += 1
            return bank, bbk, [(lambda e, kc=kc: e.matmul(bank[:, :N], lhsT=tv[:, ci, kc, :], rhs=p.ht[:, kc, :N], start=(kc == 0),
                                                          stop=(kc == DC - 1))) for kc in range(DC)]

        def view(t):
            return t[:, :].rearrange("p (c kc n) -> p c kc n", c=4, kc=DC)
        t, b = ws.take()
        tv = view(t)
        for ci in range(4):
            bank, bbk, mms = mm_chunk(tv, ci)
            fw.group(fw.pe, mms, reads=[b] + p.bht, writes=[bbk])
            fw.op(fw.act, lambda e, ci=ci, bank=bank: e.activation(out=ust[:, ci, :N], in_=bank[:, :N], func=AF.Copy), reads=[bbk], writes=[bu])
            fw.op(fw.dve, lambda e, ci=ci: e.tensor_copy(out=usb[:, ci, :N], in_=ust[:, ci, :N]), reads=[bu], writes=[bub])
        fw.dma(fw.act, p.uT.rearrange("(c q) t -> q c t", q=128)[:, :, t0:t0 + N], ust[:, :, :N], reads=[bu], sbuf_side=bu)
        fw.dma(fw.act, p.uTb.rearrange("(c q) t -> q c t", q=128)[:, :, t0:t0 + N], usb[:, :, :N], reads=[bub], sbuf_side=bub)

        def qk_group(dst, dst_c0, nchunks, kind, sidx, gain_main, tabs):
            st, bs_ = qst[sidx], bqst[sidx]
            tm, bm = ws.take()
            tvm = view(tm)
            for ci in range(nchunks):
                bankA, bbA, mmsA = mm_chunk(tvm, ci)
                fw.group(fw.pe, mmsA, reads=[bm] + p.bht, writes=[bbA])
                if not is_ctx:
                    qb_, bqb = qbf[nqb[0] % 2], bqbf[nqb[0] % 2]
                    nqb[0] += 1
                    fw.op(fw.act, lambda e: e.activation(out=qb_[:, :N], in_=bankA[:, :N], func=AF.Copy), reads=[bbA], writes=[bqb])
                    bankB, bbB = p.banks[nbank[0] % 6], p.bb[nbank[0] % 6]
                    nbank[0] += 1
                    fw.op(fw.pe, lambda e: e.matmul(bankB[:, :N], lhsT=p.perm[:, (1 if kind == 'd' else 0), :], rhs=qb_[:, :N], start=True, stop=True),
                          reads=[bqb, p.bconst], writes=[bbB])
                if kind != 'd':
                    s, bs = p.sq[ci % 2], p.bsq[ci % 2]
                    ssb, bss = p.banks[6 + ci % 2], p.bb[6 + ci % 2]
                    fw.op(fw.act, lambda e, s=s, bankA=bankA: e.activation(out=s[:, :N], in_=bankA[:, :N], func=AF.Square), reads=[bbA], writes=[bs])
                    fw.op(fw.pe, lambda e, s=s, ssb=ssb: e.matmul(ssb[:, :N], lhsT=p.ones[:], rhs=s[:, :N], start=True, stop=True),
                          reads=[bs, p.bconst], writes=[bss])
                    fw.op(fw.act, lambda e, ssb=ssb: e.activation(out=p.rs[:, :N], in_=ssb[:, :N], func=AF.Sqrt, scale=1.0 / 128, bias=p.epsc[:]),
                          reads=[bss, p.bconst], writes=[p.brs])
                    fw.op(fw.dve, lambda e: e.reciprocal(out=p.rs[:, :N], in_=p.rs[:, :N]), reads=[p.brs], writes=[p.brs])
                if is_ctx:
                    if kind == 'd':
                        fw.op(fw.dve, lambda e, ci=ci, bankA=bankA: e.tensor_copy(out=st[:, ci, :N], in_=bankA[:, :N]), reads=[bbA], writes=[bs_])
                    else:
                        fw.op(fw.dve, lambda e, ci=ci, bankA=bankA: e.scalar_tensor_tensor(out=st[:, ci, :N], in0=bankA[:, :N],
                                                                                         scalar=p.hg[:, i, gain_main:gain_main + 1],
                                                                                         in1=p.rs[:, :N], op0=ALU.mult, op1=ALU.mult),
                              reads=[bbA, p.brs, p.bsmall], writes=[bs_])
                else:
                    tabt, btab = (rp, brp) if kind == 'd' else (rq, brq)
                    t1, b1 = p.tf[0], p.btf[0]
                    t2, b2 = p.tf[1], p.btf[1]
                    fw.op(fw.dve, lambda e, bankA=bankA, t1=t1: e.tensor_tensor(out=t1[:, :N], in0=bankA[:, :N], in1=tabt[:, tabs[0], :N], op=ALU.mult),
                          reads=[bbA, btab], writes=[b1])
                    fw.op(fw.dve, lambda e, bankB=bankB, t2=t2: e.tensor_tensor(out=t2[:, :N], in0=bankB[:, :N], in1=tabt[:, tabs[1], :N], op=ALU.mult),
                          reads=[bbB, btab], writes=[b2])
                    if kind == 'd':
                        fw.op(fw.dve, lambda e, ci=ci: e.tensor_tensor(out=st[:, ci, :N], in0=t1[:, :N], in1=t2[:, :N], op=ALU.add),
                              reads=[b1, b2], writes=[bs_])
                    else:
                        fw.op(fw.dve, lambda e: e.tensor_tensor(out=t1[:, :N], in0=t1[:, :N], in1=t2[:, :N], op=ALU.add),
                              reads=[b1, b2], writes=[b1])
                        fw.op(fw.dve, lambda e, ci=ci: e.tensor_tensor(out=st[:, ci, :N], in0=t1[:, :N], in1=p.rs[:, :N], op=ALU.mult),
                              reads=[b1, p.brs], writes=[bs_])
            fw.dma(fw.act, dst.rearrange("(c q) t -> q c t", q=128)[:, dst_c0:dst_c0 + nchunks, t0:t0 + N], st[:, 0:nchunks, :N],
                   reads=[bs_], sbuf_side=bs_)
        qk_group(p.qd, 0, 4, 'd', 0, None, (2, 3))
        qk_group(p.kd, 0, 4, 'd', 1, None, (2, 3))
        qk_group(p.qg, 0, 4, 'g', 0, 1, (0, 1))
        qk_group(p.qg, 4, 4, 'g', 1, 1, (0, 1))
        qk_group(p.kg, 0, 2, 'gk', 0, 3, (2, 3))
        tvd, bvdw = ws.take()
        tvg, bvgw = ws.take()
        tvdv = tvd[:, :].rearrange("p (kc n) -> p kc n", kc=DC)
        tvgv = tvg[:, 0:DC * 256].rearrange("p (kc n) -> p kc n", kc=DC)
        for blk in range(N // 128):
            bank, bbk = p.banks[blk % 2], p.bb[blk % 2]
            fw.group(fw.pe, [(lambda e, kc=kc, bank=bank, blk=blk: e.matmul(bank[:, :], lhsT=p.ht[:, kc, blk * 128:(blk + 1) * 128], rhs=tvdv[:, kc, :],
                                                                          start=(kc == 0), stop=(kc == DC - 1))) for kc in range(DC)],
                     reads=[bvdw] + p.bht, writes=[bbk])
            fw.op(fw.act, lambda e, blk=blk, bank=bank: e.activation(out=vsd[:, blk, :], in_=bank[:, :], func=AF.Copy), reads=[bbk], writes=[bvd])
            bank2, bbk2 = p.banks[2 + blk % 2], p.bb[2 + blk % 2]
            fw.group(fw.pe, [(lambda e, kc=kc, bank2=bank2, blk=blk: e.matmul(bank2[:, 0:256], lhsT=p.ht[:, kc, blk * 128:(blk + 1) * 128], rhs=tvgv[:, kc, :],
                                                                            start=(kc == 0), stop=(kc == DC - 1))) for kc in range(DC)],
                     reads=[bvgw] + p.bht, writes=[bbk2])
            fw.op(fw.dve, lambda e, blk=blk, bank2=bank2: e.tensor_copy(out=vsg[:, blk, :], in_=bank2[:, 0:256]), reads=[bbk2], writes=[bvg])
        nbk = N // 128
        b0 = t0 // 128
        for hh in range(4):
            fw.dma(fw.act, p.vd[hh, :, b0:b0 + nbk, :], vsd[:, 0:nbk, hh * 128:(hh + 1) * 128], reads=[bvd], sbuf_side=bvd)
        for hh in range(2):
            fw.dma(fw.act, p.vg[hh, :, b0:b0 + nbk, :], vsg[:, 0:nbk, hh * 128:(hh + 1) * 128], reads=[bvg], sbuf_side=bvg)

    def s5(self, i):
        p, fw = self, self.fw
        MAGIC = 12582912.0
        with fw.scope() as sc:
            dv, pl, ac = fw.dve, fw.pool, fw.act
            bs = Buf("s5setup")
            lamc = fw.sbuf("lamc", [128, 2, 3, 16], F32, sc)
            fw.dma(fw.sp, lamc[:], p.lamT[i].rearrange("d q k c -> q d k c"), writes=[bs], sbuf_side=bs)
            Uf = fw.sbuf("Uf", [128, 128], BF16, sc)
            Ur = fw.sbuf("Ur", [128, 128], BF16, sc)
            with fw.scope() as sc0:
                U32 = fw.sbuf("U32", [128, 128], F32, sc0)
                for (Ux, pat, cm) in ((Uf, 1, -1), (Ur, -1, 1)):
                    fw.op(pl, lambda e: e.memset(U32[:], 1.0), reads=[bs], writes=[bs])
                    fw.op(pl, lambda e: e.affine_select(out=U32[:], in_=U32[:], pattern=[[pat, 128]], compare_op=ALU.is_ge, fill=0.0, base=0,
                                                        channel_multiplier=cm), reads=[bs], writes=[bs])
                    fw.op(pl, lambda e: e.tensor_copy(out=Ux[:], in_=U32[:]), reads=[bs], writes=[bs])
            PCt = [fw.sbuf("PCt", [128, 2048], F32, sc) for _ in range(2)]
            PSt = [fw.sbuf("PSt", [128, 2048], F32, sc) for _ in range(2)]
            QC = [fw.sbuf("QC", [128, 16, 128], F32, sc) for _ in range(2)]
            QS = [fw.sbuf("QS", [128, 16, 128], F32, sc) for _ in range(2)]
            RB = [fw.sbuf("RB", [128, 4, 1024], BF16, sc) for _ in range(2)]
            CT = [fw.sbuf("CT", [128, 16, 2, 128], BF16, sc) for _ in range(2)]
            carry = fw.sbuf("carry", [128, 2, 16, 2], F32, sc)
            btab = Buf("s5tab")

            def o(eng, fn):
                fw.op(eng, fn, reads=[bs, p.bconst], writes=[bs])
            with fw.scope() as sc2:
                cols = fw.sbuf("cols", [128, 2, 12, 16], F32, sc2)
                kv = fw.sbuf("kv", [128, 128], F32, sc2)
                ang = fw.sbuf("ang", [128, 16, 128], F32, sc2)
                nn = fw.sbuf("nn", [128, 16, 128], F32, sc2)
                tc_ = fw.sbuf("tc", [128, 16, 128], F32, sc2)
                ts_ = fw.sbuf("ts", [128, 16, 128], F32, sc2)
                mg = fw.sbuf("mg", [128, 16, 128], F32, sc2)
                xb = fw.sbuf("xb", [128, 2, 16, 128], F32, sc2)
                xw = fw.sbuf("xw", [128, 2, 128], F32, sc2)

                def wrap_sin(dst, src, shift):
                    o(dv, lambda e: e.tensor_scalar(out=dst, in0=src, scalar1=shift, scalar2=None, op0=ALU.add))
                    o(dv, lambda e: e.tensor_scalar(out=nn[:], in0=dst, scalar1=1.0 / (2 * PI), scalar2=MAGIC, op0=ALU.mult, op1=ALU.add))
                    o(dv, lambda e: e.tensor_scalar(out=nn[:], in0=nn[:], scalar1=-MAGIC, scalar2=-2 * PI, op0=ALU.add, op1=ALU.mult))
                    o(dv, lambda e: e.tensor_tensor(out=dst, in0=dst, in1=nn[:], op=ALU.add))
                    o(dv, lambda e: e.tensor_scalar(out=dst, in0=dst, scalar1=PI, scalar2=-PI, op0=ALU.min, op1=ALU.max))
                    o(ac, lambda e: e.activation(out=dst, in_=dst, func=AF.Sin))

                def gen_table(d, base, step):
                    o(pl, lambda e: e.iota(kv[:], pattern=[[step, 128]], base=base, channel_multiplier=0, allow_small_or_imprecise_dtypes=True))
                    for q in range(16):
                        o(dv, lambda e, q=q: e.tensor_scalar(out=ang[:, q, :], in0=kv[:], scalar1=cols[:, d, 2, q:q + 1], scalar2=None, op0=ALU.mult))
                        o(ac, lambda e, q=q: e.activation(out=mg[:, q, :], in_=kv[:], func=AF.Exp, scale=cols[:, d, 1, q:q + 1]))
                    wrap_sin(ts_[:], ang[:], 0.0)
                    wrap_sin(tc_[:], ang[:], 0.5 * PI)
                    o(dv, lambda e: e.tensor_tensor(out=ts_[:], in0=ts_[:], in1=mg[:], op=ALU.mult))
                    o(dv, lambda e: e.tensor_tensor(out=tc_[:], in0=tc_[:], in1=mg[:], op=ALU.mult))

                for d in range(2):
                    c_ = lambda k: cols[:, d, k, :]
                    o(ac, lambda e: e.activation(out=c_(0), in_=lamc[:, d, 2, :], func=AF.Exp))
                    o(dv, lambda e: e.tensor_tensor(out=c_(1), in0=lamc[:, d, 0, :], in1=c_(0), op=ALU.mult))
                    o(dv, lambda e: e.tensor_tensor(out=c_(2), in0=lamc[:, d, 1, :], in1=c_(0), op=ALU.mult))
                    o(ac, lambda e: e.activation(out=c_(9), in_=c_(1), func=AF.Exp))
                    for (dst, shift) in ((4, 0.0), (3, 0.5 * PI)):
                        o(dv, lambda e: e.tensor_scalar(out=c_(11), in0=c_(2), scalar1=shift, scalar2=None, op0=ALU.add))
                        o(dv, lambda e: e.tensor_scalar(out=c_(10), in0=c_(11), scalar1=1.0 / (2 * PI), scalar2=MAGIC, op0=ALU.mult, op1=ALU.add))
                        o(dv, lambda e: e.tensor_scalar(out=c_(10), in0=c_(10), scalar1=-MAGIC, scalar2=-2 * PI, op0=ALU.add, op1=ALU.mult))
                        o(dv, lambda e: e.tensor_tensor(out=c_(11), in0=c_(11), in1=c_(10), op=ALU.add))
                        o(dv, lambda e: e.tensor_scalar(out=c_(11), in0=c_(11), scalar1=PI, scalar2=-PI, op0=ALU.min, op1=ALU.max))
                        o(ac, lambda e, dst=dst: e.activation(out=c_(dst), in_=c_(11), func=AF.Sin))
                        o(dv, lambda e, dst=dst: e.tensor_tensor(out=c_(dst), in0=c_(dst), in1=c_(9), op=ALU.mult))
                    o(dv, lambda e: e.tensor_tensor(out=c_(10), in0=c_(9), in1=c_(9), op=ALU.mult))
                    o(dv, lambda e: e.reciprocal(out=c_(10), in_=c_(10)))
                    o(dv, lambda e: e.tensor_tensor(out=c_(7), in0=c_(3), in1=c_(10), op=ALU.mult))
                    o(dv, lambda e: e.scalar_tensor_tensor(out=c_(8), in0=c_(4), scalar=-1.0, in1=c_(10), op0=ALU.mult, op1=ALU.mult))
                    o(dv, lambda e: e.tensor_tensor(out=c_(10), in0=lamc[:, d, 0, :], in1=lamc[:, d, 0, :], op=ALU.mult))
                    o(dv, lambda e: e.tensor_tensor(out=c_(11), in0=lamc[:, d, 1, :], in1=lamc[:, d, 1, :], op=ALU.mult))
                    o(dv, lambda e: e.tensor_tensor(out=c_(10), in0=c_(10), in1=c_(11), op=ALU.add))
                    o(dv, lambda e: e.reciprocal(out=c_(10), in_=c_(10)))
                    o(dv, lambda e: e.tensor_scalar(out=c_(9), in0=c_(3), scalar1=-1.0, scalar2=None, op0=ALU.add))
                    o(dv, lambda e: e.tensor_tensor(out=c_(5), in0=c_(9), in1=lamc[:, d, 0, :], op=ALU.mult))
                    o(dv, lambda e: e.tensor_tensor(out=c_(11), in0=c_(4), in1=lamc[:, d, 1, :], op=ALU.mult))
                    o(dv, lambda e: e.tensor_tensor(out=c_(5), in0=c_(5), in1=c_(11), op=ALU.add))
                    o(dv, lambda e: e.tensor_tensor(out=c_(5), in0=c_(5), in1=c_(10), op=ALU.mult))
                    o(dv, lambda e: e.tensor_tensor(out=c_(6), in0=c_(4), in1=lamc[:, d, 0, :], op=ALU.mult))
                    o(dv, lambda e: e.tensor_tensor(out=c_(11), in0=c_(9), in1=lamc[:, d, 1, :], op=ALU.mult))
                    o(dv, lambda e: e.tensor_tensor(out=c_(6), in0=c_(6), in1=c_(11), op=ALU.subtract))
                    o(dv, lambda e: e.tensor_tensor(out=c_(6), in0=c_(6), in1=c_(10), op=ALU.mult))
                    fw.dma(fw.sp, xb[:], p.XB[i, d].rearrange("r q c f -> c r q f"), reads=[bs], writes=[bs], sbuf_side=bs)
                    for q in range(16):
                        fc, ql = q // 4, q % 4
                        g_re, g_im = cols[:, d, 5, q:q + 1], cols[:, d, 6, q:q + 1]
                        o(dv, lambda e: e.tensor_scalar(out=xw[:, 0, :], in0=xb[:, 1, q, :], scalar1=g_im, scalar2=None, op0=ALU.mult))
                        o(dv, lambda e: e.scalar_tensor_tensor(out=xw[:, 0, :], in0=xb[:, 0, q, :], scalar=g_re, in1=xw[:, 0, :], op0=ALU.mult, op1=ALU.subtract))
                        o(dv, lambda e: e.tensor_scalar(out=xw[:, 1, :], in0=xb[:, 0, q, :], scalar1=g_im, scalar2=None, op0=ALU.mult))
                        o(dv, lambda e: e.scalar_tensor_tensor(out=xw[:, 1, :], in0=xb[:, 1, q, :], scalar=g_re, in1=xw[:, 1, :], op0=ALU.mult, op1=ALU.add))
                        bank, bbk = p.banks[q % 2], p.bb[q % 2]
                        fw.group(fw.pe, [(lambda e, ri=ri: e.transpose(out=bank[:, ri * 128:(ri + 1) * 128], in_=xw[:, ri, :], identity=p.ident[:]))
                                         for ri in range(2)], reads=[bs, p.bconst], writes=[bbk])
                        for ri in range(2):
                            fw.op(ac, lambda e, ri=ri: e.activation(out=RB[d][:, fc, ri * 512 + ql * 128: ri * 512 + (ql + 1) * 128],
                                                                    in_=bank[:, ri * 128:(ri + 1) * 128], func=AF.Copy),
                                  reads=[bbk], writes=[btab])
                    fw.dma(fw.sp, xb[:], p.YC[i, d].rearrange("r q c f -> c r q f"), reads=[bs], writes=[bs], sbuf_side=bs)
                    for q in range(16):
                        ai_re, ai_im = cols[:, d, 7, q:q + 1], cols[:, d, 8, q:q + 1]
                        o(dv, lambda e: e.tensor_scalar(out=xw[:, 0, :], in0=xb[:, 1, q, :], scalar1=ai_im, scalar2=None, op0=ALU.mult))
                        fw.op(dv, lambda e: e.scalar_tensor_tensor(out=CT[d][:, q, 0, :], in0=xb[:, 0, q, :], scalar=ai_re, in1=xw[:, 0, :], op0=ALU.mult,
                                                                   op1=ALU.subtract), reads=[bs], writes=[btab])
                        o(dv, lambda e: e.tensor_scalar(out=xw[:, 1, :], in0=xb[:, 0, q, :], scalar1=ai_im, scalar2=-1.0, op0=ALU.mult, op1=ALU.mult))
                        o(dv, lambda e: e.tensor_scalar(out=xw[:, 0, :], in0=xb[:, 1, q, :], scalar1=ai_re, scalar2=None, op0=ALU.mult))
                        fw.op(dv, lambda e: e.tensor_tensor(out=CT[d][:, q, 1, :], in0=xw[:, 1, :], in1=xw[:, 0, :], op=ALU.subtract),
                              reads=[bs], writes=[btab])
                    if d == 0:
                        gen_table(d, 1, 1)
                    else:
                        gen_table(d, 128, -1)
                    fw.op(dv, lambda e: e.tensor_copy(out=QC[d][:], in_=tc_[:]), reads=[bs], writes=[btab])
                    fw.op(dv, lambda e: e.tensor_copy(out=QS[d][:], in_=ts_[:]), reads=[bs], writes=[btab])
                    if d == 0:
                        gen_table(d, 0, -1)
                    else:
                        gen_table(d, -127, 1)
                    nbk = 0
                    for (src, dst) in ((tc_, PCt[d]), (ts_, PSt[d])):
                        for qg in range(4):
                            bank, bbk = p.banks[nbk % 4], p.bb[nbk % 4]
                            nbk += 1
                            fw.group(fw.pe, [(lambda e, k=k: e.transpose(out=bank[:, k * 128:(k + 1) * 128], in_=src[:, qg * 4 + k, :], identity=p.ident[:]))
                                             for k in range(4)], reads=[bs, p.bconst], writes=[bbk])
                            eng = ac if qg % 2 == 0 else dv
                            if eng is ac:
                                fw.op(eng, lambda e: e.activation(out=dst[:, qg * 512:(qg + 1) * 512], in_=bank[:, :], func=AF.Copy), reads=[bbk], writes=[btab])
                            else:
                                fw.op(eng, lambda e: e.tensor_copy(out=dst[:, qg * 512:(qg + 1) * 512], in_=bank[:, :]), reads=[bbk], writes=[btab])
            if "s5dbg" in p.debug:
                for nm, tl in (("dPC", PCt[0]), ("dPS", PSt[0]), ("dQC", QC[0]), ("dQS", QS[0]), ("dRB", RB[0]), ("dCT", CT[0]),
                               ("dPC1", PCt[1]), ("dQC1", QC[1])):
                    dd = p.nc.dram_tensor(nm, list(tl.shape), tl.dtype, kind="ExternalOutput").ap()
                    fw.dma(fw.sp, dd, tl[:], reads=[btab], sbuf_side=btab)
            with fw.scope() as scp:
                ub = [fw.sbuf("ub", [128, 4, 128], BF16, scp) for _ in range(2)]
                bub = bufs(2, "ub")
                Z = [fw.sbuf("Z", [128, 2, 512], BF16, scp) for _ in range(8)]
                bZ = bufs(8, "Z")
                P4a = [fw.sbuf("P4", [128, 512], F32, scp) for _ in range(8)]
                bP4a = bufs(8, "P4")
                Hh = [fw.sbuf("Hh", [128, 2, 128], BF16, scp) for _ in range(32)]
                bH = bufs(32, "Hh")
                Pp = [fw.sbuf("Pp", [128, 4, 128], F32, scp) for _ in range(4)]
                bPp = bufs(4, "Pp")
                Gc = [fw.sbuf("Gc", [128, 2, 128], F32, scp) for _ in range(4)]
                bGc = bufs(4, "Gc")
                yst = [fw.sbuf("yst", [128, 4, 128], F32, scp) for _ in range(2)]
                byst = bufs(2, "yst")
                bcars = [[Buf("carry") for _ in range(16)] for _ in range(2)]
                uview = p.uTb.rearrange("(c q) t -> q c t", q=128)
                for d in range(2):
                    order = list(range(NB)) if d == 0 else [1, 0] + list(range(NB - 1, 1, -1))
                    U = Uf if d == 0 else Ur
                    L = 127 if d == 0 else 0
                    ydst = (p.yF if d == 0 else p.yR).rearrange("(c q) t -> q c t", q=128)
                    fw.op(dv, lambda e: e.memset(carry[:, d], 0.0), writes=bcars[d])

                    def stage_a(n):
                        blk = order[n]
                        u_, bu_ = ub[n % 2], bub[n % 2]
                        fw.dma(fw.sp, u_[:], uview[:, :, blk * 128:(blk + 1) * 128], writes=[bu_], sbuf_side=bu_)
                        for fc in range(4):
                            bre, bbre = p.banks[0], p.bb[0]
                            bim, bbim = p.banks[1], p.bb[1]
                            fw.op(fw.pe, lambda e: e.matmul(bre[:, :], lhsT=u_[:, fc, :], rhs=RB[d][:, fc, 0:512], start=True, stop=True),
                                  reads=[bu_, btab], writes=[bbre])
                            fw.op(fw.pe, lambda e: e.matmul(bim[:, :], lhsT=u_[:, fc, :], rhs=RB[d][:, fc, 512:1024], start=True, stop=True),
                                  reads=[bu_, btab], writes=[bbim])
                            pc_ = PCt[d][:, fc * 512:(fc + 1) * 512]
                            ps_ = PSt[d][:, fc * 512:(fc + 1) * 512]
                            z, bz = Z[4 * (n % 2) + fc], bZ[4 * (n % 2) + fc]
                            P4, bP4 = P4a[4 * (fc % 2):4 * (fc % 2) + 4], bP4a[4 * (fc % 2):4 * (fc % 2) + 4]
                            fw.op(dv, lambda e: e.tensor_tensor(out=P4[0][:], in0=bre[:, :], in1=pc_, op=ALU.mult), reads=[bbre, btab], writes=[bP4[0]])
                            fw.op(dv, lambda e: e.tensor_tensor(out=P4[1][:], in0=bim[:, :], in1=ps_, op=ALU.mult), reads=[bbim, btab], writes=[bP4[1]])
                            fw.op(pl, lambda e: e.tensor_tensor(out=z[:, 0, :], in0=P4[0][:], in1=P4[1][:], op=ALU.subtract), reads=[bP4[0], bP4[1]], writes=[bz])
                            fw.op(dv, lambda e: e.tensor_tensor(out=P4[2][:], in0=bim[:, :], in1=pc_, op=ALU.mult), reads=[bbim, btab], writes=[bP4[2]])
                            fw.op(dv, lambda e: e.tensor_tensor(out=P4[3][:], in0=bre[:, :], in1=ps_, op=ALU.mult), reads=[bbre, btab], writes=[bP4[3]])
                            fw.op(pl, lambda e: e.tensor_tensor(out=z[:, 1, :], in0=P4[2][:], in1=P4[3][:], op=ALU.add), reads=[bP4[2], bP4[3]], writes=[bz])

                    def stage_b(n):
                        for q in range(16):
                            fc, ql = q // 4, q % 4
                            z, bz = Z[4 * (n % 2) + fc], bZ[4 * (n % 2) + fc]
                            gb, bgb = p.banks[2 + q % 4], p.bb[2 + q % 4]
                            fw.group(fw.pe, [(lambda e, ri=ri: e.matmul(gb[:, ri * 128:(ri + 1) * 128], lhsT=z[:, ri, ql * 128:(ql + 1) * 128], rhs=U[:],
                                                                         start=True, stop=True)) for ri in range(2)],
                                     reads=[bz, bs], writes=[bgb])
                            pp, bpp = Pp[q % 4], bPp[q % 4]
                            cre, cim = carry[:, d, q, 0:1], carry[:, d, q, 1:2]
                            qc2 = QC[d][:, q, :].unsqueeze(1).broadcast_to([128, 2, 128])
                            qs2 = QS[d][:, q, :].unsqueeze(1).broadcast_to([128, 2, 128])
                            bcar = bcars[d][q]
                            gc, bgc = Gc[q % 4], bGc[q % 4]
                            fw.op(ac, lambda e: e.activation(out=gc[:, 0, :], in_=gb[:, 0:128], func=AF.Identity, bias=cre), reads=[bgb, bcar], writes=[bgc])
                            fw.op(ac, lambda e: e.activation(out=gc[:, 1, :], in_=gb[:, 128:256], func=AF.Identity, bias=cim), reads=[bgb, bcar], writes=[bgc])
                            fw.op(dv, lambda e: e.tensor_tensor(out=pp[:, 0:2, :], in0=gc[:, :, :], in1=qc2, op=ALU.mult), reads=[bgc, btab], writes=[bpp])
                            fw.op(dv, lambda e: e.tensor_tensor(out=pp[:, 2:4, :], in0=gc[:, :, :], in1=qs2, op=ALU.mult), reads=[bgc, btab], writes=[bpp])
                            h, bh = Hh[16 * (n % 2) + q], bH[16 * (n % 2) + q]
                            fw.op(pl, lambda e: e.tensor_tensor(out=h[:, 0, :], in0=pp[:, 0, :], in1=pp[:, 3, :], op=ALU.subtract), reads=[bpp], writes=[bh])
                            fw.op(pl, lambda e: e.tensor_tensor(out=h[:, 1, :], in0=pp[:, 1, :], in1=pp[:, 2, :], op=ALU.add), reads=[bpp], writes=[bh])
                            fw.op(dv, lambda e: e.tensor_tensor(out=cre, in0=pp[:, 0, L:L + 1], in1=pp[:, 3, L:L + 1], op=ALU.subtract), reads=[bpp, bcar], writes=[bcar])
                            fw.op(dv, lambda e: e.tensor_tensor(out=cim, in0=pp[:, 1, L:L + 1], in1=pp[:, 2, L:L + 1], op=ALU.add), reads=[bpp, bcar], writes=[bcar])

                    def stage_c(n):
                        blk = order[n]
                        ybank, bybank = p.banks[6 + n % 2], p.bb[6 + n % 2]
                        ys, bys = yst[n % 2], byst[n % 2]
                        for fc in range(4):
                            fw.group(fw.pe, [(lambda e, ql=ql, ri=ri: e.matmul(ybank[:, fc * 128:(fc + 1) * 128], lhsT=CT[d][:, 4 * fc + ql, ri, :],
                                                                               rhs=Hh[16 * (n % 2) + 4 * fc + ql][:, ri, :], start=(ql == 0 and ri == 0),
                                                                               stop=(ql == 3 and ri == 1))) for ql in range(4) for ri in range(2)],
                                     reads=[btab] + [bH[16 * (n % 2) + 4 * fc + ql] for ql in range(4)], writes=[bybank])
                        fw.op(ac, lambda e: e.activation(out=ys[:], in_=ybank[:, :].rearrange("p (c t) -> p c t", c=4), func=AF.Copy), reads=[bybank], writes=[bys])
                        fw.dma(fw.act, ydst[:, :, blk * 128:(blk + 1) * 128], ys[:], reads=[bys], sbuf_side=bys)

                    stage_a(0)
                    for n in range(len(order)):
                        if n + 1 < len(order):
                            stage_a(n + 1)
                        stage_b(n)
                        stage_c(n)
            fw.barrier()
            gw = fw.sbuf("gw", [128, 4, 512], BF16, sc)
            bgw = Buf("gw")
            fw.dma(fw.sp, gw[:], p.gluS[i], reads=[p.wbuf["glu%d" % i]], writes=[bgw], sbuf_side=bgw)
            A_ = [fw.sbuf("tA", [128, 4, 512], F32, sc) for _ in range(3)]
            bA = bufs(3, "tA")
            y3 = fw.sbuf("y3", [128, 4, 512], F32, sc)
            y3b = fw.sbuf("y3b", [128, 4, 512], BF16, sc)
            so = fw.sbuf("so", [128, 4, 512], BF16, sc)
            by3, by3b, bso = Buf("y3"), Buf("y3b"), Buf("so")
            for (t0, N, is_ctx) in TILES:
                vF = p.yF.rearrange("(c q) t -> q c t", q=128)[:, :, t0:t0 + N]
                vR = p.yR.rearrange("(c q) t -> q c t", q=128)[:, :, t0:t0 + N]
                vU = p.uT.rearrange("(c q) t -> q c t", q=128)[:, :, t0:t0 + N]
                fw.dma(fw.sp, A_[0][:, :, :N], vF, writes=[bA[0]], sbuf_side=bA[0])
                fw.dma(fw.sp, A_[1][:, :, :N], vR, writes=[bA[1]], sbuf_side=bA[1])
                fw.dma(fw.sp, A_[2][:, :, :N], vU, writes=[bA[2]], sbuf_side=bA[2])
                fw.op(pl, lambda e: e.tensor_tensor(out=A_[0][:, :, :N], in0=A_[0][:, :, :N], in1=A_[1][:, :, :N], op=ALU.add), reads=[bA[0], bA[1]], writes=[bA[0]])
                for fc in range(4):
                    fw.op(dv, lambda e: e.scalar_tensor_tensor(out=A_[0][:, fc, :N], in0=A_[2][:, fc, :N], scalar=p.s5d[:, i, 0, fc:fc + 1],
                                                               in1=A_[0][:, fc, :N], op0=ALU.mult, op1=ALU.add), reads=[bA[0], bA[2], p.bsmall], writes=[bA[0]])
                y2 = A_[0]
                fw.op(ac, lambda e: e.activation(out=A_[1][:, :, :N], in_=y2[:, :, :N], func=AF.Square), reads=[bA[0]], writes=[bA[1]])
                fw.op(dv, lambda e: e.tensor_scalar(out=A_[1][:, :, :N], in0=A_[1][:, :, :N], scalar1=0.044715, scalar2=1.0, op0=ALU.mult, op1=ALU.add),
                      reads=[bA[1]], writes=[bA[1]])
                fw.op(dv, lambda e: e.tensor_tensor(out=A_[1][:, :, :N], in0=A_[1][:, :, :N], in1=y2[:, :, :N], op=ALU.mult), reads=[bA[0], bA[1]], writes=[bA[1]])
                fw.op(ac, lambda e: e.activation(out=A_[1][:, :, :N], in_=A_[1][:, :, :N], func=AF.Sigmoid, scale=2.0 * math.sqrt(2.0 / PI)),
                      reads=[bA[1]], writes=[bA[1]])
                fw.op(dv, lambda e: e.tensor_tensor(out=y3[:, :, :N], in0=A_[1][:, :, :N], in1=y2[:, :, :N], op=ALU.mult), reads=[bA[0], bA[1]], writes=[by3])
                fw.op(pl, lambda e: e.tensor_copy(out=y3b[:, :, :N], in_=y3[:, :, :N]), reads=[by3], writes=[by3b])
                for fo in range(4):
                    bank, bbk = p.banks[fo % 2], p.bb[fo % 2]
                    fw.group(fw.pe, [(lambda e, fi=fi: e.matmul(bank[:, :N], lhsT=gw[:, fi, fo * 128:(fo + 1) * 128], rhs=y3b[:, fi, :N],
                                                                start=(fi == 0), stop=(fi == 3))) for fi in range(4)],
                             reads=[bgw, by3b], writes=[bbk])
                    fw.op(ac, lambda e: e.activation(out=A_[2][:, fo, :N], in_=bank[:, :N], func=AF.Sigmoid, bias=p.s5d[:, i, 1, fo:fo + 1]),
                          reads=[bbk, p.bsmall], writes=[bA[2]])
                fw.op(dv, lambda e: e.tensor_tensor(out=so[:, :, :N], in0=A_[2][:, :, :N], in1=y3[:, :, :N], op=ALU.mult), reads=[bA[2], by3], writes=[bso])
                fw.dma(fw.act, p.s5T.rearrange("(c q) t -> q c t", q=128)[:, :, t0:t0 + N], so[:, :, :N], reads=[bso], sbuf_side=bso)

    def lam_prep(self, i):
        p, fw = self, self.fw
        lam_init = 0.8 - 0.6 * math.exp(-0.3 * i)
        with fw.scope() as sc:
            dl = fw.sbuf("dl", [1, 256], F32, sc)
            pr = fw.sbuf("pr", [1, 128], F32, sc)
            sm = fw.sbuf("sm", [1, 4], F32, sc)
            one1 = fw.sbuf("one1", [1, 128], F32, sc)
            b = Buf("lam")
            fw.dma(fw.sp, dl[:], p.dlam[i], writes=[b], sbuf_side=b)
            fw.op(fw.dve, lambda e: e.memset(one1[:], 1.0), reads=[b], writes=[b])
            dl4 = dl[:, :].rearrange("p (k c) -> p k c", k=4)
            fw.op(fw.dve, lambda e: e.tensor_tensor(out=pr[:, 0:64], in0=dl4[:, 0, :], in1=dl4[:, 1, :], op=ALU.mult), reads=[b], writes=[b])
            fw.op(fw.dve, lambda e: e.tensor_tensor(out=pr[:, 64:128], in0=dl4[:, 2, :], in1=dl4[:, 3, :], op=ALU.mult), reads=[b], writes=[b])
            fw.op(fw.dve, lambda e: e.reduce_sum(out=sm[:, 0:2], in_=pr[:, :].rearrange("p (k c) -> p k c", k=2), axis=mybir.AxisListType.X),
                  reads=[b], writes=[b])
            fw.op(fw.act, lambda e: e.activation(out=sm[:, 0:2], in_=sm[:, 0:2], func=AF.Exp), reads=[b], writes=[b])
            fw.op(fw.dve, lambda e: e.tensor_tensor(out=sm[:, 2:3], in0=sm[:, 1:2], in1=sm[:, 0:1], op=ALU.subtract), reads=[b], writes=[b])
            fw.op(fw.dve, lambda e: e.tensor_scalar(out=sm[:, 2:3], in0=sm[:, 2:3], scalar1=-lam_init, scalar2=None, op0=ALU.add), reads=[b], writes=[b])
            bank, bbk = p.banks[0], p.bb[0]
            fw.op(fw.pe, lambda e: e.matmul(bank[:, 0:1], lhsT=one1[:], rhs=sm[:, 2:3], start=True, stop=True), reads=[b], writes=[bbk])
            fw.op(fw.dve, lambda e: e.tensor_copy(out=p.lamc[:, 0:1], in_=bank[:, 0:1]), reads=[bbk], writes=[p.blamc])
            fw.op(fw.dve, lambda e: e.tensor_scalar(out=p.lamc[:, 1:2], in0=p.hg[:, i, 0:1], scalar1=1.0 - lam_init, scalar2=None, op0=ALU.mult),
                  reads=[p.bsmall, p.blamc], writes=[p.blamc])

    def pass_b(self, i, last):
        p, fw = self, self.fw
        if i == 0:
            p.convert_layer(1)
        p.lam_prep(i)
        with fw.scope() as sc:
            p.pass_alloc(sc)
            ws = p.ws
            mix = fw.sbuf("mix", [128, DC, 512], BF16, sc)
            bmix = bufs(DC, "mix")
            tiles = [t for t in TILES if not (last and t[2])]
            wbo = p.wbuf["wout%d" % i]
            for _ in tiles:
                for dg in range(4):
                    ws.enqueue(lambda t, b, dg=dg: fw.dma(fw.sp, t[:, :].rearrange("p (c kc n) -> p c kc n", c=4, kc=DC), p.woutS[i][dg],
                                                           reads=[wbo], writes=[b], sbuf_side=b))
                p.enqueue_ffn(i, 1)
            for (t0, N, is_ctx) in tiles:
                c = 1 if is_ctx else 0
                fw.dma(fw.sp, p.xt[:, :, :N], p.xT.rearrange("(kc q) t -> q kc t", q=128)[:, :, t0:t0 + N], writes=p.bxt, sbuf_side=p.bxt[0])
                fw.dma(fw.sp, mix[:, 0:4, :N], p.s5T.rearrange("(c q) t -> q c t", q=128)[:, :, t0:t0 + N], writes=bmix[0:4], sbuf_side=bmix[0])
                with fw.scope() as sc2:
                    p.attention(i, t0, N, is_ctx, mix, bmix, sc2)
                for dg in range(4):
                    t, b = ws.take()
                    tv = t[:, :].rearrange("p (c kc n) -> p c kc n", c=4, kc=DC)
                    for ci in range(4):
                        dc = dg * 4 + ci
                        yb, byb = p.banks[4 + dc % 2], p.bb[4 + dc % 2]
                        fw.group(fw.pe, [(lambda e, kc=kc: e.matmul(yb[:, :N], lhsT=tv[:, ci, kc, :], rhs=mix[:, kc, :N], start=(kc == 0), stop=(kc == DC - 1)))
                                         for kc in range(DC)], reads=[b] + bmix, writes=[byb])
                        fw.op(fw.dve, lambda e: e.scalar_tensor_tensor(out=p.xt[:, dc, :N], in0=yb[:, :N], scalar=p.Gcol[:, 1, dc, c:c + 1],
                                                                      in1=p.xt[:, dc, :N], op0=ALU.mult, op1=ALU.add),
                              reads=[byb, p.bcols, p.bxt[dc]], writes=[p.bxt[dc]])
                with fw.scope() as sc2:
                    act = fw.sbuf("act", [128, FC, 512], BF16, sc2)
                    bact = bufs(FC, "act")
                    p.rmsnorm_mod(i, 2, c, N)
                    p.ffn(i, 1, 2, c, N, act, bact)
                if not last:
                    fw.dma(fw.act, p.xT.rearrange("(kc q) t -> q kc t", q=128)[:, :, t0:t0 + N], p.xt[:, :, :N], reads=p.bxt, sbuf_side=p.bxt[0])
                else:
                    with fw.scope() as sc2:
                        p.final_out(t0, N, sc2)

    def final_out(self, t0, N, sc):
        p, fw = self, self.fw
        ssb, bss = p.banks[6], p.bb[6]
        for kc in range(DC):
            s_, bs = p.sq[kc % 2], p.bsq[kc % 2]
            fw.op(fw.act, lambda e: e.activation(out=s_[:, :N], in_=p.xt[:, kc, :N], func=AF.Square), reads=[p.bxt[kc]], writes=[bs])
            fw.op(fw.pe, lambda e: e.matmul(ssb[:, :N], lhsT=p.ones[:], rhs=s_[:, :N], start=(kc == 0), stop=(kc == DC - 1)),
                  reads=[bs, p.bconst], writes=[bss])
        fw.op(fw.act, lambda e: e.activation(out=p.rs[:, :N], in_=ssb[:, :N], func=AF.Sqrt, scale=1.0 / D, bias=p.epsc[:]),
              reads=[bss, p.bconst], writes=[p.brs])
        fw.op(fw.dve, lambda e: e.reciprocal(out=p.rs[:, :N], in_=p.rs[:, :N]), reads=[p.brs], writes=[p.brs])
        for kc in range(DC):
            fw.op(fw.dve, lambda e: e.scalar_tensor_tensor(out=p.xt[:, kc, :N], in0=p.xt[:, kc, :N], scalar=p.fgT[:, kc:kc + 1], in1=p.rs[:, :N],
                                                          op0=ALU.mult, op1=ALU.mult), reads=[p.bxt[kc], p.brs, p.bsmall], writes=[p.bxt[kc]])
        otok = [fw.sbuf("otok", [128, D], F32, sc) for _ in range(2)]
        bot = bufs(2, "otok")
        nb = 0
        for blk in range(N // 128):
            ot, bo = otok[blk % 2], bot[blk % 2]
            for dg in range(4):
                bank, bbk = p.banks[nb % 4], p.bb[nb % 4]
                nb += 1
                fw.group(fw.pe, [(lambda e, q=q: e.transpose(out=bank[:, q * 128:(q + 1) * 128], in_=p.xt[:, dg * 4 + q, blk * 128:(blk + 1) * 128],
                                                             identity=p.ident[:])) for q in range(4)],
                         reads=p.bxt[dg * 4:(dg + 1) * 4] + [p.bconst], writes=[bbk])
                if dg % 2 == 0:
                    fw.op(fw.act, lambda e: e.activation(out=ot[:, dg * 512:(dg + 1) * 512], in_=bank[:, :], func=AF.Copy), reads=[bbk], writes=[bo])
                else:
                    fw.op(fw.dve, lambda e: e.tensor_copy(out=ot[:, dg * 512:(dg + 1) * 512], in_=bank[:, :]), reads=[bbk], writes=[bo])
            r0 = t0 - CTX + blk * 128
            fw.dma(fw.act, p.out[r0:r0 + 128, :], ot[:], reads=[bo], sbuf_side=bo)

    def attention(self, i, t0, N, is_ctx, mix, bmix, sc):
        p, fw = self, self.fw
        nkb = 2 if is_ctx else NB
        nk = nkb * 128
        qd = fw.sbuf("qd", [128, 4, 512], BF16, sc)
        qg = fw.sbuf("qg", [128, 8, 512], BF16, sc)
        bq = Buf("q")
        fw.dma(fw.sp, qd[:, :, :N], p.qd.rearrange("(c q) t -> q c t", q=128)[:, :, t0:t0 + N], writes=[bq], sbuf_side=bq)
        bq2 = Buf("q2")
        fw.dma(fw.sp, qg[:, :, :N], p.qg.rearrange("(c q) t -> q c t", q=128)[:, :, t0:t0 + N], writes=[bq2], sbuf_side=bq2)
        KT = [fw.sbuf("KT", [128, S], BF16, sc) for _ in range(2)]
        VV = [fw.sbuf("VV", [128, NB, 128], BF16, sc) for _ in range(2)]
        bkv = bufs(2, "kv")
        Pt = [fw.sbuf("Pt", [128, 2, 512], BF16, sc) for _ in range(3)]
        bPt = bufs(3, "Pt")
        Ps2 = [fw.sbuf("Ps2", [128, 512], BF16, sc) for _ in range(2)]
        bPs2 = bufs(2, "Ps2")
        rc = [fw.sbuf("rc", [128, 512], F32, sc) for _ in range(2)]
        brc = bufs(2, "rc")
        units = [("d", h) for h in range(4)] + [("g", kv) for kv in range(2)]

        def load_unit(u):
            kind, idx = units[u]
            k, b = u % 2, bkv[u % 2]
            ksrc = p.kd if kind == "d" else p.kg
            vsrc = p.vd if kind == "d" else p.vg
            fw.dma(fw.sp, KT[k][:, 0:nk], ksrc[idx * 128:(idx + 1) * 128, 0:nk], writes=[b], sbuf_side=b)
            fw.dma(fw.sp, VV[k][:, 0:nkb, :], vsrc[idx, :, 0:nkb, :], writes=[b], sbuf_side=b)
        npt = [0]

        def softmax_av(kt, vv, bkvu, qap, pbase, K, scale, obank, bob, dbank, bdb, readsq):
            npair = nkb // 2

            def s_pair(pi):
                sp, bsp0, bsp1 = p.bpair[pi % 2], p.bb[2 * (pi % 2)], p.bb[2 * (pi % 2) + 1]
                fw.group(fw.pe, [(lambda e, j=j: e.matmul(sp[:, j, :N], lhsT=kt[pbase:pbase + K, (2 * pi + j) * 128:(2 * pi + j + 1) * 128], rhs=qap,
                                                          start=True, stop=True)) for j in range(2)],
                         reads=[bkvu, readsq], writes=[bsp0, bsp1])
            s_pair(0)
            for pi in range(npair):
                if pi + 1 < npair:
                    s_pair(pi + 1)
                sp, bsp0, bsp1 = p.bpair[pi % 2], p.bb[2 * (pi % 2)], p.bb[2 * (pi % 2) + 1]
                pt, bpt = Pt[npt[0] % 3], bPt[npt[0] % 3]
                npt[0] += 1
                fw.op(fw.act, lambda e: e.activation(out=pt[:, :, :N], in_=sp[:, :, :N], func=AF.Exp, scale=scale), reads=[bsp0, bsp1], writes=[bpt])
                fw.group(fw.pe, [(lambda e, j=j: e.matmul(obank[:, :N], lhsT=vv[:, 2 * pi + j, :], rhs=pt[:, j, :N], start=(pi == 0 and j == 0),
                                                          stop=(pi == npair - 1 and j == 1))) for j in range(2)],
                         reads=[bkvu, bpt], writes=[bob])
                ps2, bps2 = Ps2[pi % 2], bPs2[pi % 2]
                fw.op(fw.dve, lambda e: e.tensor_tensor(out=ps2[:, :N], in0=pt[:, 0, :N], in1=pt[:, 1, :N], op=ALU.add), reads=[bpt], writes=[bps2])
                fw.op(fw.pe, lambda e: e.matmul(dbank[:, :N], lhsT=p.ones[:], rhs=ps2[:, :N], start=(pi == 0), stop=(pi == npair - 1)),
                      reads=[bps2, p.bconst], writes=[bdb])
        load_unit(0)
        for u, (kind, idx) in enumerate(units):
            if u + 1 < len(units):
                load_unit(u + 1)
            kt, vv, bkvu = KT[u % 2], VV[u % 2], bkv[u % 2]
            if kind == "d":
                h = idx
                for m in range(2):
                    softmax_av(kt, vv, bkvu, qd[64 * m:64 * m + 64, h, :N], 64 * m, 64, 0.125,
                               p.banks[4 + m], p.bb[4 + m], p.banks[6 + m], p.bb[6 + m], bq)
                t1, b1 = p.tf[0], p.btf[0]
                t2, b2 = p.tf[1], p.btf[1]
                for m, (tt, bt) in enumerate(((t1, b1), (t2, b2))):
                    fw.op(fw.dve, lambda e: e.reciprocal(out=rc[m][:, :N], in_=p.banks[6 + m][:, :N]), reads=[p.bb[6 + m]], writes=[brc[m]])
                    fw.op(fw.dve, lambda e: e.tensor_tensor(out=tt[:, :N], in0=p.banks[4 + m][:, :N], in1=rc[m][:, :N], op=ALU.mult),
                          reads=[p.bb[4 + m], brc[m]], writes=[bt])
                fw.op(fw.dve, lambda e: e.scalar_tensor_tensor(out=t1[:, :N], in0=t2[:, :N], scalar=p.lamc[:, 0:1], in1=t1[:, :N], op0=ALU.mult, op1=ALU.add),
                      reads=[b1, b2, p.blamc], writes=[b1])
                s_, bs = p.sq[h % 2], p.bsq[h % 2]
                ssb, bss = p.banks[0], p.bb[0]
                fw.op(fw.act, lambda e: e.activation(out=s_[:, :N], in_=t1[:, :N], func=AF.Square), reads=[b1], writes=[bs])
                fw.op(fw.pe, lambda e: e.matmul(ssb[:, :N], lhsT=p.ones[:], rhs=s_[:, :N], start=True, stop=True), reads=[bs, p.bconst], writes=[bss])
                fw.op(fw.act, lambda e: e.activation(out=p.rs[:, :N], in_=ssb[:, :N], func=AF.Sqrt, scale=1.0 / 128, bias=p.epsc[:]),
                      reads=[bss, p.bconst], writes=[p.brs])
                fw.op(fw.dve, lambda e: e.reciprocal(out=p.rs[:, :N], in_=p.rs[:, :N]), reads=[p.brs], writes=[p.brs])
                fw.op(fw.dve, lambda e: e.scalar_tensor_tensor(out=mix[:, 4 + h, :N], in0=t1[:, :N], scalar=p.lamc[:, 1:2], in1=p.rs[:, :N],
                                                              op0=ALU.mult, op1=ALU.mult), reads=[b1, p.brs, p.blamc], writes=[bmix[4 + h]])
            else:
                kv = idx
                for r in range(4):
                    h = kv * 4 + r
                    ob, bob = p.banks[4 + r % 2], p.bb[4 + r % 2]
                    db, bdb = p.banks[6 + r % 2], p.bb[6 + r % 2]
                    softmax_av(kt, vv, bkvu, qg[:, h, :N], 0, 128, 128.0 ** -0.5, ob, bob, db, bdb, bq2)
                    fw.op(fw.dve, lambda e: e.reciprocal(out=rc[r % 2][:, :N], in_=db[:, :N]), reads=[bdb], writes=[brc[r % 2]])
                    fw.op(fw.dve, lambda e: e.tensor_tensor(out=mix[:, 8 + h, :N], in0=ob[:, :N], in1=rc[r % 2][:, :N], op=ALU.mult),
                          reads=[bob, brc[r % 2]], writes=[bmix[8 + h]])

    def finish(self):
        self.fw.barrier()


def build(debug=None, upto="all", conv="all", ntiles=9, stop=None):
    p = Prog(debug)
    p.ntiles = ntiles
    p.stop = stop
    p.declare()
    fw = p.fw
    with fw.stack:
        p.setup()
        p.conv_filter = None if conv == "all" else conv.split(",")
        p.rope_tables()
        p.convert_layer(0)
        for i in range(2):
            if upto == "rope":
                break
            p.modulation(i)
            if upto == "mod":
                break
            p.pass_a(i)
            if upto == "A":
                break
            p.s5(i)
            if upto == "S5":
                break
            p.pass_b(i, i == 1)
            if upto == "B":
                break
        p.finish()
    return p


def host_inputs(inp):
    f = np.float32
    g = {k: np.asarray(v) for k, v in inp.items()}
    common = {}
    common["ada_w"] = np.ascontiguousarray(g["ada_w"], f)
    common["ada_bT"] = np.ascontiguousarray(g["ada_b"].reshape(2, 144, 128).transpose(0, 2, 1), f)
    common["norm_gT"] = np.ascontiguousarray(g["norm_g"].reshape(2, 3, DC, 128).transpose(0, 3, 1, 2), f)
    common["final_gT"] = np.ascontiguousarray(g["final_g"].reshape(DC, 128).T, f)
    common["ffn_w_gate"] = np.ascontiguousarray(g["ffn_w_gate"], f)
    common["ffn_w_up"] = np.ascontiguousarray(g["ffn_w_up"], f)
    common["ffn_w_down"] = np.ascontiguousarray(g["ffn_w_down"], f)
    w_in = np.ascontiguousarray(g["w_in"], f)
    common["w_in"] = w_in
    i64 = np.arange(512) ^ 16
    i128 = np.arange(1024) ^ 32
    k128 = np.arange(256) ^ 32
    common["w_in_rot"] = np.ascontiguousarray(np.concatenate([w_in[:, :, 512 + i64], w_in[:, :, 1024 + i64], w_in[:, :, 2048 + i128],
                                                              w_in[:, :, 3072 + k128]], axis=2), f)
    common["w_out"] = np.ascontiguousarray(g["w_out"], f)
    lam = np.stack([g["s5_lam_re"], g["s5_lam_im"], np.broadcast_to(g["s5_log_dt"][..., None], g["s5_lam_re"].shape)], axis=0)
    lamT = lam.reshape(3, 2, 2, 16, 128).transpose(1, 2, 4, 0, 3)
    common["lamT"] = np.ascontiguousarray(lamT, f)
    XB = np.zeros((2, 2, 2, 16, 128, 128), f)
    YC = np.zeros((2, 2, 2, 16, 128, 128), f)
    for ri, (bsrc, csrc) in enumerate([(g["s5_b_re"], g["s5_c_re"]), (g["s5_b_im"], g["s5_c_im"])]):
        for q in range(16):
            for e in range(2):
                gg = 2 * q + e
                gl = gg % 8
                XB[:, :, ri, q, e * 64:(e + 1) * 64, gl * 16:(gl + 1) * 16] = bsrc[:, :, gg]
                YC[:, :, ri, q, e * 64:(e + 1) * 64, gl * 16:(gl + 1) * 16] = csrc[:, :, gg].transpose(0, 1, 3, 2)
    common["XB"] = XB
    common["YC"] = YC
    common["s5dT"] = np.ascontiguousarray(np.stack([g["s5_d"].reshape(2, 4, 128), g["s5_glu_b"].reshape(2, 4, 128)], axis=1).transpose(0, 3, 1, 2), f)
    common["s5_glu_w"] = np.ascontiguousarray(g["s5_glu_w"], f)
    common["dlam"] = np.ascontiguousarray(g["diff_lam"].reshape(2, 1, 256), f)
    p32 = np.arange(128) ^ 32
    common["hgT"] = np.ascontiguousarray(np.stack([g["diff_subln_g"], g["gqa_q_g"], g["gqa_q_g"][:, p32], g["gqa_k_g"], g["gqa_k_g"][:, p32]], axis=2), f)
    maps = []
    for b in range(NCORES):
        m = dict(common)
        m["x"] = np.ascontiguousarray(g["x"][b], f)
        m["ctx"] = np.ascontiguousarray(g["ctx"][b], f)
        m["cT"] = np.ascontiguousarray(np.stack([g["c"][b].reshape(DC, 128).T, g["c_ctx"].reshape(DC, 128).T], axis=2), f)
        maps.append(m)
    idle = dict(common)
    for k in ("ada_w", "ffn_w_gate", "ffn_w_up", "ffn_w_down", "w_in", "w_in_rot", "w_out", "s5_glu_w", "XB", "YC"):
        idle[k] = np.zeros_like(common[k])
    idle["x"] = np.zeros_like(maps[0]["x"])
    idle["ctx"] = np.zeros_like(maps[0]["ctx"])
    idle["cT"] = np.zeros_like(maps[0]["cT"])
    full = []
    for b in range(NCORES):
        full.append(maps[b])
        full.append(idle)
    return full


_CACHE = {}


def kernel(**inputs):
    maps = host_inputs(inputs)
    if "p" not in _CACHE:
        _CACHE["p"] = build()
    p = _CACHE["p"]
    res = run_bass_kernel_spmd(p.nc, maps, core_ids=list(range(2 * NCORES)))
    return np.stack([np.asarray(res.results[2 * b]["out"], np.float32) for b in range(NCORES)], axis=0)
```
